# Optimizing a Trainium2 kernel written in Bass

```python
import jax, jax.numpy as jnp
from jax import lax
import numpy as np

D_MODEL = 1024
BATCH = 8
SEQ = 2048
DEPTH = 1
DEC_BATCH = 128
DEC_SEQ = 8
PAST_LEN = 8192
PAGE_SIZE = 128

HEAD_DIM = 64
FOX_HEADS = 8
FOX_KV_HEADS = 2
FOX_REP = FOX_HEADS // FOX_KV_HEADS
NSA_HEADS = 8
NSA_GROUPS = 2
NSA_REP = NSA_HEADS // NSA_GROUPS
CMP_STRIDE = 16
CMP_BLOCK = 2 * CMP_STRIDE
CMP_HIDDEN = 2 * HEAD_DIM
SEL_BLOCK = 64
N_SELECT = 16
WINDOW = 512
Q_BLOCK = 128
D_FF = 4 * D_MODEL
RMS_EPS = 1e-6
FORCE_BONUS = 1.0e4
FOX_QW = FOX_HEADS * HEAD_DIM
FOX_KVW = FOX_KV_HEADS * HEAD_DIM
NSA_QW = NSA_HEADS * HEAD_DIM
NSA_KVW = NSA_GROUPS * HEAD_DIM
IN_SPLITS = (FOX_QW, FOX_KVW, FOX_KVW, FOX_HEADS, NSA_QW, 6 * NSA_KVW, 3 * NSA_HEADS, 2 * D_MODEL)
IN_COLS = sum(IN_SPLITS)
IN_OFFSETS = tuple(int(o) for o in np.cumsum(IN_SPLITS)[:-1])

kernel_name = 'fox_nsa_gated_hybrid_step'


def rms_norm(x, g):
    xf = x.astype(jnp.float32)
    y = xf * lax.rsqrt(jnp.mean(xf * xf, axis=-1, keepdims=True) + RMS_EPS)
    return (y * g.astype(jnp.float32)).astype(x.dtype)


def alibi_slopes(n):
    return jnp.asarray(2.0 ** (-8.0 * np.arange(1, n + 1) / n), jnp.float32)


def masked_softmax(s, mask):
    s = jnp.where(mask, s, -jnp.inf)
    mx = jnp.max(s, axis=-1, keepdims=True)
    p = jnp.exp(s - jnp.where(jnp.isfinite(mx), mx, 0.0))
    den = jnp.sum(p, axis=-1, keepdims=True)
    return p / jnp.where(den > 0, den, 1.0)


def gather_pages(pool, page_table):
    g = pool[page_table]
    return g.reshape((g.shape[0], g.shape[1] * g.shape[2]) + g.shape[3:])


def block_importance(p_cmp, n_sel):
    r = SEL_BLOCK // CMP_STRIDE
    n_cmp = p_cmp.shape[-1]
    pp = jnp.pad(p_cmp, [(0, 0)] * (p_cmp.ndim - 1) + [(1, r * n_sel - n_cmp)])
    main = pp[..., :r * n_sel].reshape(p_cmp.shape[:-1] + (n_sel, r))
    last = pp[..., r:r * n_sel + 1:r]
    return 0.5 * main[..., 0] + jnp.sum(main[..., 1:], axis=-1) + 0.5 * last


def compress_blocks(x, pos, w1, w2):
    b, n = x.shape[:2]
    n_chunk = n // CMP_STRIDE
    xc = x[:, :n_chunk * CMP_STRIDE].reshape(b, n_chunk, CMP_STRIDE, NSA_GROUPS, HEAD_DIM)
    pos = pos.reshape(2, CMP_STRIDE, 1, HEAD_DIM)
    w1 = w1.reshape(2, CMP_STRIDE, HEAD_DIM, CMP_HIDDEN)
    lead = jnp.einsum('bncgd,cdh->bngh', xc + pos[0], w1[0])
    trail = jnp.einsum('bncgd,cdh->bngh', xc + pos[1], w1[1])
    return jnp.einsum('bngh,hd->bngd', jax.nn.silu(lead[:, :-1] + trail[:, 1:]), w2)


def compressed_kv(k_raw, v_raw, lp):
    kc = rms_norm(compress_blocks(k_raw, lp['cmp_pos_k'], lp['cmp_w1_k'], lp['cmp_w2_k']), lp['nsa_kn_cmp_g'])
    vc = compress_blocks(v_raw, lp['cmp_pos_v'], lp['cmp_w1_v'], lp['cmp_w2_v'])
    c_end = jnp.arange(kc.shape[1]) * CMP_STRIDE + (CMP_BLOCK - 1)
    return kc, vc, c_end


def to_sel_blocks(x, n_sel):
    b, n = x.shape[:2]
    x = jnp.pad(x, ((0, 0), (0, n_sel * SEL_BLOCK - n), (0, 0), (0, 0)))
    return x.reshape(b, n_sel, SEL_BLOCK, NSA_GROUPS, HEAD_DIM).transpose(0, 3, 1, 2, 4)


def fox_attend(q, cq, q_pos, k, v, ck, k_pos):
    s = jnp.einsum('btgrd,bsgd->bgrts', q, k).astype(jnp.float32) * HEAD_DIM ** -0.5
    bias = jnp.moveaxis(cq, 1, -1)[..., :, None] - jnp.moveaxis(ck, 1, -1)[..., None, :]
    p = masked_softmax(s + bias, k_pos[None, :] <= q_pos[:, None])
    return jnp.einsum('bgrts,bsgd->btgrd', p.astype(v.dtype), v)


def nsa_attend(q, q_pos, gates, kc, vc, c_end, ks_blk, vs_blk, kw, vw, w_pos):
    b, t_len = q.shape[:2]
    scale = HEAD_DIM ** -0.5
    m = alibi_slopes(NSA_HEADS).reshape(NSA_GROUPS, NSA_REP)[None, :, :, None, None]
    dc = q_pos[:, None] - c_end[None, :]
    s = jnp.einsum('btgrd,bngd->bgrtn', q, kc).astype(jnp.float32) * scale - m * dc.astype(jnp.float32)
    p_c = masked_softmax(s, dc >= 0)
    o_c = jnp.einsum('bgrtn,bngd->btgrd', p_c.astype(vc.dtype), vc)
    n_sel = ks_blk.shape[2]
    imp = block_importance(jnp.sum(p_c, axis=2), n_sel)
    blk = jnp.arange(n_sel)[None, :]
    cur = (q_pos // SEL_BLOCK)[:, None]
    forced = (blk == 0) | (blk == cur) | (blk == cur - 1)
    score = jnp.where(blk <= cur, imp + jnp.where(forced, FORCE_BONUS, 0.0), -jnp.inf)
    _, idx = lax.top_k(score, min(N_SELECT, n_sel))
    n_k = idx.shape[-1]
    take = jax.vmap(jax.vmap(lambda blocks, i: blocks[i]))
    ksel = take(ks_blk, idx)
    vsel = take(vs_blk, idx)
    ds = q_pos[None, None, :, None, None] - (idx[..., None] * SEL_BLOCK + jnp.arange(SEL_BLOCK))
    s = jnp.einsum('btgrd,bgtjsd->bgrtjs', q, ksel).astype(jnp.float32) * scale - m[..., None] * ds[:, :, None].astype(jnp.float32)
    p_s = masked_softmax(s.reshape(b, NSA_GROUPS, NSA_REP, t_len, n_k * SEL_BLOCK),
                         (ds >= 0)[:, :, None].reshape(b, NSA_GROUPS, 1, t_len, n_k * SEL_BLOCK))
    p_s = p_s.reshape(b, NSA_GROUPS, NSA_REP, t_len, n_k, SEL_BLOCK)
    o_s = jnp.einsum('bgrtjs,bgtjsd->btgrd', p_s.astype(vsel.dtype), vsel)
    dw = q_pos[:, None] - w_pos[None, :]
    s = jnp.einsum('btgrd,bwgd->bgrtw', q, kw).astype(jnp.float32) * scale - m * dw.astype(jnp.float32)
    p_w = masked_softmax(s, (dw >= 0) & (dw < WINDOW) & (w_pos >= 0)[None, :])
    o_w = jnp.einsum('bgrtw,bwgd->btgrd', p_w.astype(vw.dtype), vw)
    return gates[..., 0:1] * o_c + gates[..., 1:2] * o_s + gates[..., 2:3] * o_w


def pre_mixer(x, c, lp):
    b, t_len, _ = x.shape
    mod = jnp.einsum('bd,de->be', jax.nn.silu(c), lp['w_ada']) + lp['b_ada']
    sh1, sc1, gt1, sh2, sc2, gt2 = jnp.split(mod[:, None, :], 6, axis=-1)
    h = rms_norm(x, lp['norm1_g']) * (1.0 + sc1) + sh1
    z = jnp.einsum('btd,dc->btc', h, lp['w_in'])
    fq, fk, fv, ff, nq, nkv, ng, mg = jnp.split(z, IN_OFFSETS, axis=-1)
    fox_k = rms_norm(fk.reshape(b, t_len, FOX_KV_HEADS, HEAD_DIM), lp['fox_kn_g'])
    fox_v = fv.reshape(b, t_len, FOX_KV_HEADS, HEAD_DIM)
    nkv = nkv.reshape(b, t_len, 6, NSA_GROUPS, HEAD_DIM)
    slc_k = rms_norm(nkv[:, :, 2], lp['nsa_kn_slc_g'])
    win_k = rms_norm(nkv[:, :, 4], lp['nsa_kn_win_g'])
    g_fox, g_nsa = jnp.split(jax.nn.sigmoid(mg), 2, axis=-1)
    return dict(
        sc2=sc2, sh2=sh2, gt1=gt1, gt2=gt2, g_fox=g_fox, g_nsa=g_nsa,
        fox_q=rms_norm(fq.reshape(b, t_len, FOX_KV_HEADS, FOX_REP, HEAD_DIM), lp['fox_qn_g']),
        fox_k=fox_k, fox_v=fox_v,
        fox_rows=jnp.stack([fox_k, fox_v], axis=2),
        logf=jax.nn.log_sigmoid((ff + lp['b_fox_f']).astype(jnp.float32)),
        nsa_q=rms_norm(nq.reshape(b, t_len, NSA_GROUPS, NSA_REP, HEAD_DIM), lp['nsa_qn_g']),
        nsa_gate=jax.nn.sigmoid(ng.reshape(b, t_len, NSA_GROUPS, NSA_REP, 3)),
        nsa_rows=jnp.stack([nkv[:, :, 0], nkv[:, :, 1], slc_k, nkv[:, :, 3]], axis=2),
        win_rows=jnp.stack([win_k, nkv[:, :, 5]], axis=2),
    )


def post_mixer(x, pm, o_fox, o_nsa, lp):
    b, t_len, _ = x.shape
    br_fox = jnp.einsum('btk,kd->btd', o_fox.reshape(b, t_len, FOX_QW), lp['w_br_fox'])
    br_nsa = jnp.einsum('btk,kd->btd', o_nsa.reshape(b, t_len, NSA_QW), lp['w_br_nsa'])
    mix = pm['g_fox'] * br_fox + pm['g_nsa'] * br_nsa
    x = x + pm['gt1'] * jnp.einsum('btd,de->bte', mix, lp['w_out'])
    h = rms_norm(x, lp['norm2_g']) * (1.0 + pm['sc2']) + pm['sh2']
    u = jax.nn.relu(jnp.einsum('btd,df->btf', h, lp['w_up']))
    return x + pm['gt2'] * jnp.einsum('btf,fd->btd', u * u, lp['w_down'])


def layer_prompt(x, c, lp, win_len):
    b, t_len, _ = x.shape
    pm = pre_mixer(x, c, lp)
    pos = jnp.arange(t_len)
    nb = t_len // Q_BLOCK
    blocks = lambda a: a.reshape((b, nb, Q_BLOCK) + a.shape[2:]).swapaxes(0, 1)
    cf = jnp.cumsum(pm['logf'], axis=1).reshape(b, t_len, FOX_KV_HEADS, FOX_REP)
    fk, fv = pm['fox_k'], pm['fox_v']
    o_fox = lax.map(lambda a: fox_attend(a[0], a[1], a[2], fk, fv, cf, pos),
                    (blocks(pm['fox_q']), blocks(cf), pos.reshape(nb, Q_BLOCK)))
    o_fox = o_fox.swapaxes(0, 1).reshape(b, t_len, FOX_KV_HEADS, FOX_REP, HEAD_DIM)
    rows = pm['nsa_rows']
    kc, vc, c_end = compressed_kv(rows[:, :, 0], rows[:, :, 1], lp)
    n_sel = -(-t_len // SEL_BLOCK)
    ks_blk = to_sel_blocks(rows[:, :, 2], n_sel)
    vs_blk = to_sel_blocks(rows[:, :, 3], n_sel)
    w_pad = jnp.pad(pm['win_rows'], ((0, 0), (WINDOW, 0), (0, 0), (0, 0), (0, 0)))

    def nsa_block(a):
        qb, gb, i = a
        start = i * Q_BLOCK
        w = lax.dynamic_slice_in_dim(w_pad, start, WINDOW + Q_BLOCK, axis=1)
        q_pos = start + jnp.arange(Q_BLOCK)
        w_pos = start - WINDOW + jnp.arange(WINDOW + Q_BLOCK)
        return nsa_attend(qb, q_pos, gb, kc, vc, c_end, ks_blk, vs_blk, w[:, :, 0], w[:, :, 1], w_pos)

    o_nsa = lax.map(nsa_block, (blocks(pm['nsa_q']), blocks(pm['nsa_gate']), jnp.arange(nb)))
    o_nsa = o_nsa.swapaxes(0, 1).reshape(b, t_len, NSA_GROUPS, NSA_REP, HEAD_DIM)
    y = post_mixer(x, pm, o_fox, o_nsa, lp)
    win_new = jnp.pad(pm['win_rows'], ((0, 0), (max(win_len - t_len, 0), 0), (0, 0), (0, 0), (0, 0)))[:, -win_len:]
    return y, pm['fox_rows'], pm['logf'].astype(x.dtype), rows, win_new


def layer_sample(x, c, fox_pool, logf_pool, nsa_pool, win_buf, page_table, lp):
    b, t_len, _ = x.shape
    pm = pre_mixer(x, c, lp)
    fox_all = jnp.concatenate([gather_pages(fox_pool, page_table), pm['fox_rows']], axis=1)
    past = fox_all.shape[1] - t_len
    n_all = past + t_len
    lf_all = jnp.concatenate([gather_pages(logf_pool, page_table).astype(jnp.float32), pm['logf']], axis=1)
    cf = jnp.cumsum(lf_all, axis=1).reshape(b, n_all, FOX_KV_HEADS, FOX_REP)
    pos = jnp.arange(n_all)
    o_fox = fox_attend(pm['fox_q'], cf[:, past:], pos[past:], fox_all[:, :, 0], fox_all[:, :, 1], cf, pos)
    nsa_all = jnp.concatenate([gather_pages(nsa_pool, page_table), pm['nsa_rows']], axis=1)
    kc, vc, c_end = compressed_kv(nsa_all[:, :, 0], nsa_all[:, :, 1], lp)
    n_sel = -(-n_all // SEL_BLOCK)
    ks_blk = to_sel_blocks(nsa_all[:, :, 2], n_sel)
    vs_blk = to_sel_blocks(nsa_all[:, :, 3], n_sel)
    win_len = win_buf.shape[1]
    win_all = jnp.concatenate([win_buf, pm['win_rows']], axis=1)
    w_pos = past - win_len + jnp.arange(win_len + t_len)
    o_nsa = nsa_attend(pm['nsa_q'], pos[past:], pm['nsa_gate'], kc, vc, c_end, ks_blk, vs_blk,
                       win_all[:, :, 0], win_all[:, :, 1], w_pos)
    y = post_mixer(x, pm, o_fox, o_nsa, lp)
    return y, pm['fox_rows'], pm['logf'].astype(x.dtype), pm['nsa_rows'], win_all[:, t_len:]


def setup_inputs(seed: int = 0) -> dict:
    key = jax.random.key(seed)
    keys = iter(jax.random.split(key, 64))
    nrm = lambda shape, scale=1.0: scale * jax.random.normal(next(keys), shape, jnp.float32)
    gain = lambda n: 1.0 + nrm((DEPTH, n), 0.05)
    n_pages = PAST_LEN // PAGE_SIZE
    n_phys = (DEC_BATCH * n_pages * 5) // 4
    win_len = min(WINDOW, PAST_LEN)
    page_table = jax.random.permutation(next(keys), n_phys)[:DEC_BATCH * n_pages].reshape(DEC_BATCH, n_pages).astype(jnp.int32)
    return {
        'x_prompt': nrm((BATCH, SEQ, D_MODEL)),
        'x_sample': nrm((DEC_BATCH, DEC_SEQ, D_MODEL)),
        'cache_fox_kv': nrm((DEPTH, n_phys, PAGE_SIZE, 2, FOX_KV_HEADS, HEAD_DIM)),
        'cache_fox_logf': jax.nn.log_sigmoid(nrm((DEPTH, n_phys, PAGE_SIZE, FOX_HEADS)) + 4.5),
        'cache_nsa_kv': nrm((DEPTH, n_phys, PAGE_SIZE, 4, NSA_GROUPS, HEAD_DIM)),
        'state_win_kv': nrm((DEPTH, DEC_BATCH, win_len, 2, NSA_GROUPS, HEAD_DIM)),
        'page_table': page_table,
        'c_prompt': nrm((BATCH, D_MODEL)),
        'c_sample': nrm((DEC_BATCH, D_MODEL)),
        'norm1_g': gain(D_MODEL),
        'norm2_g': gain(D_MODEL),
        'w_ada': nrm((DEPTH, D_MODEL, 6 * D_MODEL), 0.5 * D_MODEL ** -0.5),
        'b_ada': nrm((DEPTH, 6 * D_MODEL), 0.02),
        'w_in': nrm((DEPTH, D_MODEL, IN_COLS), D_MODEL ** -0.5),
        'b_fox_f': jnp.linspace(2.0, 7.0, FOX_HEADS)[None, :] + nrm((DEPTH, FOX_HEADS), 0.1),
        'fox_qn_g': gain(HEAD_DIM),
        'fox_kn_g': gain(HEAD_DIM),
        'nsa_qn_g': gain(HEAD_DIM),
        'nsa_kn_cmp_g': gain(HEAD_DIM),
        'nsa_kn_slc_g': gain(HEAD_DIM),
        'nsa_kn_win_g': gain(HEAD_DIM),
        'cmp_pos_k': nrm((DEPTH, CMP_BLOCK, HEAD_DIM), 0.1),
        'cmp_w1_k': nrm((DEPTH, CMP_BLOCK * HEAD_DIM, CMP_HIDDEN), (CMP_BLOCK * HEAD_DIM) ** -0.5),
        'cmp_w2_k': nrm((DEPTH, CMP_HIDDEN, HEAD_DIM), CMP_HIDDEN ** -0.5),
        'cmp_pos_v': nrm((DEPTH, CMP_BLOCK, HEAD_DIM), 0.1),
        'cmp_w1_v': nrm((DEPTH, CMP_BLOCK * HEAD_DIM, CMP_HIDDEN), (CMP_BLOCK * HEAD_DIM) ** -0.5),
        'cmp_w2_v': nrm((DEPTH, CMP_HIDDEN, HEAD_DIM), CMP_HIDDEN ** -0.5),
        'w_br_fox': nrm((DEPTH, FOX_QW, D_MODEL), FOX_QW ** -0.5),
        'w_br_nsa': nrm((DEPTH, NSA_QW, D_MODEL), NSA_QW ** -0.5),
        'w_out': nrm((DEPTH, D_MODEL, D_MODEL), D_MODEL ** -0.5),
        'w_up': nrm((DEPTH, D_MODEL, D_FF), D_MODEL ** -0.5),
        'w_down': nrm((DEPTH, D_FF, D_MODEL), D_FF ** -0.5),
    }


def reference(x_prompt, x_sample, cache_fox_kv, cache_fox_logf, cache_nsa_kv, state_win_kv, page_table,
              c_prompt, c_sample, norm1_g, norm2_g, w_ada, b_ada, w_in, b_fox_f, fox_qn_g, fox_kn_g,
              nsa_qn_g, nsa_kn_cmp_g, nsa_kn_slc_g, nsa_kn_win_g, cmp_pos_k, cmp_w1_k, cmp_w2_k,
              cmp_pos_v, cmp_w1_v, cmp_w2_v, w_br_fox, w_br_nsa, w_out, w_up, w_down):
    yp, ys = x_prompt, x_sample
    p_fox, p_lf, p_nsa, p_win = [], [], [], []
    s_fox, s_lf, s_nsa, s_win = [], [], [], []
    for l in range(DEPTH):
        lp = dict(norm1_g=norm1_g[l], norm2_g=norm2_g[l], w_ada=w_ada[l], b_ada=b_ada[l], w_in=w_in[l],
                  b_fox_f=b_fox_f[l], fox_qn_g=fox_qn_g[l], fox_kn_g=fox_kn_g[l], nsa_qn_g=nsa_qn_g[l],
                  nsa_kn_cmp_g=nsa_kn_cmp_g[l], nsa_kn_slc_g=nsa_kn_slc_g[l], nsa_kn_win_g=nsa_kn_win_g[l],
                  cmp_pos_k=cmp_pos_k[l], cmp_w1_k=cmp_w1_k[l], cmp_w2_k=cmp_w2_k[l],
                  cmp_pos_v=cmp_pos_v[l], cmp_w1_v=cmp_w1_v[l], cmp_w2_v=cmp_w2_v[l],
                  w_br_fox=w_br_fox[l], w_br_nsa=w_br_nsa[l], w_out=w_out[l], w_up=w_up[l], w_down=w_down[l])
        yp, a_fox, a_lf, a_nsa, a_win = layer_prompt(yp, c_prompt, lp, state_win_kv.shape[2])
        ys, b_fox, b_lf, b_nsa, b_win = layer_sample(ys, c_sample, cache_fox_kv[l], cache_fox_logf[l],
                                                     cache_nsa_kv[l], state_win_kv[l], page_table, lp)
        p_fox.append(a_fox); p_lf.append(a_lf); p_nsa.append(a_nsa); p_win.append(a_win)
        s_fox.append(b_fox); s_lf.append(b_lf); s_nsa.append(b_nsa); s_win.append(b_win)
    return (yp, ys, jnp.stack(p_fox), jnp.stack(p_lf), jnp.stack(p_nsa), jnp.stack(p_win),
            jnp.stack(s_fox), jnp.stack(s_lf), jnp.stack(s_nsa), jnp.stack(s_win))
```

```python
import numpy as np
import ml_dtypes
import concourse.bass as bass
import concourse.mybir as mybir
from concourse.bass_utils import run_bass_kernel_spmd

F32 = mybir.dt.float32
BF16 = mybir.dt.bfloat16
I32 = mybir.dt.int32
AF = mybir.ActivationFunctionType
ALU = mybir.AluOpType
AX = mybir.AxisListType

NT = 17
EPS = 1e-6
NEG = -30000.0


class Buf:
    def __init__(self, name, t):
        self.name = name
        self.t = t
        self.writers = {}
        self.readers = {}
        self.dsem = None
        self.dcount = 0
        self.excl = False

    def __getitem__(self, k):
        return self.t[k]


class Op:
    __slots__ = ("eng", "fn", "deps", "signal", "idx", "count", "dtok")

    def __init__(self, eng, fn):
        self.eng = eng
        self.fn = fn
        self.deps = {}
        self.signal = False
        self.count = None
        self.dtok = None


ENGS = ("pe", "act", "dve", "pool", "sp")


class Prog:
    def __init__(self, nc):
        self.nc = nc
        self.ops = {e: [] for e in ENGS}
        self.sems = {e: nc.alloc_semaphore("s_" + e) for e in ENGS}
        self.known = {e: {} for e in ENGS}
        self.final = {}
        self.rr = 0

    def sbuf(self, name, shape, dt):
        return Buf(name, self.nc.alloc_sbuf_tensor("sb_" + name, list(shape), dt))

    def psum(self, name, shape, dt=F32):
        b = Buf(name, self.nc.alloc_psum_tensor("ps_" + name, list(shape), dt))
        b.excl = True
        return b

    def _collect(self, eng, reads, writes):
        deps = {}

        def add(d):
            for k, v in d.items():
                if deps.get(k, -1) < v:
                    deps[k] = v

        for b in reads:
            add(b.writers)
        for b in writes:
            add(b.writers)
            add(b.readers)
        out = {}
        kn = self.known[eng]
        for k, v in deps.items():
            if k == eng and eng == "pe":
                continue
            if kn.get(k, -1) >= v:
                continue
            kn[k] = v
            out[k] = v
            if isinstance(k, str):
                self.ops[k][v].signal = True
        return out

    def op(self, eng, fn, reads=(), writes=(), full=True):
        o = Op(eng, fn)
        xr = [b for b in reads if b.excl]
        o.deps = self._collect(eng, reads, list(writes) + xr)
        o.idx = len(self.ops[eng])
        self.ops[eng].append(o)
        for b in reads:
            if b.readers.get(eng, -1) < o.idx:
                b.readers[eng] = o.idx
        for b in writes:
            if full:
                b.writers = {eng: o.idx}
                b.readers = {}
            else:
                b.writers[eng] = o.idx
        return o

    def dma(self, q, fn, sb, reads=(), writes=(), final=False):
        o = Op(q, fn)
        o.deps = self._collect(q, reads, writes)
        o.idx = len(self.ops[q])
        self.ops[q].append(o)
        if sb.dsem is None:
            sb.dsem = self.nc.alloc_semaphore("d_" + sb.name)
        sb.dcount += 16
        o.dtok = (sb.dsem, sb.dcount)
        key = sb.dsem
        for b in reads:
            if b.readers.get(key, -1) < sb.dcount:
                b.readers[key] = sb.dcount
        for b in writes:
            b.writers[key] = sb.dcount
        if final:
            self.final[key] = sb.dcount
        return o

    def emit(self):
        nc = self.nc
        for e in ENGS:
            c = 0
            for o in self.ops[e]:
                if o.signal:
                    c += 1
                    o.count = c
        prog = self
        print("signal counts:", {e: sum(1 for o in self.ops[e] if o.signal) for e in ENGS},
              "waits:", {e: sum(len(o.deps) for o in self.ops[e]) for e in ENGS},
              "dma sem max:", max([0] + [o.dtok[1] for e in ENGS for o in self.ops[e] if o.dtok]))

        def run(e, eng):
            for o in prog.ops[e]:
                for k, v in o.deps.items():
                    if isinstance(k, str):
                        eng.wait_ge(prog.sems[k], prog.ops[k][v].count)
                    else:
                        eng.wait_ge(k, v)
                ins = o.fn(eng)
                if o.dtok is not None:
                    ins.then_inc(o.dtok[0], 16)
                elif o.signal:
                    ins.then_inc(prog.sems[e], 1)
            if e == "sp":
                for k, v in prog.final.items():
                    eng.wait_ge(k, v)

        with nc.Block() as block:
            @block.tensor
            def _(eng):
                run("pe", eng)

            @block.scalar
            def _(eng):
                run("act", eng)

            @block.vector
            def _(eng):
                run("dve", eng)

            @block.gpsimd
            def _(eng):
                run("pool", eng)

            @block.sync
            def _(eng):
                run("sp", eng)


import os
NPHYS = int(os.environ.get("KNPHYS", "10240"))
IN_SPECS = [
    ("xp", [2048, 1024], F32), ("xs", [128, 1024], F32), ("cT", [128, 8, 17], F32),
    ("w_ada", [1024, 6144], F32), ("b_adaT", [128, 48], F32), ("b_ada", [1, 6144], F32), ("w_in", [1024, 4128], F32),
    ("g1T", [128, 8], F32), ("g2T", [128, 8], F32), ("b_fox_f", [1, 8], F32),
    ("fox_qn_g", [1, 64], F32), ("fox_kn_g", [1, 64], F32), ("nsa_qn_g", [1, 64], F32),
    ("nsa_kn_cmp_g", [1, 64], F32), ("nsa_kn_slc_g", [1, 64], F32), ("nsa_kn_win_g", [1, 64], F32),
    ("w1k", [128, 32 * 128], F32), ("w1v", [128, 32 * 128], F32), ("w2k", [128, 64], F32), ("w2v", [128, 64], F32),
    ("poskT", [128, 32], F32), ("posvT", [128, 32], F32),
    ("w_br_fox", [512, 1024], F32), ("w_br_nsa", [512, 1024], F32), ("w_out", [1024, 1024], F32),
    ("w_up", [1024, 4096], F32), ("w_down", [4096, 1024], F32),
    ("win_state", [16, 512, 256], F32),
    ("ident", [128, 128], BF16), ("tri_p", [128, 128], F32), ("tri_s", [128, 128], F32),
    ("ones", [128, 128], F32), ("alq", [128, NT * 32], BF16), ("alk", [128, NT * 4], BF16), ("alkc", [128, 4], BF16),
    ("cm_diag", [128, 512], BF16), ("cm_win", [128, 512], BF16), ("cmaskc", [128, 2048], BF16),
    ("esel", [32, 16 * 128], BF16), ("bmat", [128, 32], BF16), ("fz2", [128, 16 * 32], F32), ("vz", [128, 16 * 32], F32),
    ("selT", [17, 256], F32),
    ("cache_fox_kv", [NPHYS * 8, 4096], F32), ("cache_nsa_cmp", [NPHYS * 16, 2048], F32), ("cache_nsa_slc", [NPHYS * 16, 2048], F32), ("cache_fox_logf", [NPHYS * 8, 128], F32),
    ("bmask", [128, 512], BF16), ("zall", [128, 512], F32), ("zsel", [32, 1024], BF16), ("bmats", [128, 528], BF16), ("esels", [128, 1024], BF16),
    ("alks_n", [128, 256], BF16), ("alkw", [128, 16], BF16), ("alkcs", [128, 16], BF16), ("ustr", [128, 128], F32), ("s0m", [128, 32], BF16),
    ("wm0", [128, 32], BF16), ("gsum", [32, 32], F32), ("fz2s", [32, 132], F32), ("ohc", [32, 12], F32), ("ptq", [128, 192], I32),
    ("mulc", [128, 12], F32), ("addc", [128, 12], F32),
]
OUT_SPECS = [
    ("yp", [2048, 1024]), ("ys", [128, 1024]), ("fox_p", [2048, 256]), ("lf_p", [2048, 8]),
    ("nsa_p", [2048, 512]), ("win_p", [512, 256]), ("fox_s", [128, 256]), ("lf_s", [128, 8]),
    ("nsa_s", [128, 512]), ("win_s", [16, 512, 256]),
]


import os
STAGE = int(os.environ.get("KSTAGE", "9"))
KSKIP = os.environ.get("KSKIP", "")
KCMP = int(os.environ.get("KCMP", "9"))


def build_program():
    nc = bass.Bass("TRN2", target_bir_lowering=False)
    I = {n: nc.dram_tensor(n, s, d, kind="ExternalInput").ap() for n, s, d in IN_SPECS}
    O = {n: nc.dram_tensor(n, s, F32, kind="ExternalOutput").ap() for n, s in OUT_SPECS}
    P = Prog(nc)

    def load(name, shape, dt, src, q="sp"):
        b = P.sbuf(name, shape, dt)
        P.dma(q, lambda e: e.dma_start(out=b[:], in_=src), b, writes=[b])
        return b

    ident = load("ident", [128, 128], BF16, I["ident"])
    tri_p = load("tri_p", [128, 128], F32, I["tri_p"])
    tri_s = load("tri_s", [128, 128], F32, I["tri_s"])
    ones = load("ones", [128, 128], F32, I["ones"])
    alq = load("alq", [128, NT, 8, 4], BF16, I["alq"].rearrange("p (t h c) -> p t h c", t=NT, h=8))
    alk = load("alk", [128, NT, 4], BF16, I["alk"].rearrange("p (t c) -> p t c", t=NT))
    alkc = load("alkc", [128, 4], BF16, I["alkc"])
    cm_diag = load("cm_diag", [128, 512], BF16, I["cm_diag"])
    cm_win = load("cm_win", [128, 512], BF16, I["cm_win"])
    cmaskc = load("cmaskc", [128, 2048], BF16, I["cmaskc"])
    esel = load("esel", [32, 16, 128], BF16, I["esel"].rearrange("p (k t) -> p k t", k=16))
    bmat = load("bmat", [128, 32], BF16, I["bmat"])
    fz2 = load("fz2", [128, 16, 32], F32, I["fz2"].rearrange("p (t j) -> p t j", t=16))
    vz = load("vz", [128, 16, 32], F32, I["vz"].rearrange("p (t j) -> p t j", t=16))
    selT = load("selT", [17, 2, 128], F32, I["selT"].rearrange("p (a t) -> p a t", a=2))
    cT = load("cT", [128, 8, 17], F32, I["cT"])
    b_adaT = load("b_adaT", [128, 48], F32, I["b_adaT"])
    g1T = load("g1T", [128, 8], F32, I["g1T"])
    g2T = load("g2T", [128, 8], F32, I["g2T"])
    bff = load("bff", [128, 8], F32, I["b_fox_f"].partition_broadcast(128))
    gains = {}
    for gname in ("fox_qn_g", "fox_kn_g", "nsa_qn_g", "nsa_kn_cmp_g", "nsa_kn_slc_g", "nsa_kn_win_g"):
        gains[gname] = load(gname, [128, 64], F32, I[gname].partition_broadcast(128))
    for gname in ("fox_qn_g", "nsa_qn_g"):
        gb = gains[gname]
        P.op("dve", lambda e, gb=gb: e.tensor_scalar(out=gb[:], in0=gb[:], scalar1=0.125, scalar2=None, op0=ALU.mult),
             reads=[gb], writes=[gb])

    ptr = P.psum("ptr", [128, 1024], BF16)
    pz = [P.psum("pz%d" % j, [128, 512], F32) for j in range(2)]
    pmisc = P.psum("pmisc", [128, 512], F32)
    pss = [P.psum("pss%d" % j, [128, 512], F32) for j in range(2)]
    po = [P.psum("po%d" % j, [128, 512], F32) for j in range(2)]
    cnt = {"pz": 0, "pss": 0, "pt": 0}

    def nxt(lst, key):
        b = lst[cnt[key] % len(lst)]
        cnt[key] += 1
        return b

    wst = [P.sbuf("wst0", [128, 2080], F32)] * 2
    wbf = P.sbuf("wbf", [128, 8, 2080], BF16)
    kvreg_t = nc.alloc_sbuf_tensor("sb_kvreg", [128, 20 * 1024], BF16)
    W17 = NT * 128
    fkT = Buf("fkT", None); skT = Buf("skT", None); wkT = Buf("wkT", None)
    FVA = Buf("FVA", None); SVA = Buf("SVA", None); WVA = Buf("WVA", None)
    wbf2 = Buf("wbf2", None)
    kv_all = [fkT, skT, wkT, FVA, SVA, WVA]
    o_kT = {"f": 0, "s": 2 * W17, "w": 4 * W17}
    o_VA = {"f": 6 * W17, "s": 6 * W17 + NT * 130, "w": 6 * W17 + 2 * NT * 130}
    assert 6 * W17 + 3 * NT * 130 <= 20 * 1024

    def kT_ap(which, g, c0, c1):
        base = o_kT[which] + g * W17
        return kvreg_t[:, base + c0:base + c1]

    def kT_tile2(which, i):
        base = o_kT[which]
        return kvreg_t[:, base:base + 2 * W17].rearrange("p (g c) -> p g c", g=2)[:, :, i * 128:(i + 1) * 128]

    def VA_ap(which, i, g):
        base = o_VA[which] + (i * 2 + g) * 65
        return kvreg_t[:, base:base + 65]

    def VA_tile(which, i):
        base = o_VA[which] + i * 130
        return kvreg_t[:, base:base + 130].rearrange("p (g c) -> p g c", g=2)

    def VA_all(which):
        base = o_VA[which]
        return kvreg_t[:, base:base + NT * 130].rearrange("p (n c) -> p n c", c=65)

    wbf2_v = kvreg_t[:, 0:16 * 1024].rearrange("p (k c) -> p k c", k=16)

    oscr = nc.dram_tensor("oscr", [NT, 128, 1024], BF16).ap()
    hscr = nc.dram_tensor("hscr", [NT, 128, 1024], BF16).ap()
    oscr_b = [Buf("oscr%d" % i, None) for i in range(NT)]
    hscr_b = [Buf("hscr%d" % i, None) for i in range(NT)]
    ot = P.sbuf("ot", [128, 1024], BF16)
    h2t = P.sbuf("h2t", [128, 8, 128], BF16)
    arena_t = nc.alloc_sbuf_tensor("sb_arena", [128, 18 * 1024], BF16)
    ar_off = [0]
    ar_bufs = []

    def carve(name, shape, dt):
        n = 1
        for d_ in shape[1:]:
            n *= d_
        nb = n * (2 if dt == F32 else 1)
        ap = arena_t[:, ar_off[0]:ar_off[0] + nb]
        ar_off[0] += nb
        assert ar_off[0] <= 18 * 1024, (name, ar_off[0])
        if dt == F32:
            ap = ap.bitcast(F32)
        if len(shape) == 3:
            ap = ap.rearrange("p (a b) -> p a b", a=shape[1])
        b = Buf(name, ap)
        ar_bufs.append(b)
        return b

    shreg_t = nc.alloc_sbuf_tensor("sb_shreg", [128, 16 * 512], BF16)
    fqT = Buf("fqT", None); nqT = Buf("nqT", None); uT = Buf("uT", None)
    fqT_v = shreg_t[:, 0:4096].rearrange("p (h c) -> p h c", h=8)
    nqT_v = shreg_t[:, 4096:8192].rearrange("p (h c) -> p h c", h=8)
    uT_v = shreg_t[:, :].rearrange("p (f c) -> p f c", f=16)

    scT = P.sbuf("scT", [128, 8, 17], BF16)
    P.op("act", lambda e: e.activation(out=scT[:], in_=cT[:], func=AF.Silu), reads=[cT], writes=[scT])
    modT = P.sbuf("modT", [128, 48, 17], F32)
    modrow = P.sbuf("modrow", [17, 2048], F32)
    P.dma("sp", lambda e: e.dma_start(out=modrow[:, 0:1024], in_=I["b_ada"][:, 2048:3072].partition_broadcast(17)), modrow, writes=[modrow])
    P.dma("sp", lambda e: e.dma_start(out=modrow[:, 1024:2048], in_=I["b_ada"][:, 5120:6144].partition_broadcast(17)), modrow, writes=[modrow])
    pmod = [pz[0], pz[1]]
    wada_v = I["w_ada"].rearrange("(k p) c -> p k c", p=128)
    for eg in range(24):
        st = wst[eg % 2]
        P.dma("sp", lambda e, st=st, eg=eg: e.dma_start(out=st[:, 0:2048].rearrange("p (k c) -> p k c", k=8),
                                                        in_=wada_v[:, :, eg * 256:(eg + 1) * 256]), st, writes=[st])
        ce = "pool" if eg % 2 == 0 else "dve"
        P.op(ce, lambda e, st=st: e.tensor_copy(out=wbf[:, :, 0:256], in_=st[:, 0:2048].rearrange("p (k c) -> p k c", k=8)),
             reads=[st], writes=[wbf])
        for ec in range(2):
            e_idx = eg * 2 + ec
            pm = pmod[e_idx // 24]
            col = (e_idx % 24) * 17
            for k in range(8):
                P.op("pe", lambda e, pm=pm, col=col, k=k, ec=ec: e.matmul(pm[:, col:col + 17], lhsT=wbf[:, k, ec * 128:(ec + 1) * 128],
                                                                         rhs=scT[:, k, :], start=(k == 0), stop=(k == 7)),
                     reads=[wbf, scT], writes=[pm], full=False)
        mr = None
        if 8 <= eg < 12:
            mr = (eg - 8) * 256
        if 20 <= eg < 24:
            mr = 1024 + (eg - 20) * 256
        if mr is not None:
            for k in range(8):
                P.op("pe", lambda e, k=k: e.matmul(pss[0][0:17, 0:256], lhsT=scT[:, k, :], rhs=wbf[:, k, 0:256], start=(k == 0), stop=(k == 7)),
                     reads=[wbf, scT], writes=[pss[0]], full=False)
            P.op("dve", lambda e, mr=mr: e.tensor_tensor(out=modrow[:, mr:mr + 256], in0=pss[0][0:17, 0:256], in1=modrow[:, mr:mr + 256], op=ALU.add),
                 reads=[pss[0], modrow], writes=[modrow], full=False)
    for hlf in range(2):
        P.op("dve", lambda e, hlf=hlf: e.tensor_tensor(out=modT[:, hlf * 24:(hlf + 1) * 24, :],
                                                       in0=pmod[hlf][:, 0:408].rearrange("p (c j) -> p c j", j=17),
                                                       in1=b_adaT[:, hlf * 24:(hlf + 1) * 24].unsqueeze(2).to_broadcast([128, 24, 17]),
                                                       op=ALU.add),
             reads=[pmod[hlf], b_adaT], writes=[modT], full=False)
    G1 = P.sbuf("G1", [128, 8, 17], F32)
    G2 = P.sbuf("G2", [128, 8, 17], F32)
    P.op("dve", lambda e: e.scalar_tensor_tensor(out=G1[:], in0=modT[:, 8:16, :], scalar=1.0,
                                                 in1=g1T[:].unsqueeze(2).to_broadcast([128, 8, 17]), op0=ALU.add, op1=ALU.mult),
         reads=[modT, g1T], writes=[G1])
    P.op("dve", lambda e: e.scalar_tensor_tensor(out=G2[:], in0=modT[:, 32:40, :], scalar=1.0,
                                                 in1=g2T[:].unsqueeze(2).to_broadcast([128, 8, 17]), op0=ALU.add, op1=ALU.mult),
         reads=[modT, g2T], writes=[G2])

    def seq_bc(src3, lo, i):
        if i < 16:
            return src3[:, lo:lo + 8, 0:1].to_broadcast([128, 8, 128])
        return src3[:, lo:lo + 8, 1:17].unsqueeze(3).to_broadcast([128, 8, 16, 8])

    cast_engs = ["pool", "dve"]

    def load_weight(dst_fn, src_fn, nk, width, reads_bufs, dst_buf):
        for k in range(nk):
            st = wst[k % 2]
            src = src_fn(k)
            dst = dst_fn(k)
            P.dma("sp", lambda e, st=st, src=src: e.dma_start(out=st[:, 0:width], in_=src), st, writes=[st])
            ce = cast_engs[k % 2]
            P.op(ce, lambda e, st=st, dst=dst: e.tensor_copy(out=dst, in_=st[:, 0:width]), reads=[st], writes=[dst_buf], full=False)

    win_v = I["w_in"].rearrange("(k p) c -> p k c", p=128)
    load_weight(lambda k: wbf[:, k, 0:2080], lambda k: win_v[:, k, 0:2080], 8, 2080, [], wbf)

    w1 = {}
    w2 = {}
    posT = {}
    for ty, nm1, nm2, nmp in (("k", "w1k", "w2k", "poskT"), ("v", "w1v", "w2v", "posvT")):
        w1[ty] = carve("w1" + ty, [128, 32, 128], BF16)
        for half in range(2):
            st = wst[half]
            P.dma("sp", lambda e, st=st, nm1=nm1, half=half: e.dma_start(out=st[:, 0:2048], in_=I[nm1][:, half * 2048:(half + 1) * 2048]), st, writes=[st])
            P.op("pool", lambda e, st=st, ty=ty, half=half: e.tensor_copy(out=w1[ty][:, half * 16:(half + 1) * 16, :],
                                                                         in_=st[:, 0:2048].rearrange("p (c h) -> p c h", c=16)),
                 reads=[st], writes=[w1[ty]], full=False)
        w2f = load("w2f" + ty, [128, 64], F32, I[nm2])
        w2[ty] = P.sbuf("w2" + ty, [128, 64], BF16)
        P.op("pool", lambda e, ty=ty, w2f=w2f: e.tensor_copy(out=w2[ty][:], in_=w2f[:]), reads=[w2f], writes=[w2[ty]])
        pf = load("posf" + ty, [128, 32], F32, I[nmp])
        posT[ty] = P.sbuf("posT" + ty, [128, 32], BF16)
        P.op("pool", lambda e, ty=ty, pf=pf: e.tensor_copy(out=posT[ty][:], in_=pf[:]), reads=[pf], writes=[posT[ty]])
    cbias = P.sbuf("cbias", [128, 2], F32)
    for ti, ty in enumerate(("k", "v") if "b" not in KSKIP else ()):
        for lc in range(32):
            P.op("pe", lambda e, ty=ty, lc=lc, ti=ti: e.matmul(pmisc[:, 300 + ti:301 + ti], lhsT=w1[ty][0:64, lc, :], rhs=posT[ty][0:64, lc:lc + 1],
                                                              start=(lc == 0), stop=(lc == 31)),
                 reads=[w1[ty], posT[ty]], writes=[pmisc], full=False)
    if "b" not in KSKIP:
        P.op("act", lambda e: e.copy(out=cbias[:], in_=pmisc[:, 300:302]), reads=[pmisc], writes=[cbias])

    xt = [P.sbuf("xt0", [128, 1024], F32)] * 2
    ssq = P.sbuf("ssq", [128, 1], F32)
    rstd = P.sbuf("rstd", [128, 1], F32)
    xn = P.sbuf("xn", [128, 1024], BF16)
    htmp = P.sbuf("htmp", [128, 8, 128], F32)
    hT = P.sbuf("hT", [128, 8, 128], BF16)
    sqb = P.sbuf("sqb", [128, 512], F32)
    ss8 = P.sbuf("ss8", [128, 8], F32)
    ntmp = P.sbuf("ntmp", [128, 512], F32)
    rows_fox = P.sbuf("rows_fox", [128, 256], F32)
    rows_nsa = P.sbuf("rows_nsa", [128, 512], F32)
    rows_win = P.sbuf("rows_win", [128, 256], F32)
    lf = [P.sbuf("lf%d" % j, [128, 8], F32) for j in range(2)]
    t8 = P.sbuf("t8", [128, 8], F32)
    c8 = P.sbuf("c8", [128, 8], F32)
    r8 = P.sbuf("r8", [128, 8], F32)
    carry = P.sbuf("carry", [128, 8], F32)
    PC = P.sbuf("PC", [128, 8, 3], BF16)
    QA = P.sbuf("QA", [128, 8, 128], BF16)
    NQA = P.sbuf("NQA", [128, 8, 128], BF16)
    KA = P.sbuf("KA", [128, 2, 128], BF16)
    SKA = P.sbuf("SKA", [128, 2, 128], BF16)
    WKA = P.sbuf("WKA", [128, 2, 128], BF16)
    XB = P.sbuf("XB", [128, 256], BF16)
    XT = carve("XT", [128, 4, 512], BF16)
    gates = P.sbuf("gates", [128, NT, 24], F32)
    PT = [P.sbuf("PT%d" % j, [128, 512], BF16) for j in range(2)]
    den8 = P.sbuf("den8", [128, 8], F32)
    wk8 = P.sbuf("wk8", [128, 8], F32)
    oacc = carve("oacc", [128, 512], F32)
    otmp = carve("otmp", [128, 512], F32)
    Lsb = carve("Lsb", [128, 8, 32], F32)
    pre = carve("pre", [128, 4, 32], F32)
    leadprev = P.sbuf("leadprev", [128, 4], F32)
    Spad = carve("Spad", [128, 4, 128], BF16)
    KCA = P.sbuf("KCA", [128, 2, 128], BF16)
    kcn = carve("kcn", [128, 128], F32)
    kcT = P.sbuf("kcT", [128, 2, 128], BF16)
    VCA = P.sbuf("VCA", [128, 2, 65], BF16)
    PTc = carve("PTc", [128, 2, 512], BF16)
    imp = carve("imp", [128, 8, 32], F32)
    score = P.sbuf("score", [128, 2, 32], F32)
    swork = P.sbuf("swork", [128, 2, 32], F32)
    mx8 = P.sbuf("mx8", [128, 2, 8], F32)
    thr = P.sbuf("thr", [128, 2], F32)
    nsel = P.sbuf("nsel", [128, 2, 32], BF16)
    nselT = carve("nselT", [128, 2, 512], BF16)

    P.op("pool", lambda e: e.memset(carry[:], 0.0), writes=[carry])
    for b_ in (QA, NQA, KA, SKA, WKA, Spad, KCA, VCA, leadprev, XT):
        P.op("pool", lambda e, b_=b_: e.memset(b_[:], 0.0), writes=[b_])
    for h in range(8):
        r = h % 4
        P.op("pool", lambda e, h=h, r=r: e.memset(QA[:, h, 64 + 3 * r:64 + 3 * r + 3], 1.0), writes=[QA], full=False)
    P.op("pool", lambda e: e.memset(KA[:, :, 76:79], 1.0), writes=[KA], full=False)
    P.op("pool", lambda e: e.memset(VCA[:, :, 64:65], 1.0), writes=[VCA], full=False)
    for which in ("f", "s", "w"):
        P.op("pool", lambda e, which=which: e.memset(VA_all(which)[:, :, 64:65], 1.0), writes=kv_all, full=False)
    for g in range(2):
        P.op("pool", lambda e, g=g: e.tensor_copy(out=KCA[:, g, 64:68], in_=alkc[:]), reads=[alkc], writes=[KCA], full=False)

    src_buf = [None]

    def headnorm(src_ap, nh, gain, out_ap, out_bufs, p0=0, p1=128):
        w = nh * 64
        sb = src_buf[0]
        P.op("act", lambda e: e.activation(out=sqb[p0:p1, 0:w], in_=src_ap, func=AF.Square), reads=[sb], writes=[sqb])
        P.op("dve", lambda e: e.reduce_sum(out=ss8[p0:p1, 0:nh], in_=sqb[p0:p1, 0:w].rearrange("p (h d) -> p h d", h=nh), axis=AX.X),
             reads=[sqb], writes=[ss8])
        P.op("act", lambda e: e.activation(out=ss8[p0:p1, 0:nh], in_=ss8[p0:p1, 0:nh], func=AF.Sqrt, scale=1.0 / 64, bias=EPS),
             reads=[ss8], writes=[ss8])
        P.op("dve", lambda e: e.reciprocal(out=ss8[p0:p1, 0:nh], in_=ss8[p0:p1, 0:nh]), reads=[ss8], writes=[ss8])
        P.op("dve", lambda e: e.tensor_tensor(out=ntmp[p0:p1, 0:w].rearrange("p (h d) -> p h d", h=nh),
                                              in0=src_ap.rearrange("p (h d) -> p h d", h=nh),
                                              in1=ss8[p0:p1, 0:nh].unsqueeze(2).to_broadcast([p1 - p0, nh, 64]), op=ALU.mult),
             reads=[sb, ss8], writes=[ntmp])
        P.op("dve", lambda e: e.tensor_tensor(out=out_ap, in0=ntmp[p0:p1, 0:w].rearrange("p (h d) -> p h d", h=nh),
                                              in1=gain[p0:p1, :].unsqueeze(1).to_broadcast([p1 - p0, nh, 64]), op=ALU.mult),
             reads=[ntmp, gain], writes=out_bufs, full=False)

    def norm_hT(x_t, G, lo_sh, i):
        P.op("pool", lambda e: e.memset(ssq[:], 0.0), writes=[ssq])
        P.op("act", lambda e: e.activation(out=htmp[:].rearrange("p k c -> p (k c)"), in_=x_t[:], func=AF.Square, accum_out=ssq[:]), reads=[x_t], writes=[htmp, ssq])
        P.op("act", lambda e: e.activation(out=rstd[:], in_=ssq[:], func=AF.Sqrt, scale=1.0 / 1024, bias=EPS), reads=[ssq], writes=[rstd])
        P.op("dve", lambda e: e.reciprocal(out=rstd[:], in_=rstd[:]), reads=[rstd], writes=[rstd])
        P.op("dve", lambda e: e.tensor_scalar(out=xn[:], in0=x_t[:], scalar1=rstd[:, 0:1], scalar2=None, op0=ALU.mult),
             reads=[x_t, rstd], writes=[xn])
        for k in range(8):
            P.op("pe", lambda e, k=k: e.transpose(out=ptr[:, k * 128:(k + 1) * 128], in_=xn[:, k * 128:(k + 1) * 128], identity=ident[:]),
                 reads=[xn, ident], writes=[ptr], full=False)
        vw = (lambda a: a) if i < 16 else (lambda a: a.rearrange("p k (s t) -> p k s t", t=8))
        P.op("dve", lambda e: e.tensor_tensor(out=vw(htmp[:]), in0=vw(ptr[:].rearrange("p (k t) -> p k t", k=8)),
                                              in1=seq_bc(G[:], 0, i), op=ALU.mult), reads=[ptr, G], writes=[htmp])
        return vw

    def mm8(out_ap, out_buf, lhs_buf, lhs_fn, rhs_buf, rhs_fn, nk=8):
        rb = list(lhs_buf) if isinstance(lhs_buf, (list, tuple)) else [lhs_buf]
        rb += list(rhs_buf) if isinstance(rhs_buf, (list, tuple)) else [rhs_buf]
        for k in range(nk):
            l_ap = lhs_fn(k)
            r_ap = rhs_fn(k)
            P.op("pe", lambda e, k=k, l_ap=l_ap, r_ap=r_ap: e.matmul(out_ap, lhsT=l_ap, rhs=r_ap, start=(k == 0), stop=(k == nk - 1)),
                 reads=rb, writes=[out_buf], full=False)

    CH = [(0, 512), (512, 264), (776, 512), (1288, 512), (1800, 280)]

    def premixer(i):
        x_t = xt[i % 2]
        rf, rn, rw, lfi = rows_fox, rows_nsa, rows_win, lf[i % 2]
        li = i % 4
        xsrc = I["xp"][i * 128:(i + 1) * 128, :] if i < 16 else I["xs"]
        P.dma("sp", lambda e: e.dma_start(out=x_t[:], in_=xsrc), x_t, writes=[x_t])
        vw = norm_hT(x_t, G1, 0, i)
        P.op("dve", lambda e: e.tensor_tensor(out=vw(hT[:]), in0=vw(htmp[:]), in1=seq_bc(modT[:], 0, i), op=ALU.add),
             reads=[htmp, modT], writes=[hT])

        def zchunk(ci):
            c0, cw = CH[ci]
            pzb = nxt(pz, "pz")
            mm8(pzb[:, 0:cw], pzb, hT, lambda k: hT[:, k, :], wbf, lambda k: wbf[:, k, c0:c0 + cw])
            src_buf[0] = pzb
            return pzb

        zA = zchunk(0)
        headnorm(zA[:, 0:512], 8, gains["fox_qn_g"], QA[:, :, 0:64], [QA])
        zB = zchunk(1)
        headnorm(zB[:, 0:128], 2, gains["fox_kn_g"], rf[:, 0:128].rearrange("p (h d) -> p h d", h=2), [rf])
        P.op("act", lambda e: e.copy(out=rf[:, 128:256], in_=zB[:, 128:256]), reads=[zB], writes=[rf], full=False)
        P.op("dve", lambda e: e.tensor_tensor(out=t8[:], in0=zB[:, 256:264], in1=bff[:], op=ALU.add), reads=[zB, bff], writes=[t8])
        P.op("act", lambda e: e.activation(out=t8[:], in_=t8[:], func=AF.Exp, scale=-1.0), reads=[t8], writes=[t8])
        P.op("act", lambda e: e.activation(out=t8[:], in_=t8[:], func=AF.Ln, bias=1.0), reads=[t8], writes=[t8])
        P.op("dve", lambda e: e.tensor_scalar(out=lfi[:], in0=t8[:], scalar1=-1.0, scalar2=None, op0=ALU.mult), reads=[t8], writes=[lfi])
        P.op("pool", lambda e: e.tensor_copy(out=KA[:, :, 0:64], in_=rf[:, 0:128].rearrange("p (h d) -> p h d", h=2)),
             reads=[rf], writes=[KA], full=False)
        P.op("pool", lambda e: e.tensor_copy(out=VA_tile("f", i)[:, :, 0:64], in_=rf[:, 128:256].rearrange("p (h d) -> p h d", h=2)),
             reads=[rf], writes=[FVA], full=False)
        if i < 16:
            P.dma("sp", lambda e: e.dma_start(out=O["fox_p"][i * 128:(i + 1) * 128, :], in_=rf[:]), rf, reads=[rf], final=True)
            P.dma("sp", lambda e: e.dma_start(out=O["lf_p"][i * 128:(i + 1) * 128, :], in_=lfi[:]), lfi, reads=[lfi], final=True)
        else:
            P.dma("sp", lambda e: e.dma_start(out=O["fox_s"], in_=rf[:]), rf, reads=[rf], final=True)
            P.dma("sp", lambda e: e.dma_start(out=O["lf_s"], in_=lfi[:]), lfi, reads=[lfi], final=True)
        tri = tri_p if i < 16 else tri_s
        P.op("pe", lambda e: e.matmul(pmisc[:, 0:8], lhsT=tri[:], rhs=lfi[:], start=True, stop=True), reads=[tri, lfi], writes=[pmisc], full=False)
        if i < 16:
            P.op("pe", lambda e: e.matmul(pmisc[:, 8:16], lhsT=ones[:], rhs=lfi[:], start=True, stop=True), reads=[ones, lfi], writes=[pmisc], full=False)
            P.op("dve", lambda e: e.tensor_tensor(out=c8[:], in0=pmisc[:, 0:8], in1=carry[:], op=ALU.add), reads=[pmisc, carry], writes=[c8])
            P.op("dve", lambda e: e.tensor_tensor(out=carry[:], in0=pmisc[:, 8:16], in1=carry[:], op=ALU.add), reads=[pmisc, carry], writes=[carry])
        else:
            P.op("dve", lambda e: e.tensor_copy(out=c8[:], in_=pmisc[:, 0:8]), reads=[pmisc], writes=[c8])
        P.op("dve", lambda e: e.tensor_copy(out=PC[:, :, 0], in_=c8[:]), reads=[c8], writes=[PC], full=False)
        P.op("dve", lambda e: e.tensor_tensor(out=r8[:], in0=c8[:], in1=PC[:, :, 0], op=ALU.subtract), reads=[c8, PC], writes=[r8])
        P.op("dve", lambda e: e.tensor_copy(out=PC[:, :, 1], in_=r8[:]), reads=[r8], writes=[PC], full=False)
        P.op("dve", lambda e: e.tensor_tensor(out=c8[:], in0=r8[:], in1=PC[:, :, 1], op=ALU.subtract), reads=[r8, PC], writes=[c8])
        P.op("dve", lambda e: e.tensor_copy(out=PC[:, :, 2], in_=c8[:]), reads=[c8], writes=[PC], full=False)
        for g in range(2):
            P.op("pool", lambda e, g=g: e.tensor_scalar(out=KA[:, g, 64:76].rearrange("p (r c) -> p r c", c=3), in0=PC[:, 4 * g:4 * g + 4, :],
                                                        scalar1=-1.0, scalar2=None, op0=ALU.mult), reads=[PC], writes=[KA], full=False)
        P.op("pool", lambda e: e.tensor_copy(out=QA[:, :, 76:79], in_=PC[:]), reads=[PC], writes=[QA], full=False)
        zC = zchunk(2)
        headnorm(zC[:, 0:512], 8, gains["nsa_qn_g"], NQA[:, :, 0:64], [NQA])
        P.op("pool", lambda e: e.tensor_copy(out=NQA[:, :, 64:68], in_=alq[:, i, :, :]), reads=[alq], writes=[NQA], full=False)
        zD = zchunk(3)
        P.op("act", lambda e: e.copy(out=rn[:, 0:256], in_=zD[:, 0:256]), reads=[zD], writes=[rn], full=False)
        headnorm(zD[:, 256:384], 2, gains["nsa_kn_slc_g"], rn[:, 256:384].rearrange("p (h d) -> p h d", h=2), [rn])
        P.op("act", lambda e: e.copy(out=rn[:, 384:512], in_=zD[:, 384:512]), reads=[zD], writes=[rn], full=False)
        P.op("pool", lambda e: e.tensor_copy(out=SKA[:, :, 0:64], in_=rn[:, 256:384].rearrange("p (h d) -> p h d", h=2)),
             reads=[rn], writes=[SKA], full=False)
        P.op("pool", lambda e: e.tensor_copy(out=SKA[:, :, 64:68], in_=alk[:, i, :].unsqueeze(1).to_broadcast([128, 2, 4])),
             reads=[alk], writes=[SKA], full=False)
        P.op("pool", lambda e: e.tensor_copy(out=VA_tile("s", i)[:, :, 0:64], in_=rn[:, 384:512].rearrange("p (h d) -> p h d", h=2)),
             reads=[rn], writes=[SVA], full=False)
        if i < 16:
            P.op("pool", lambda e: e.tensor_copy(out=XB[:], in_=rn[:, 0:256]), reads=[rn], writes=[XB])
        zE = zchunk(4)
        headnorm(zE[:, 0:128], 2, gains["nsa_kn_win_g"], rw[:, 0:128].rearrange("p (h d) -> p h d", h=2), [rw])
        P.op("act", lambda e: e.copy(out=rw[:, 128:256], in_=zE[:, 128:256]), reads=[zE], writes=[rw], full=False)
        P.op("act", lambda e: e.activation(out=gates[:, i, :], in_=zE[:, 256:280], func=AF.Sigmoid), reads=[zE], writes=[gates], full=False)
        P.op("pool", lambda e: e.tensor_copy(out=WKA[:, :, 0:64], in_=rw[:, 0:128].rearrange("p (h d) -> p h d", h=2)),
             reads=[rw], writes=[WKA], full=False)
        P.op("pool", lambda e: e.tensor_copy(out=WKA[:, :, 64:68], in_=alk[:, i, :].unsqueeze(1).to_broadcast([128, 2, 4])),
             reads=[alk], writes=[WKA], full=False)
        P.op("pool", lambda e: e.tensor_copy(out=VA_tile("w", i)[:, :, 0:64], in_=rw[:, 128:256].rearrange("p (h d) -> p h d", h=2)),
             reads=[rw], writes=[WVA], full=False)
        if i < 16:
            P.dma("sp", lambda e: e.dma_start(out=O["nsa_p"][i * 128:(i + 1) * 128, :], in_=rn[:]), rn, reads=[rn], final=True)
            if i >= 12:
                P.dma("sp", lambda e: e.dma_start(out=O["win_p"][(i - 12) * 128:(i - 11) * 128, :], in_=rw[:]), rw, reads=[rw], final=True)
        else:
            P.dma("sp", lambda e: e.dma_start(out=O["nsa_s"], in_=rn[:]), rn, reads=[rn], final=True)
            P.dma("sp", lambda e: e.dma_start(out=O["win_s"][:, 504:512, :], in_=rw[:]), rw, reads=[rw], final=True)
        if "c" in KSKIP:
            return
        for j, (src, which, kb) in enumerate(((KA, "f", fkT), (SKA, "s", skT), (WKA, "w", wkT)) if "k" not in KSKIP else ()):
            for g in range(2):
                P.op("pe", lambda e, src=src, g=g, j=j: e.transpose(out=ptr[:, (2 * j + g) * 128:(2 * j + g + 1) * 128], in_=src[:, g, :], identity=ident[:]),
                     reads=[src, ident], writes=[ptr], full=False)
        if i < 16 and "x" not in KSKIP:
            for ty in range(2):
                P.op("pe", lambda e, ty=ty: e.transpose(out=ptr[:, (6 + ty) * 128:(7 + ty) * 128], in_=XB[:, ty * 128:(ty + 1) * 128], identity=ident[:]),
                     reads=[XB, ident], writes=[ptr], full=False)
        for j, (which, kb) in enumerate((("f", fkT), ("s", skT), ("w", wkT)) if "k" not in KSKIP else ()):
            eng = "act" if j != 1 else "dve"
            P.op(eng, lambda e, j=j, which=which, eng=eng: e.tensor_copy(out=kT_tile2(which, i), in_=ptr[:, 2 * j * 128:(2 * j + 2) * 128].rearrange("p (g c) -> p g c", g=2))
                 if eng == "dve" else e.copy(out=kT_tile2(which, i), in_=ptr[:, 2 * j * 128:(2 * j + 2) * 128].rearrange("p (g c) -> p g c", g=2)),
                 reads=[ptr], writes=[kb], full=False)
        if i < 16 and "x" not in KSKIP:
            for g in range(2):
                P.op("dve", lambda e, g=g: e.tensor_copy(
                    out=XT[g * 64:(g + 1) * 64, 2 * g:2 * g + 2, :].rearrange("p t (c n) -> p t c n", c=16)[:, :, :, li * 8:(li + 1) * 8],
                    in_=ptr[g * 64:(g + 1) * 64, 768:1024].rearrange("p (t n c) -> p t c n", t=2, c=16)),
                    reads=[ptr], writes=[XT], full=False)
        if "q" not in KSKIP:
            for (src, dstb, dv) in ((QA, fqT, fqT_v), (NQA, nqT, nqT_v))[int(os.environ.get("KQ0", "0")):int(os.environ.get("KQ1", "2"))]:
                for h in range(8):
                    P.op("pe", lambda e, src=src, h=h: e.transpose(out=ptr[:, h * 128:(h + 1) * 128], in_=src[:, h, :], identity=ident[:]),
                         reads=[src, ident], writes=[ptr], full=False)
                if "V" not in KSKIP:
                    P.op("dve", lambda e, dv=dv: e.tensor_copy(out=dv[:, :, li * 128:(li + 1) * 128], in_=ptr[:].rearrange("p (h c) -> p h c", h=8)),
                         reads=[ptr], writes=[dstb], full=False)
                else:
                    P.op("act", lambda e, dv=dv: e.copy(out=dv[:, :, li * 128:(li + 1) * 128], in_=ptr[:].rearrange("p (h c) -> p h c", h=8)),
                         reads=[ptr], writes=[dstb], full=False)

    def compress(sg):
        for ty in range(2):
            tyn = "kv"[ty]
            for g in range(2):
                for lt in range(2):
                    col = ((ty * 2 + g) * 2 + lt) * 32
                    for c in range(16):
                        P.op("pe", lambda e, tyn=tyn, ty=ty, g=g, lt=lt, c=c, col=col: e.matmul(
                            pmisc[:, col:col + 32], lhsT=w1[tyn][:, lt * 16 + c, :],
                            rhs=XT[:, 2 * g + ty, c * 32:(c + 1) * 32], start=(c == 0), stop=(c == 15)),
                            reads=[w1[tyn], XT], writes=[pmisc], full=False)
        P.op("act", lambda e: e.copy(out=Lsb[:].rearrange("p a c -> p (a c)"), in_=pmisc[:, 0:256]), reads=[pmisc], writes=[Lsb])
        if KCMP < 2:
            return
        L4 = Lsb[:].rearrange("p (a l) c -> p a l c", l=2)
        P.op("dve", lambda e: e.tensor_tensor(out=pre[:, :, 1:32], in0=L4[:, :, 0, 0:31], in1=L4[:, :, 1, 1:32], op=ALU.add), reads=[Lsb], writes=[pre], full=False)
        P.op("dve", lambda e: e.tensor_tensor(out=pre[:, :, 0:1], in0=leadprev[:].unsqueeze(2), in1=L4[:, :, 1, 0:1], op=ALU.add),
             reads=[Lsb, leadprev], writes=[pre], full=False)
        P.op("dve", lambda e: e.tensor_copy(out=leadprev[:].unsqueeze(2), in_=L4[:, :, 0, 31:32]), reads=[Lsb], writes=[leadprev])
        for ty in range(2):
            P.op("act", lambda e, ty=ty: e.activation(out=Spad[:, 2 * ty:2 * ty + 2, 32 * sg:32 * sg + 32], in_=pre[:, 2 * ty:2 * ty + 2, :],
                                                      func=AF.Silu, bias=cbias[:, ty:ty + 1]), reads=[pre, cbias], writes=[Spad], full=False)
        if KCMP < 3:
            return
        pzb = nxt(pz, "pz")
        for ty in range(2):
            tyn = "kv"[ty]
            for g in range(2):
                cc = (ty * 2 + g) * 64
                P.op("pe", lambda e, ty=ty, g=g, cc=cc, tyn=tyn: e.matmul(pzb[:, cc:cc + 64], lhsT=Spad[:, 2 * ty + g, :], rhs=w2[tyn][:],
                                                                         start=True, stop=True), reads=[Spad, w2[tyn]], writes=[pzb], full=False)
        if KCMP < 4:
            return
        p0, p1 = 32 * sg, 32 * sg + 32
        src_buf[0] = pzb
        headnorm(pzb[p0:p1, 0:128], 2, gains["nsa_kn_cmp_g"], kcn[p0:p1, :].rearrange("p (h d) -> p h d", h=2), [kcn], p0=p0, p1=p1)
        if KCMP < 5:
            return
        P.op("pool", lambda e: e.tensor_copy(out=KCA[p0:p1, :, 0:64], in_=kcn[p0:p1, :].rearrange("p (h d) -> p h d", h=2)),
             reads=[kcn], writes=[KCA], full=False)
        P.op("act", lambda e: e.copy(out=VCA[p0:p1, :, 0:64], in_=pzb[p0:p1, 128:256].rearrange("p (h d) -> p h d", h=2)),
             reads=[pzb], writes=[VCA], full=False)
        if KCMP < 6:
            return
        for g in range(2):
            P.op("pe", lambda e, g=g: e.transpose(out=ptr[:, g * 128:(g + 1) * 128], in_=KCA[:, g, :], identity=ident[:]),
                 reads=[KCA, ident], writes=[ptr], full=False)
        P.op("act", lambda e: e.copy(out=kcT[:], in_=ptr[:, 0:256].rearrange("p (g c) -> p g c", g=2)), reads=[ptr], writes=[kcT])

    def attend(g, q_rhs, q_buf, k_list, pog):
        n = len(k_list)
        for idx, (kT, kTb, va, vab, masks, keep) in enumerate(k_list):
            ps = nxt(pss, "pss")
            P.op("pe", lambda e, ps=ps, kT=kT, masks=masks: e.matmul(ps[:, 0:512], lhsT=kT, rhs=q_rhs, start=True, stop=(len(masks) == 0)),
                 reads=[kTb, q_buf], writes=[ps], full=False)
            for mi, (ml, mr, c0, c1, mb) in enumerate(masks):
                P.op("pe", lambda e, ps=ps, ml=ml, mr=mr, c0=c0, c1=c1, mi=mi, masks=masks: e.matmul(
                    ps[:, c0:c1], lhsT=ml, rhs=mr, start=False, stop=(mi == len(masks) - 1), skip_group_check=True),
                    reads=list(mb), writes=[ps], full=False)
            if keep is not None:
                pt_ap, pt_buf = keep
            else:
                ptb = nxt(PT, "pt")
                pt_ap, pt_buf = ptb[:], ptb
            P.op("act", lambda e, ps=ps, pt_ap=pt_ap: e.activation(out=pt_ap, in_=ps[:, 0:512], func=AF.Exp), reads=[ps], writes=[pt_buf],
                 full=(keep is None))
            for h in range(4):
                P.op("pe", lambda e, h=h, idx=idx, pt_ap=pt_ap, va=va: e.matmul(
                    pog[:, h * 65:(h + 1) * 65], lhsT=pt_ap[:, h * 128:(h + 1) * 128], rhs=va,
                    start=(idx == 0 and h == 0), stop=(idx == n - 1), skip_group_check=True),
                    reads=[pt_buf, vab], writes=[pog], full=False)

    def den_recip(pog, g):
        dv = pog[:, 0:260].rearrange("p (h c) -> p h c", c=65)[:, :, 64]
        P.op("dve", lambda e: e.tensor_scalar(out=den8[:, 4 * g:4 * g + 4], in0=dv, scalar1=1e-30, scalar2=None, op0=ALU.max),
             reads=[pog], writes=[den8], full=False)
        P.op("dve", lambda e: e.reciprocal(out=den8[:, 4 * g:4 * g + 4], in_=den8[:, 4 * g:4 * g + 4]), reads=[den8], writes=[den8], full=False)

    def po_view(pog):
        return pog[:, 0:260].rearrange("p (h c) -> p h c", c=65)[:, :, 0:64]

    def q_rhs(qv, g, li):
        return qv[:, 4 * g:4 * g + 4, li * 128:(li + 1) * 128]

    def attention_prompt(i):
        li = i % 4
        sl = ot
        o_fox = ot[:, 0:512]
        o_nsa = ot[:, 512:1024]
        diag = (ident[:], cm_diag[:], 0, 512, [ident, cm_diag])
        for g in range(2):
            kl = []
            for kt in range(i + 1):
                masks = [diag] if kt == i else []
                kl.append((kT_ap("f", g, kt * 128, (kt + 1) * 128), fkT, VA_ap("f", kt, g), FVA, masks, None))
            attend(g, q_rhs(fqT_v, g, li), fqT, kl, po[g])
            den_recip(po[g], g)
            P.op("dve", lambda e, g=g: e.tensor_tensor(out=o_fox[:, g * 256:(g + 1) * 256].rearrange("p (h d) -> p h d", h=4), in0=po_view(po[g]),
                                                       in1=den8[:, 4 * g:4 * g + 4].unsqueeze(2).to_broadcast([128, 4, 64]), op=ALU.mult),
                 reads=[po[g], den8], writes=[sl], full=False)
        gate3 = gates[:, i, :].rearrange("p (h k) -> p h k", k=3)
        for g in range(2):
            masks = [(ident[:], cmaskc[:, i * 128:(i + 1) * 128], h * 128, (h + 1) * 128, [ident, cmaskc]) for h in range(4)]
            kl = [(kcT[:, g, :], kcT, VCA[:, g, :], VCA, masks, (PTc[:, g, :], PTc))]
            attend(g, q_rhs(nqT_v, g, li), nqT, kl, po[g])
            den_recip(po[g], g)
        if i >= 8:
            for g in range(2):
                for h in range(4):
                    hh = 4 * g + h
                    P.op("pe", lambda e, g=g, h=h, hh=hh: e.matmul(pmisc[:, hh * 32:(hh + 1) * 32], lhsT=PTc[:, g, h * 128:(h + 1) * 128], rhs=bmat[:],
                                                                  start=True, stop=True), reads=[PTc, bmat], writes=[pmisc], full=False)
            P.op("dve", lambda e: e.tensor_tensor(out=imp[:], in0=pmisc[:, 0:256].rearrange("p (h j) -> p h j", h=8),
                                                  in1=den8[:].unsqueeze(2).to_broadcast([128, 8, 32]), op=ALU.mult), reads=[pmisc, den8], writes=[imp])
            P.op("dve", lambda e: e.reduce_sum(out=score[:], in_=imp[:].rearrange("p (g h) j -> p g j h", g=2), axis=AX.X), reads=[imp], writes=[score])
            P.op("dve", lambda e: e.tensor_tensor(out=score[:], in0=score[:], in1=vz[:, i, :].unsqueeze(1).to_broadcast([128, 2, 32]), op=ALU.mult),
                 reads=[score, vz], writes=[score])
            P.op("dve", lambda e: e.tensor_tensor(out=score[:], in0=score[:], in1=fz2[:, i, :].unsqueeze(1).to_broadcast([128, 2, 32]), op=ALU.add),
                 reads=[score, fz2], writes=[score])
            for g in range(2):
                P.op("dve", lambda e, g=g: e.max(out=mx8[:, g, :], in_=score[:, g, :]), reads=[score], writes=[mx8], full=False)
                P.op("dve", lambda e, g=g: e.match_replace(out=swork[:, g, :], in_to_replace=mx8[:, g, :], in_values=score[:, g, :], imm_value=-2.0),
                     reads=[score, mx8], writes=[swork], full=False)
                P.op("dve", lambda e, g=g: e.max(out=mx8[:, g, :], in_=swork[:, g, :]), reads=[swork], writes=[mx8], full=False)
                P.op("dve", lambda e, g=g: e.tensor_reduce(out=thr[:, g:g + 1], in_=mx8[:, g, :], axis=AX.X, op=ALU.min), reads=[mx8], writes=[thr], full=False)
                P.op("dve", lambda e, g=g: e.tensor_scalar(out=nsel[:, g, :], in0=score[:, g, :], scalar1=thr[:, g:g + 1], scalar2=NEG,
                                                           op0=ALU.is_lt, op1=ALU.mult), reads=[score, thr], writes=[nsel], full=False)
            for g in range(2):
                P.op("pe", lambda e, g=g: e.transpose(out=ptr[0:32, g * 128:(g + 1) * 128], in_=nsel[:, g, :], identity=ident[:]),
                     reads=[nsel, ident], writes=[ptr], full=False)
            P.op("dve", lambda e: e.tensor_copy(out=nselT[0:32, :, :].rearrange("p g (h c) -> p g h c", h=4),
                                                in_=ptr[0:32, 0:256].rearrange("p (g c) -> p g c", g=2).unsqueeze(2).to_broadcast([32, 2, 4, 128])),
                 reads=[ptr], writes=[nselT])
        P.op("dve", lambda e: e.tensor_tensor(out=wk8[:], in0=den8[:], in1=gate3[:, :, 0], op=ALU.mult), reads=[den8, gates], writes=[wk8])
        for g in range(2):
            P.op("dve", lambda e, g=g: e.tensor_tensor(out=oacc[:, g * 256:(g + 1) * 256].rearrange("p (h d) -> p h d", h=4), in0=po_view(po[g]),
                                                       in1=wk8[:, 4 * g:4 * g + 4].unsqueeze(2).to_broadcast([128, 4, 64]), op=ALU.mult),
                 reads=[po[g], wk8], writes=[oacc], full=False)
        for br, which, kb, vb, kts in ((1, "s", skT, SVA, list(range(i + 1))), (2, "w", wkT, WVA, list(range(max(0, i - 4), i + 1)))):
            for g in range(2):
                kl = []
                for kt in kts:
                    masks = []
                    if kt == i:
                        masks.append(diag)
                    if br == 2 and kt == i - 4:
                        masks.append((ident[:], cm_win[:], 0, 512, [ident, cm_win]))
                    if br == 1 and i >= 8:
                        masks.append((esel[:, kt, :], nselT[0:32, g, :], 0, 512, [esel, nselT]))
                    kl.append((kT_ap(which, g, kt * 128, (kt + 1) * 128), kb, VA_ap(which, kt, g), vb, masks, None))
                attend(g, q_rhs(nqT_v, g, li), nqT, kl, po[g])
                den_recip(po[g], g)
            P.op("dve", lambda e, br=br: e.tensor_tensor(out=wk8[:], in0=den8[:], in1=gate3[:, :, br], op=ALU.mult), reads=[den8, gates], writes=[wk8])
            for g in range(2):
                P.op("dve", lambda e, g=g: e.tensor_tensor(out=otmp[:, g * 256:(g + 1) * 256].rearrange("p (h d) -> p h d", h=4), in0=po_view(po[g]),
                                                           in1=wk8[:, 4 * g:4 * g + 4].unsqueeze(2).to_broadcast([128, 4, 64]), op=ALU.mult),
                     reads=[po[g], wk8], writes=[otmp], full=False)
            if br == 1:
                P.op("dve", lambda e: e.tensor_tensor(out=oacc[:], in0=oacc[:], in1=otmp[:], op=ALU.add), reads=[oacc, otmp], writes=[oacc])
            else:
                P.op("dve", lambda e: e.tensor_tensor(out=o_nsa, in0=oacc[:], in1=otmp[:], op=ALU.add), reads=[oacc, otmp], writes=[sl], full=False)
        P.dma("sp", lambda e: e.dma_start(out=oscr[i], in_=ot[:]), ot, reads=[ot], writes=[oscr_b[i]])

    for sg in range(4):
        for i in range(4 * sg, 4 * sg + 4):
            premixer(i)
        if STAGE >= 1:
            compress(sg)
        for i in range(4 * sg, 4 * sg + 4):
            if STAGE >= 2:
                attention_prompt(i)
    premixer(16)
    def mk_carver(flat_ap, nelem, bufs):
        off = [0]

        def cv(name, shape, dt):
            n = 1
            for d_ in shape[1:]:
                n *= d_
            nb = n * (2 if dt in (F32, I32) else 1)
            ap = flat_ap[:, off[0]:off[0] + nb]
            off[0] += nb
            assert off[0] <= nelem, (name, off[0], nelem)
            if dt in (F32, I32):
                ap = ap.bitcast(dt)
            if len(shape) == 3:
                ap = ap.rearrange("p (a b) -> p a b", a=shape[1])
            elif len(shape) == 4:
                ap = ap.rearrange("p (a b c) -> p a b c", a=shape[1], b=shape[2])
            elif len(shape) == 5:
                ap = ap.rearrange("p (a b c d) -> p a b c d", a=shape[1], b=shape[2], c=shape[3])
            if shape[0] < 128:
                ap = ap[0:shape[0]]
            b = Buf(name, ap)
            bufs.append(b)
            return b
        return cv

    sbufs = []
    c1bufs = []
    cw = mk_carver(wbf[:].rearrange("p k c -> p (k c)"), 8 * 2080, sbufs)
    ca = mk_carver(arena_t[:, 8192:18 * 1024], 10 * 1024, sbufs)
    c1 = mk_carver(arena_t[:, 0:8192], 8192, c1bufs)
    fresh_t = nc.alloc_sbuf_tensor("sb_sfresh", [128, 2304], BF16)
    cs = mk_carver(fresh_t[:, :], 2304, sbufs)
    stg = cw("stg", [128, 4096], F32)
    KAs = cw("KAs", [128, 16, 2, 80], BF16)
    VAs = cw("VAs", [128, 16, 2, 65], BF16)
    KTs = cw("KTs", [128, 1024], BF16)
    Ls = cw("Ls", [128, 4, 128], F32)
    sbfc = KAs.t.rearrange("p a b c -> p (a b c)")[:, 0:2048].rearrange("p (t q r d) -> p t q r d", t=4, q=4, r=2)
    W1s = {"k": ca("W1sk", [128, 16, 128], BF16), "v": ca("W1sv", [128, 16, 128], BF16)}
    OTs = ca("OTs", [128, 1024], F32)
    Lf0 = ca("Lf0", [128, 4, 16, 8], F32)
    Lf1 = ca("Lf1", [128, 4, 16, 8], F32)
    PCs = ca("PCs", [128, 4, 16, 24], BF16)
    kcTs = c1("kcTs", [128, 2, 512], BF16)
    VCAs = c1("VCAs", [128, 4, 2, 65], BF16)
    KCAs = c1("KCAs", [128, 2, 80], BF16)
    Ssp = c1("Ssp", [128, 4, 128], BF16)
    PTcs = c1("PTcs", [128, 256], BF16)
    nselTs = c1("nselTs", [128, 2, 32], BF16)
    lt64 = c1("lt64", [128, 2, 64], F32)
    pre64 = c1("pre64", [128, 64], F32)
    lps = c1("lps", [128, 4], F32)
    ctot = c1("ctot", [128, 4, 8], F32)
    scb = c1("scb", [128, 2, 4, 8], F32)
    S2 = c1("S2", [128, 4, 8], F32)
    impn = c1("impn", [32, 2, 132], F32)
    scs = c1("scs", [32, 132], F32)
    sws = c1("sws", [32, 132], F32)
    mx8s = c1("mx8s", [32, 8], F32)
    thrs = c1("thrs", [32, 2], F32)
    nsels = c1("nsels", [32, 128], BF16)
    ghall = c1("ghall", [32, 24], F32)
    ghm = c1("ghm", [32, 4, 3], F32)
    gh = c1("gh", [32, 2, 3], F32)
    rd1 = c1("rd1", [32, 2], F32)
    wk1 = c1("wk1", [32, 2], F32)
    oaccs = c1("oaccs", [32, 2, 64], F32)
    otmps = c1("otmps", [32, 2, 64], F32)
    osb = c1("osb", [32, 4, 64], BF16)
    KTs2 = c1("KTs2", [128, 1024], BF16)
    KTb = [KTs, KTs2]
    idxf = cs("idxf", [128, 16, 12], F32)
    idxi = cs("idxi", [128, 16, 12], I32)
    ptq = cs("ptq", [128, 16, 12], I32)
    clist = []

    def cdef(cv, name, shape, dt, src):
        b = cv(name, shape, dt)
        clist.append((b, src))
        return b
    bmask = cdef(ca, "bmask", [128, 512], BF16, I["bmask"])
    esels = cdef(cs, "esels", [128, 8, 128], BF16, I["esels"].rearrange("p (t j) -> p t j", t=8))
    zall = cdef(cw, "zall", [128, 512], F32, I["zall"])
    bmats = cdef(cw, "bmats", [128, 4, 132], BF16, I["bmats"].rearrange("p (t j) -> p t j", t=4))
    zsel = cdef(c1, "zsel", [32, 4, 256], BF16, I["zsel"].rearrange("p (h c) -> p h c", h=4))
    alks_n = cdef(c1, "alks_n", [128, 8, 8, 4], BF16, I["alks_n"].rearrange("p (b r c) -> p b r c", b=8, r=8))
    alkw = cdef(c1, "alkw", [128, 4, 4], BF16, I["alkw"].rearrange("p (t c) -> p t c", t=4))
    alkcs = cdef(c1, "alkcs", [128, 4, 4], BF16, I["alkcs"].rearrange("p (t c) -> p t c", t=4))
    ustr = cdef(c1, "ustr", [128, 128], F32, I["ustr"])
    s0m = cdef(c1, "s0m", [128, 32], BF16, I["s0m"])
    wm0 = cdef(c1, "wm0", [128, 32], BF16, I["wm0"])
    gsum = cdef(cs, "gsum", [32, 32], F32, I["gsum"])
    fz2s = cdef(c1, "fz2s", [32, 132], F32, I["fz2s"])
    ohc = cdef(c1, "ohc", [32, 4, 3], F32, I["ohc"].rearrange("p (h k) -> p h k", h=4))
    mulc = cdef(cs, "mulc", [128, 12], F32, I["mulc"])
    addc = cdef(cs, "addc", [128, 12], F32, I["addc"])
    clist.append((ptq, I["ptq"].rearrange("p (s b) -> p s b", s=16)))
    P.op("pool", lambda e: e.memset(ssq[:], 0.0), reads=[], writes=[wbf, fqT, nqT] + [b for b in ar_bufs if b not in (w1["k"], w1["v"])] + sbufs + [ssq])
    for ty in ("k", "v"):
        for r2 in range(2):
            P.op("dve", lambda e, ty=ty, r2=r2: e.tensor_copy(
                out=W1s[ty][r2 * 64:(r2 + 1) * 64, :, :],
                in_=w1[ty][r2 * 64:(r2 + 1) * 64, :, :].rearrange("p (a r) h -> p a r h", r=2)[:, :, r2, :]),
                reads=[w1[ty]], writes=[W1s[ty]], full=False)
    P.op("pool", lambda e: e.memset(ssq[:], 0.0), reads=[], writes=[w1["k"], w1["v"], ssq] + c1bufs)
    sbufs += c1bufs
    for (b_, src_) in clist:
        P.dma("sp", lambda e, b_=b_, src_=src_: e.dma_start(out=b_[:], in_=src_), b_, writes=[b_])
    P.op("dve", lambda e: e.tensor_copy(out=idxf[:], in_=ptq[:]), reads=[ptq], writes=[idxf])
    P.op("dve", lambda e: e.tensor_tensor(out=idxf[:], in0=idxf[:], in1=mulc[:].unsqueeze(1).to_broadcast([128, 16, 12]), op=ALU.mult), reads=[idxf, mulc], writes=[idxf])
    P.op("dve", lambda e: e.tensor_tensor(out=idxf[:], in0=idxf[:], in1=addc[:].unsqueeze(1).to_broadcast([128, 16, 12]), op=ALU.add), reads=[idxf, addc], writes=[idxf])
    P.op("dve", lambda e: e.tensor_copy(out=idxi[:], in_=idxf[:]), reads=[idxf], writes=[idxi])
    for b_ in (KAs, VAs, OTs, S2, Ssp, KCAs, VCAs, kcTs):
        P.op("pool", lambda e, b_=b_: e.memset(b_[:], 0.0), writes=[b_])
    P.op("pool", lambda e: e.memset(VCAs[:, :, :, 64:65], 1.0), writes=[VCAs], full=False)
    gates16 = gates[:, 16, :]

    def ktile_ones():
        P.op("pool", lambda e: e.memset(KAs[:, :, :, 64:80], 0.0), writes=[KAs], full=False)
        P.op("pool", lambda e: e.memset(VAs[:, :, :, 64:65], 1.0), writes=[VAs], full=False)

    fox_rows = I["cache_fox_kv"]
    nsa_cmp_rows = I["cache_nsa_cmp"]
    nsa_slc_rows = I["cache_nsa_slc"]
    stgh = [Buf("stgh0", stg.t[:, 0:2048]), Buf("stgh1", stg.t[:, 2048:4096])]
    sbufs.extend(stgh)

    def stg_guard():
        P.op("pool", lambda e: e.memset(ssq[:], 0.0), reads=[], writes=[stg, stgh[0], stgh[1], ssq])
    lf_rows = I["cache_fox_logf"]

    def gather(out_ap, out_buf, rows, s, col):
        P.dma("pool", lambda e: e.indirect_dma_start(out=out_ap, out_offset=None, in_=rows,
                                                     in_offset=bass.IndirectOffsetOnAxis(ap=idxi[:, s, col:col + 1], axis=0)),
              out_buf, reads=[idxi], writes=[out_buf])

    rcnt = [0]

    def round_a(pairs, nrow, q_rhs_fn, qbuf, masks_fn):
        n = len(pairs)
        kt = KTb[rcnt[0] % 2]
        rcnt[0] += 1
        for j, (ka, kab, va, vab, g) in enumerate(pairs):
            P.op("pe", lambda e, j=j, ka=ka: e.transpose(out=ptr[0:nrow, j * 128:(j + 1) * 128], in_=ka, identity=ident[:]),
                 reads=[kab, ident], writes=[ptr], full=False)
        P.op("dve", lambda e: e.tensor_copy(out=kt[0:nrow, 0:n * 128], in_=ptr[0:nrow, 0:n * 128]), reads=[ptr], writes=[kt])
        ps = nxt(pss, "pss")
        for j, (ka, kab, va, vab, g) in enumerate(pairs):
            ms = masks_fn(j, g)
            P.op("pe", lambda e, j=j, g=g, ms=ms: e.matmul(ps[:, j * 32:(j + 1) * 32], lhsT=kt[0:nrow, j * 128:(j + 1) * 128], rhs=q_rhs_fn(g, nrow),
                                                          start=True, stop=(len(ms) == 0)), reads=[kt, qbuf], writes=[ps], full=False)
            for mi, (ml, mr, mb) in enumerate(ms):
                P.op("pe", lambda e, j=j, ml=ml, mr=mr, mi=mi, ms=ms: e.matmul(ps[:, j * 32:(j + 1) * 32], lhsT=ml, rhs=mr, start=False,
                                                                             stop=(mi == len(ms) - 1), skip_group_check=True),
                     reads=list(mb), writes=[ps], full=False)
        return (pairs, ps)

    def round_b(state, acc_started):
        pairs, ps = state
        n = len(pairs)
        ptb = nxt(PT, "pt")
        P.op("act", lambda e: e.activation(out=ptb[:, 0:n * 32], in_=ps[:, 0:n * 32], func=AF.Exp), reads=[ps], writes=[ptb])
        for j, (ka, kab, va, vab, g) in enumerate(pairs):
            first = not acc_started[g]
            acc_started[g] = True
            P.op("pe", lambda e, j=j, g=g, va=va, first=first: e.matmul(po[g][0:32, 0:65], lhsT=ptb[:, j * 32:(j + 1) * 32], rhs=va,
                                                                       start=first, stop=False, skip_group_check=True),
                 reads=[ptb, vab], writes=[po[g]], full=False)

    def run_rounds(items, acc_started):
        pending = None
        for kind, arg in items:
            if kind == "pre":
                arg()
                continue
            pairs_fn, nrow, qf, qb, mf = arg
            st = round_a(pairs_fn(), nrow, qf, qb, mf)
            if pending is not None:
                round_b(pending, acc_started)
            pending = st
        if pending is not None:
            round_b(pending, acc_started)

    def s_round(pairs, nrow, q_rhs_fn, qbuf, masks_fn, acc_started):
        round_b(round_a(pairs, nrow, q_rhs_fn, qbuf, masks_fn), acc_started)

    def new_tile(which, kb, vb, q_rhs_fn, qbuf, s, acc_started):
        ps = nxt(pss, "pss")
        for g in range(2):
            P.op("pe", lambda e, g=g: e.matmul(ps[:, g * 32:(g + 1) * 32], lhsT=kT_ap(which, g, 2048, 2176), rhs=q_rhs_fn(g, 128), start=True, stop=False),
                 reads=[kb, qbuf], writes=[ps], full=False)
            P.op("pe", lambda e, g=g: e.matmul(ps[:, g * 32:(g + 1) * 32], lhsT=ident[:], rhs=bmask[:, s * 32:(s + 1) * 32], start=False, stop=True,
                                               skip_group_check=True), reads=[ident, bmask], writes=[ps], full=False)
        ptb = nxt(PT, "pt")
        P.op("act", lambda e: e.activation(out=ptb[:, 0:64], in_=ps[:, 0:64], func=AF.Exp), reads=[ps], writes=[ptb])
        for g in range(2):
            first = not acc_started[g]
            acc_started[g] = True
            P.op("pe", lambda e, g=g, first=first: e.matmul(po[g][0:32, 0:65], lhsT=ptb[:, g * 32:(g + 1) * 32], rhs=VA_ap(which, 16, g),
                                                           start=first, stop=True, skip_group_check=True), reads=[ptb, vb], writes=[po[g]], full=False)

    def s_den():
        for g in range(2):
            P.op("dve", lambda e, g=g: e.tensor_scalar(out=rd1[:, g:g + 1], in0=po[g][0:32, 64:65], scalar1=1e-30, scalar2=None, op0=ALU.max),
                 reads=[po[g]], writes=[rd1], full=False)
        P.op("dve", lambda e: e.reciprocal(out=rd1[:], in_=rd1[:]), reads=[rd1], writes=[rd1])

    def fq_rhs(s):
        return lambda g, nrow: fqT_v[0:nrow, 4 * g:4 * g + 4, 8 * s:8 * s + 8]

    def nq_rhs(s):
        return lambda g, nrow: nqT_v[0:nrow, 4 * g:4 * g + 4, 8 * s:8 * s + 8]

    nomask = lambda j, g: []

    def sample(s):
        P.op("pe", lambda e: e.matmul(pmisc[0:32, 300:324], lhsT=zall[:, s * 32:(s + 1) * 32], rhs=gates16, start=True, stop=True),
             reads=[zall, gates], writes=[pmisc], full=False)
        P.op("act", lambda e: e.copy(out=ghall[:], in_=pmisc[0:32, 300:324]), reads=[pmisc], writes=[ghall])
        for g in range(2):
            P.op("dve", lambda e, g=g: e.tensor_tensor(out=ghm[:], in0=ghall[:, g * 12:(g + 1) * 12].rearrange("p (h k) -> p h k", k=3), in1=ohc[:], op=ALU.mult),
                 reads=[ghall, ohc], writes=[ghm])
            P.op("dve", lambda e, g=g: e.reduce_sum(out=gh[:, g, :], in_=ghm[:].rearrange("p h k -> p k h"), axis=AX.X), reads=[ghm], writes=[gh], full=False)
        for b in range(4):
            gather(Lf0[:, b, :, :].rearrange("p r h -> p (r h)"), Lf0, lf_rows, s, b)
        P.op("dve", lambda e: e.reduce_sum(out=ctot[:], in_=Lf0[:].rearrange("p b r h -> p b h r"), axis=AX.X), reads=[Lf0], writes=[ctot])
        P.op("pe", lambda e: e.matmul(pmisc[:, 0:32], lhsT=ustr[:], rhs=ctot[:].rearrange("p b h -> p (b h)"), start=True, stop=True),
             reads=[ustr, ctot], writes=[pmisc], full=False)
        P.op("pe", lambda e: e.matmul(pmisc[:, 32:64], lhsT=ones[:], rhs=ctot[:].rearrange("p b h -> p (b h)"), start=True, stop=True),
             reads=[ones, ctot], writes=[pmisc], full=False)
        P.op("act", lambda e: e.copy(out=scb[:].rearrange("p a b h -> p (a b h)"), in_=pmisc[:, 0:64]), reads=[pmisc], writes=[scb])
        P.op("dve", lambda e: e.tensor_copy(out=S2[:, 2, :], in_=scb[:, 1, 3, :]), reads=[scb], writes=[S2], full=False)
        P.op("dve", lambda e: e.tensor_tensor(out=S2[:, 1, :], in0=S2[:, 2, :], in1=scb[:, 1, 2, :], op=ALU.add), reads=[scb, S2], writes=[S2], full=False)
        P.op("dve", lambda e: e.tensor_tensor(out=S2[:, 0, :], in0=S2[:, 1, :], in1=scb[:, 1, 1, :], op=ALU.add), reads=[scb, S2], writes=[S2], full=False)
        P.op("dve", lambda e: e.tensor_tensor(out=ctot[:], in0=S2[:], in1=scb[:, 0, :, :], op=ALU.add), reads=[scb, S2], writes=[ctot])
        P.op("pool", lambda e: e.memset(Lf1[:, :, 15:16, :], 0.0), writes=[Lf1], full=False)
        P.op("dve", lambda e: e.tensor_copy(out=Lf1[:, :, 0:15, :], in_=Lf0[:, :, 1:16, :]), reads=[Lf0], writes=[Lf1], full=False)
        src, dst = Lf1, Lf0
        for k in (1, 2, 4, 8):
            P.op("dve", lambda e, src=src, dst=dst, k=k: e.tensor_tensor(out=dst[:, :, 0:16 - k, :], in0=src[:, :, 0:16 - k, :], in1=src[:, :, k:16, :], op=ALU.add),
                 reads=[src], writes=[dst], full=False)
            P.op("pool", lambda e, src=src, dst=dst, k=k: e.tensor_copy(out=dst[:, :, 16 - k:16, :], in_=src[:, :, 16 - k:16, :]), reads=[src], writes=[dst], full=False)
            src, dst = dst, src
        suf, tmpb = src, dst
        P.op("dve", lambda e: e.tensor_tensor(out=suf[:], in0=suf[:], in1=ctot[:].unsqueeze(2).to_broadcast([128, 4, 16, 8]), op=ALU.add),
             reads=[suf, ctot], writes=[suf])
        PC5 = PCs[:].rearrange("p b r (h c) -> p b r h c", c=3)
        P.op("dve", lambda e: e.tensor_copy(out=PC5[:, :, :, :, 0], in_=suf[:]), reads=[suf], writes=[PCs], full=False)
        P.op("dve", lambda e: e.tensor_tensor(out=tmpb[:], in0=suf[:], in1=PC5[:, :, :, :, 0], op=ALU.subtract), reads=[suf, PCs], writes=[tmpb])
        P.op("dve", lambda e: e.tensor_copy(out=PC5[:, :, :, :, 1], in_=tmpb[:]), reads=[tmpb], writes=[PCs], full=False)
        P.op("dve", lambda e: e.tensor_tensor(out=suf[:], in0=tmpb[:], in1=PC5[:, :, :, :, 1], op=ALU.subtract), reads=[tmpb, PCs], writes=[suf])
        P.op("dve", lambda e: e.tensor_copy(out=PC5[:, :, :, :, 2], in_=suf[:]), reads=[suf], writes=[PCs], full=False)
        ktile_ones()
        P.op("pool", lambda e: e.memset(KAs[:, :, :, 76:79], 1.0), writes=[KAs], full=False)
        acc = [False, False]
        stg4 = stg[:].rearrange("p (r t d) -> p r t d", r=16, t=4)
        stg_guard()
        items = []
        for b in range(4):
            def pre(b=b):
                gather(stg[:], stg, fox_rows, s, b)
                P.op("dve", lambda e: e.tensor_copy(out=KAs[:, :, :, 0:64], in_=stg4[:, :, 0:2, :]), reads=[stg], writes=[KAs], full=False)
                P.op("pool", lambda e: e.tensor_copy(out=VAs[:, :, :, 0:64], in_=stg4[:, :, 2:4, :]), reads=[stg], writes=[VAs], full=False)
                P.op("pool", lambda e: e.tensor_copy(out=KAs[:, :, :, 64:76], in_=PCs[:, b, :, :].rearrange("p r (g c) -> p r g c", g=2)),
                     reads=[PCs], writes=[KAs], full=False)
            items.append(("pre", pre))
            for rnd in range(4):
                pf = lambda rnd=rnd: [(KAs[:, r, g, :], KAs, VAs[:, r, g, :], VAs, g) for r in range(4 * rnd, 4 * rnd + 4) for g in range(2)]
                items.append(("round", (pf, 80, fq_rhs(s), fqT, nomask)))
        run_rounds(items, acc)
        new_tile("f", fkT, FVA, fq_rhs(s), fqT, s, acc)
        s_den()
        for g in range(2):
            P.op("dve", lambda e, g=g: e.tensor_scalar(out=osb[:, g, :], in0=po[g][0:32, 0:64], scalar1=rd1[:, g:g + 1], scalar2=None, op0=ALU.mult),
                 reads=[po[g], rd1], writes=[osb], full=False)
        P.op("pool", lambda e: e.memset(lps[:], 0.0), writes=[lps])
        stg_guard()
        for bb in range(8):
            sh = stgh[bb % 2]
            gather(sh[:], sh, nsa_cmp_rows, s, 4 + bb)
            for r2 in range(2):
                P.op("dve" if r2 == 0 else "pool", lambda e, r2=r2, sh=sh: e.tensor_copy(
                    out=sbfc[:, :, :, r2, :], in_=sh[:].rearrange("p (q r t d) -> p r t q d", q=4, r=2, t=4)[:, r2, :, :, :]),
                    reads=[sh], writes=[KAs], full=False)
            for half in range(2):
                ty = "kv"[half]
                for g in range(2):
                    for rp in range(4):
                        j = g * 4 + rp
                        P.op("pe", lambda e, j=j, g=g, rp=rp, half=half: e.transpose(
                            out=ptr[:, j * 128:(j + 1) * 128], in_=sbfc[:, half * 2 + g, rp, :, :].rearrange("p r d -> p (r d)"), identity=ident[:]),
                            reads=[KAs, ident], writes=[ptr], full=False)
                P.op("dve", lambda e: e.tensor_copy(out=KTs[:], in_=ptr[:]), reads=[ptr], writes=[KTs])
                for g in range(2):
                    tg = half * 2 + g
                    pcm = nxt(pz, "pz")
                    for part in range(4):
                        for rp in range(4):
                            P.op("pe", lambda e, pcm=pcm, part=part, rp=rp, g=g, ty=ty: e.matmul(
                                pcm[:, part * 128:(part + 1) * 128], lhsT=W1s[ty][:, part * 4 + rp, :], rhs=KTs[:, (g * 4 + rp) * 128:(g * 4 + rp + 1) * 128],
                                start=(rp == 0), stop=(rp == 3)), reads=[W1s[ty], KTs], writes=[pcm], full=False)
                    P.op("act", lambda e, pcm=pcm: e.copy(out=Ls[:].rearrange("p a c -> p (a c)"), in_=pcm[:, 0:512]), reads=[pcm], writes=[Ls])
                    Lv = Ls[:].rearrange("p a (q t) -> p a q t", t=2)
                    P.op("dve", lambda e, Lv=Lv: e.tensor_tensor(out=lt64[:, 0, :], in0=Lv[:, 0, :, 0], in1=Lv[:, 1, :, 1], op=ALU.add), reads=[Ls], writes=[lt64], full=False)
                    P.op("dve", lambda e, Lv=Lv: e.tensor_tensor(out=lt64[:, 1, :], in0=Lv[:, 2, :, 0], in1=Lv[:, 3, :, 1], op=ALU.add), reads=[Ls], writes=[lt64], full=False)
                    P.op("dve", lambda e: e.tensor_tensor(out=pre64[:, 1:64], in0=lt64[:, 0, 0:63], in1=lt64[:, 1, 1:64], op=ALU.add), reads=[lt64], writes=[pre64], full=False)
                    P.op("dve", lambda e, tg=tg: e.tensor_tensor(out=pre64[:, 0:1], in0=lps[:, tg:tg + 1], in1=lt64[:, 1, 0:1], op=ALU.add),
                         reads=[lt64, lps], writes=[pre64], full=False)
                    P.op("dve", lambda e, tg=tg: e.tensor_copy(out=lps[:, tg:tg + 1], in_=lt64[:, 0, 63:64]), reads=[lt64], writes=[lps], full=False)
                    P.op("act", lambda e, tg=tg, half=half, bb=bb: e.activation(out=Ssp[:, tg, (bb % 2) * 64:(bb % 2) * 64 + 64], in_=pre64[:], func=AF.Silu,
                                                                               bias=cbias[:, half:half + 1]), reads=[pre64, cbias], writes=[Ssp], full=False)
            if bb % 2 == 1:
                st = bb // 2
                pzb = nxt(pz, "pz")
                for tg in range(4):
                    tyn = "kv"[tg // 2]
                    P.op("pe", lambda e, tg=tg, tyn=tyn, pzb=pzb: e.matmul(pzb[:, tg * 64:(tg + 1) * 64], lhsT=Ssp[:, tg, :], rhs=w2[tyn][:], start=True, stop=True),
                         reads=[Ssp, w2[tyn]], writes=[pzb], full=False)
                src_buf[0] = pzb
                headnorm(pzb[:, 0:128], 2, gains["nsa_kn_cmp_g"], KCAs[:, :, 0:64], [KCAs])
                P.op("pool", lambda e, st=st: e.tensor_copy(out=KCAs[:, :, 64:68], in_=alkcs[:, st, :].unsqueeze(1).to_broadcast([128, 2, 4])),
                     reads=[alkcs], writes=[KCAs], full=False)
                P.op("act", lambda e, st=st, pzb=pzb: e.copy(out=VCAs[:, st, :, 0:64], in_=pzb[:, 128:256].rearrange("p (h d) -> p h d", h=2)),
                     reads=[pzb], writes=[VCAs], full=False)
                for g in range(2):
                    P.op("pe", lambda e, g=g: e.transpose(out=ptr[0:80, g * 128:(g + 1) * 128], in_=KCAs[:, g, :], identity=ident[:]),
                         reads=[KCAs, ident], writes=[ptr], full=False)
                P.op("dve", lambda e, st=st: e.tensor_copy(out=kcTs[0:80, :, st * 128:(st + 1) * 128], in_=ptr[0:80, 0:256].rearrange("p (g c) -> p g c", g=2)),
                     reads=[ptr], writes=[kcTs], full=False)
        ps = nxt(pss, "pss")
        for g in range(2):
            for st in range(4):
                blk = g * 4 + st
                P.op("pe", lambda e, g=g, st=st, blk=blk: e.matmul(ps[:, blk * 32:(blk + 1) * 32], lhsT=kcTs[0:80, g, st * 128:(st + 1) * 128], rhs=nq_rhs(s)(g, 80),
                                                                 start=True, stop=(st != 0)), reads=[kcTs, nqT], writes=[ps], full=False)
                if st == 0:
                    P.op("pe", lambda e, blk=blk: e.matmul(ps[:, blk * 32:(blk + 1) * 32], lhsT=ident[:], rhs=s0m[:], start=False, stop=True, skip_group_check=True),
                         reads=[ident, s0m], writes=[ps], full=False)
        P.op("act", lambda e: e.activation(out=PTcs[:], in_=ps[:, 0:256], func=AF.Exp), reads=[ps], writes=[PTcs])
        for g in range(2):
            for st in range(4):
                blk = g * 4 + st
                P.op("pe", lambda e, g=g, st=st, blk=blk: e.matmul(po[g][0:32, 0:65], lhsT=PTcs[:, blk * 32:(blk + 1) * 32], rhs=VCAs[:, st, g, :],
                                                                 start=(st == 0), stop=(st == 3), skip_group_check=True), reads=[PTcs, VCAs], writes=[po[g]], full=False)
        for g in range(2):
            for st in range(4):
                blk = g * 4 + st
                P.op("pe", lambda e, g=g, st=st, blk=blk: e.matmul(pmisc[0:32, g * 132:g * 132 + 132], lhsT=PTcs[:, blk * 32:(blk + 1) * 32], rhs=bmats[:, st, :],
                                                                 start=(st == 0 and g == 0), stop=(st == 3), skip_group_check=True),
                     reads=[PTcs, bmats], writes=[pmisc], full=False)
        s_den()
        for g in range(2):
            P.op("dve", lambda e, g=g: e.tensor_scalar(out=impn[:, g, :], in0=pmisc[0:32, g * 132:(g + 1) * 132], scalar1=rd1[:, g:g + 1], scalar2=None, op0=ALU.mult),
                 reads=[pmisc, rd1], writes=[impn], full=False)
        P.op("dve", lambda e: e.tensor_tensor(out=wk1[:], in0=rd1[:], in1=gh[:, :, 0], op=ALU.mult), reads=[rd1, gh], writes=[wk1])
        for g in range(2):
            P.op("dve", lambda e, g=g: e.tensor_scalar(out=oaccs[:, g, :], in0=po[g][0:32, 0:64], scalar1=wk1[:, g:g + 1], scalar2=None, op0=ALU.mult),
                 reads=[po[g], wk1], writes=[oaccs], full=False)
        for g in range(2):
            P.op("pe", lambda e, g=g: e.matmul(pmisc[0:32, 300:432], lhsT=gsum[:], rhs=impn[:, g, :], start=True, stop=True), reads=[gsum, impn], writes=[pmisc], full=False)
            P.op("dve", lambda e: e.tensor_tensor(out=scs[:], in0=pmisc[0:32, 300:432], in1=fz2s[:], op=ALU.add), reads=[pmisc, fz2s], writes=[scs])
            P.op("dve", lambda e: e.max(out=mx8s[:], in_=scs[:]), reads=[scs], writes=[mx8s])
            P.op("dve", lambda e: e.match_replace(out=sws[:], in_to_replace=mx8s[:], in_values=scs[:], imm_value=-2.0), reads=[scs, mx8s], writes=[sws])
            P.op("dve", lambda e: e.max(out=mx8s[:], in_=sws[:]), reads=[sws], writes=[mx8s])
            P.op("dve", lambda e: e.tensor_reduce(out=thrs[:, 0:1], in_=mx8s[:], axis=AX.X, op=ALU.min), reads=[mx8s], writes=[thrs])
            P.op("dve", lambda e: e.tensor_scalar(out=nsels[:], in0=scs[:, 0:128], scalar1=thrs[:, 0:1], scalar2=NEG, op0=ALU.is_lt, op1=ALU.mult),
                 reads=[scs, thrs], writes=[nsels])
            P.op("pe", lambda e: e.transpose(out=ptr[:, 0:32], in_=nsels[:], identity=ident[0:32, 0:32]), reads=[nsels, ident], writes=[ptr], full=False)
            P.op("dve", lambda e, g=g: e.tensor_copy(out=nselTs[:, g, :], in_=ptr[:, 0:32]), reads=[ptr], writes=[nselTs], full=False)
        ktile_ones()
        acc = [False, False]
        items = []
        for bb in range(8):
            def pre(bb=bb):
                sh = stgh[bb % 2]
                sh4 = sh[:].rearrange("p (r t d) -> p r t d", r=8, t=4)
                gather(sh[:], sh, nsa_slc_rows, s, 4 + bb)
                P.op("dve", lambda e: e.tensor_copy(out=KAs[:, 0:8, :, 0:64], in_=sh4[:, :, 0:2, :]), reads=[sh], writes=[KAs], full=False)
                P.op("pool", lambda e: e.tensor_copy(out=VAs[:, 0:8, :, 0:64], in_=sh4[:, :, 2:4, :]), reads=[sh], writes=[VAs], full=False)
                P.op("pool", lambda e: e.tensor_copy(out=KAs[:, 0:8, :, 64:68], in_=alks_n[:, bb, :, :].unsqueeze(2).to_broadcast([128, 8, 2, 4])),
                     reads=[alks_n], writes=[KAs], full=False)
            items.append(("pre", pre))
            mfn = lambda j, g, bb=bb: [(esels[:, bb, :], nselTs[:, g, :], [esels, nselTs])]
            for rnd in range(2):
                pf = lambda rnd=rnd: [(KAs[:, r, g, :], KAs, VAs[:, r, g, :], VAs, g) for r in range(4 * rnd, 4 * rnd + 4) for g in range(2)]
                items.append(("round", (pf, 80, nq_rhs(s), nqT, mfn)))
        run_rounds(items, acc)
        new_tile("s", skT, SVA, nq_rhs(s), nqT, s, acc)
        s_den()
        P.op("dve", lambda e: e.tensor_tensor(out=wk1[:], in0=rd1[:], in1=gh[:, :, 1], op=ALU.mult), reads=[rd1, gh], writes=[wk1])
        for g in range(2):
            P.op("dve", lambda e, g=g: e.tensor_scalar(out=otmps[:, g, :], in0=po[g][0:32, 0:64], scalar1=wk1[:, g:g + 1], scalar2=None, op0=ALU.mult),
                 reads=[po[g], wk1], writes=[otmps], full=False)
        P.op("dve", lambda e: e.tensor_tensor(out=oaccs[:], in0=oaccs[:], in1=otmps[:], op=ALU.add), reads=[oaccs, otmps], writes=[oaccs])
        stg_guard()
        P.dma("sp", lambda e: e.dma_start(out=stg[:, 0:1024].rearrange("p (t c) -> p t c", t=4), in_=I["win_state"][s].rearrange("(t p) c -> p t c", p=128)),
              stg, writes=[stg])
        stw = stg[:, 0:1024].rearrange("p (t a d) -> p t a d", t=4, a=4)
        P.op("dve", lambda e: e.tensor_copy(out=KAs[:, 0:4, :, 0:64], in_=stw[:, :, 0:2, :]), reads=[stg], writes=[KAs], full=False)
        P.op("pool", lambda e: e.tensor_copy(out=VAs[:, 0:4, :, 0:64], in_=stw[:, :, 2:4, :]), reads=[stg], writes=[VAs], full=False)
        P.op("pool", lambda e: e.tensor_copy(out=KAs[:, 0:4, :, 64:68], in_=alkw[:].unsqueeze(2).to_broadcast([128, 4, 2, 4])), reads=[alkw], writes=[KAs], full=False)
        acc = [False, False]
        pairs = [(KAs[:, wt, g, :], KAs, VAs[:, wt, g, :], VAs, g) for wt in range(4) for g in range(2)]
        wfn = lambda j, g: ([(ident[:], wm0[:], [ident, wm0])] if j < 2 else [])
        s_round(pairs, 80, nq_rhs(s), nqT, wfn, acc)
        new_tile("w", wkT, WVA, nq_rhs(s), nqT, s, acc)
        s_den()
        P.op("dve", lambda e: e.tensor_tensor(out=wk1[:], in0=rd1[:], in1=gh[:, :, 2], op=ALU.mult), reads=[rd1, gh], writes=[wk1])
        for g in range(2):
            P.op("dve", lambda e, g=g: e.tensor_scalar(out=otmps[:, g, :], in0=po[g][0:32, 0:64], scalar1=wk1[:, g:g + 1], scalar2=None, op0=ALU.mult),
                 reads=[po[g], wk1], writes=[otmps], full=False)
        P.op("dve", lambda e: e.tensor_tensor(out=osb[:, 2:4, :], in0=oaccs[:], in1=otmps[:], op=ALU.add), reads=[oaccs, otmps], writes=[osb], full=False)
        for br in range(2):
            pp = nxt(pz, "pz")
            for g in range(2):
                for h in range(4):
                    cc = (g * 4 + h) * 64
                    P.op("pe", lambda e, pp=pp, br=br, g=g, h=h, cc=cc: e.matmul(pp[:, cc:cc + 64], lhsT=zsel[:, h, 128 - 8 * s:256 - 8 * s], rhs=osb[:, br * 2 + g, :],
                                                                              start=True, stop=True), reads=[zsel, osb], writes=[pp], full=False)
            P.op("dve", lambda e, pp=pp, br=br: e.tensor_tensor(out=OTs[:, br * 512:(br + 1) * 512], in0=pp[:, 0:512], in1=OTs[:, br * 512:(br + 1) * 512], op=ALU.add),
                 reads=[pp, OTs], writes=[OTs], full=False)

    NSAMP = int(os.environ.get("KNS", "16"))
    for s in range(NSAMP):
        sample(s)
    P.op("dve", lambda e: e.tensor_copy(out=ot[:], in_=OTs[:]), reads=[OTs], writes=[ot])
    P.dma("sp", lambda e: e.dma_start(out=oscr[16], in_=ot[:]), ot, reads=[ot], writes=[oscr_b[16]])
    P.op("pool", lambda e: e.memset(ssq[:], 0.0), reads=[], writes=sbufs + [wbf, fqT, nqT, uT, ssq] + ar_bufs)

    load_weight(lambda k: wbf[:, k, 0:2048], lambda k: win_v[:, k, 2080:4128], 8, 2048, [], wbf)
    brf_v = I["w_br_fox"].rearrange("(k p) c -> p k c", p=128)
    brn_v = I["w_br_nsa"].rearrange("(k p) c -> p k c", p=128)
    wout_v = I["w_out"].rearrange("(k p) c -> p k c", p=128)
    for dst in kv_all:
        pass
    P.op("pool", lambda e: e.memset(ssq[:], 0.0), reads=[], writes=kv_all + [wbf2, ssq])
    load_weight(lambda k: wbf2_v[:, k, :], lambda k: brf_v[:, k, :], 4, 1024, [], wbf2)
    load_weight(lambda k: wbf2_v[:, 4 + k, :], lambda k: brn_v[:, k, :], 4, 1024, [], wbf2)
    load_weight(lambda k: wbf2_v[:, 8 + k, :], lambda k: wout_v[:, k, :], 8, 1024, [], wbf2)
    phaseB_bufs = list(ar_bufs)
    ar_off[0] = 0
    gsig = carve("gsig", [128, 2048], F32)
    oT = carve("oT", [128, 8, 128], BF16)
    mix = carve("mix", [128, 1024], BF16)
    mtmp = carve("mtmp", [128, 512], F32)
    mtmp2 = carve("mtmp2", [128, 512], F32)
    gtrow = carve("gtrow", [128, 1024], F32)
    rl = carve("rl", [128, 512], F32)
    h2g = carve("h2g", [128, 4, 1024], BF16)
    P.op("pool", lambda e: e.memset(ssq[:], 0.0), reads=[], writes=ar_bufs + [ssq])
    ybuf = [Buf("ybuf%d" % i, None) for i in range(NT)]

    def y_ap(i):
        return O["yp"][i * 128:(i + 1) * 128, :] if i < 16 else O["ys"]

    def gt_rows(i, which):
        for ch in range(2):
            ps = nxt(pss, "pss")
            P.op("pe", lambda e, ps=ps, ch=ch: e.matmul(ps[:, 0:512], lhsT=selT[:, 0 if i < 16 else 1, :],
                                                        rhs=modrow[:, which * 1024 + ch * 512: which * 1024 + (ch + 1) * 512], start=True, stop=True),
                 reads=[selT, modrow], writes=[ps], full=False)
            P.op("act", lambda e, ps=ps, ch=ch: e.copy(out=gtrow[:, ch * 512:(ch + 1) * 512], in_=ps[:, 0:512]), reads=[ps], writes=[gtrow], full=False)

    for i in range(NT if STAGE >= 3 else 0):
        x_t = xt[i % 2]
        xsrc = I["xp"][i * 128:(i + 1) * 128, :] if i < 16 else I["xs"]
        P.dma("sp", lambda e, x_t=x_t, xsrc=xsrc: e.dma_start(out=x_t[:], in_=xsrc), x_t, writes=[x_t])
        vw = norm_hT(x_t, G1, 0, i)
        P.op("dve", lambda e, vw=vw, i=i: e.tensor_tensor(out=vw(hT[:]), in0=vw(htmp[:]), in1=seq_bc(modT[:], 0, i), op=ALU.add),
             reads=[htmp, modT], writes=[hT])
        for c in range(4):
            pzb = nxt(pz, "pz")
            mm8(pzb[:, 0:512], pzb, hT, lambda k: hT[:, k, :], wbf, lambda k, c=c: wbf[:, k, c * 512:(c + 1) * 512])
            P.op("act", lambda e, pzb=pzb, c=c: e.activation(out=gsig[:, c * 512:(c + 1) * 512], in_=pzb[:, 0:512], func=AF.Sigmoid),
                 reads=[pzb], writes=[gsig], full=False)
        P.dma("sp", lambda e, i=i: e.dma_start(out=ot[:], in_=oscr[i]), ot, reads=[oscr_b[i]], writes=[ot])
        for k in range(8):
            P.op("pe", lambda e, k=k: e.transpose(out=ptr[:, k * 128:(k + 1) * 128], in_=ot[:, k * 128:(k + 1) * 128], identity=ident[:]),
                 reads=[ot, ident], writes=[ptr], full=False)
        P.op("act", lambda e: e.copy(out=oT[:].rearrange("p k c -> p (k c)"), in_=ptr[:]), reads=[ptr], writes=[oT])
        for ch in range(2):
            pf = nxt(pz, "pz")
            mm8(pf[:, 0:512], pf, oT, lambda k: oT[:, k, :], wbf2, lambda k, ch=ch: wbf2_v[:, k, ch * 512:(ch + 1) * 512], nk=4)
            P.op("dve", lambda e, pf=pf, ch=ch: e.tensor_tensor(out=mtmp[:], in0=pf[:, 0:512], in1=gsig[:, ch * 512:(ch + 1) * 512], op=ALU.mult),
                 reads=[pf, gsig], writes=[mtmp])
            pn = nxt(pz, "pz")
            mm8(pn[:, 0:512], pn, oT, lambda k: oT[:, 4 + k, :], wbf2, lambda k, ch=ch: wbf2_v[:, 4 + k, ch * 512:(ch + 1) * 512], nk=4)
            P.op("dve", lambda e, pn=pn, ch=ch: e.tensor_tensor(out=mtmp2[:], in0=pn[:, 0:512], in1=gsig[:, 1024 + ch * 512:1024 + (ch + 1) * 512], op=ALU.mult),
                 reads=[pn, gsig], writes=[mtmp2])
            P.op("dve", lambda e, ch=ch: e.tensor_tensor(out=mix[:, ch * 512:(ch + 1) * 512], in0=mtmp[:], in1=mtmp2[:], op=ALU.add),
                 reads=[mtmp, mtmp2], writes=[mix], full=False)
        for k in range(8):
            P.op("pe", lambda e, k=k: e.transpose(out=ptr[:, k * 128:(k + 1) * 128], in_=mix[:, k * 128:(k + 1) * 128], identity=ident[:]),
                 reads=[mix, ident], writes=[ptr], full=False)
        P.op("act", lambda e: e.copy(out=oT[:].rearrange("p k c -> p (k c)"), in_=ptr[:]), reads=[ptr], writes=[oT])
        gt_rows(i, 0)
        for ch in range(2):
            pw = nxt(pz, "pz")
            mm8(pw[:, 0:512], pw, oT, lambda k: oT[:, k, :], wbf2, lambda k, ch=ch: wbf2_v[:, 8 + k, ch * 512:(ch + 1) * 512])
            P.op("dve", lambda e, pw=pw, ch=ch: e.tensor_tensor(out=mtmp[:], in0=pw[:, 0:512], in1=gtrow[:, ch * 512:(ch + 1) * 512], op=ALU.mult),
                 reads=[pw, gtrow], writes=[mtmp])
            P.op("dve", lambda e, ch=ch, x_t=x_t: e.tensor_tensor(out=x_t[:, ch * 512:(ch + 1) * 512], in0=mtmp[:], in1=x_t[:, ch * 512:(ch + 1) * 512], op=ALU.add),
                 reads=[mtmp, x_t], writes=[x_t], full=False)
        P.dma("sp", lambda e, x_t=x_t, i=i: e.dma_start(out=y_ap(i), in_=x_t[:]), x_t, reads=[x_t], writes=[ybuf[i]])
        vw = norm_hT(x_t, G2, 24, i)
        P.op("dve", lambda e, vw=vw, i=i: e.tensor_tensor(out=vw(h2t[:]), in0=vw(htmp[:]),
                                                          in1=seq_bc(modT[:], 24, i), op=ALU.add), reads=[htmp, modT], writes=[h2t])
        P.dma("sp", lambda e, i=i: e.dma_start(out=hscr[i], in_=h2t[:].rearrange("p k c -> p (k c)")), h2t, reads=[h2t], writes=[hscr_b[i]])

    wup_v = I["w_up"].rearrange("(k p) c -> p k c", p=128)
    wdn_v = I["w_down"].rearrange("(k p) c -> p k c", p=128)
    h2g4 = h2g[:].rearrange("p n (k c) -> p n k c", k=8)
    P.op("pool", lambda e: e.memset(ssq[:], 0.0), reads=[], writes=[fqT, nqT, uT, ssq])
    groups = [(0, 4), (4, 4), (8, 4), (12, 4), (16, 1)]
    for half in range(2 if STAGE >= 4 else 0):
        load_weight(lambda k: wbf[:, k, 0:2048], lambda k: wup_v[:, k, half * 2048:(half + 1) * 2048], 8, 2048, [], wbf)
        load_weight(lambda k: wbf2_v[:, k, :], lambda k: wdn_v[:, half * 16 + k, :], 16, 1024, [], wbf2)
        for (t0, nt) in groups:
            w = nt * 128
            P.dma("sp", lambda e, t0=t0, nt=nt: e.dma_start(out=h2g[:, 0:nt, :], in_=hscr[t0:t0 + nt].rearrange("n p c -> p n c")), h2g,
                  reads=[hscr_b[t] for t in range(t0, t0 + nt)], writes=[h2g])
            for fc in range(16):
                pu = nxt(pz, "pz")
                mm8(pu[:, 0:w], pu, wbf, lambda k, fc=fc: wbf[:, k, fc * 128:(fc + 1) * 128], h2g,
                    lambda k, nt=nt: h2g4[:, 0:nt, k, :])
                P.op("act", lambda e, pu=pu, w=w: e.activation(out=rl[:, 0:w], in_=pu[:, 0:w], func=AF.Relu), reads=[pu], writes=[rl])
                P.op("dve", lambda e, fc=fc, w=w: e.tensor_tensor(out=uT_v[:, fc, 0:w], in0=rl[:, 0:w], in1=rl[:, 0:w], op=ALU.mult),
                     reads=[rl], writes=[uT], full=False)
            for tl in range(nt):
                i = t0 + tl
                x_t = xt[i % 2]
                P.dma("sp", lambda e, x_t=x_t, i=i: e.dma_start(out=x_t[:], in_=y_ap(i)), x_t, reads=[ybuf[i]], writes=[x_t])
                gt_rows(i, 1)
                for ch in range(2):
                    pd = nxt(pz, "pz")
                    mm8(pd[:, 0:512], pd, uT, lambda k, tl=tl: uT_v[:, k, tl * 128:(tl + 1) * 128], wbf2,
                        lambda k, ch=ch: wbf2_v[:, k, ch * 512:(ch + 1) * 512], nk=16)
                    P.op("dve", lambda e, pd=pd, ch=ch: e.tensor_tensor(out=mtmp[:], in0=pd[:, 0:512], in1=gtrow[:, ch * 512:(ch + 1) * 512], op=ALU.mult),
                         reads=[pd, gtrow], writes=[mtmp])
                    P.op("dve", lambda e, ch=ch, x_t=x_t: e.tensor_tensor(out=x_t[:, ch * 512:(ch + 1) * 512], in0=mtmp[:], in1=x_t[:, ch * 512:(ch + 1) * 512], op=ALU.add),
                         reads=[mtmp, x_t], writes=[x_t], full=False)
                P.dma("sp", lambda e, x_t=x_t, i=i: e.dma_start(out=y_ap(i), in_=x_t[:]), x_t, reads=[x_t], writes=[ybuf[i]], final=(half == 1))

    dummy = Buf("dummy_ws", None)
    P.dma("sp", lambda e: e.dma_start(out=O["win_s"][:, 0:504, :], in_=I["win_state"][:, 8:512, :]), dummy, final=True)
    print("sbuf bytes remaining:", nc.sbuf_bytes_remaining, " ops:", {k: len(v) for k, v in P.ops.items()})
    P.emit()
    return nc


def _consts():
    bf = ml_dtypes.bfloat16
    ident = np.eye(128, dtype=np.float32).astype(bf)
    idx = np.arange(128)
    tri_p = (idx[:, None] <= idx[None, :]).astype(np.float32)
    same = (idx[:, None] // 8) == (idx[None, :] // 8)
    tri_s = (same & (idx[:, None] <= idx[None, :])).astype(np.float32)
    ones = np.ones((128, 128), np.float32)
    slopes = 2.0 ** (-np.arange(1, 9, dtype=np.float64))
    alq = np.zeros((128, NT, 8, 4), np.float32)
    alk = np.zeros((128, NT, 4), np.float32)
    for t in range(NT):
        pos = (t * 128 + idx) if t < 16 else (8192 + idx % 8)
        hi = (pos // 64) * 64
        lo = pos % 64
        for h in range(8):
            alq[:, t, h, 0] = slopes[h]
            alq[:, t, h, 1] = slopes[h]
            alq[:, t, h, 2] = -slopes[h] * hi
            alq[:, t, h, 3] = -slopes[h] * lo
        alk[:, t, 0] = hi
        alk[:, t, 1] = lo
        alk[:, t, 2] = 1
        alk[:, t, 3] = 1
    cend = 16 * idx + 15
    alkc = np.stack([(cend // 64) * 64, cend % 64, np.ones(128), np.ones(128)], axis=1).astype(np.float32)
    tq = idx[None, :]
    tk = idx[:, None]
    cm_diag = np.tile(np.where(tk > tq, NEG, 0.0), (1, 4)).astype(np.float32)
    cm_win = np.tile(np.where(tk <= tq, NEG, 0.0), (1, 4)).astype(np.float32)
    tq_all = np.arange(2048)[None, :]
    cmaskc = np.where((idx[:, None] == 0) | (cend[:, None] > tq_all), NEG, 0.0).astype(np.float32)
    esel = np.zeros((32, 16, 128), np.float32)
    for kt in range(16):
        esel[2 * kt, kt, 0:64] = 1
        esel[2 * kt + 1, kt, 64:128] = 1
    bmat = np.zeros((128, 32), np.float32)
    for j in range(32):
        for m, wgt in ((4 * j, 0.5), (4 * j + 1, 1.0), (4 * j + 2, 1.0), (4 * j + 3, 1.0), (4 * j + 4, 0.5)):
            if m < 128:
                bmat[m, j] += wgt
    fz2 = np.zeros((128, 16, 32), np.float32)
    vz = np.zeros((128, 16, 32), np.float32)
    blk = np.arange(32)[None, :]
    for t in range(16):
        cur = ((t * 128 + idx) // 64)[:, None]
        valid = (blk <= cur).astype(np.float32)
        forced = ((blk == 0) | (blk == cur) | (blk == cur - 1)).astype(np.float32)
        vz[:, t, :] = valid
        fz2[:, t, :] = 1.0e4 * forced * valid + (valid - 1.0)
    selT = np.zeros((17, 2, 128), np.float32)
    selT[0, 0, :] = 1
    for p in range(128):
        selT[1 + p // 8, 1, p] = 1
    p = idx
    bmask = np.full((128, 16, 4, 8), NEG, np.float32)
    for s_ in range(16):
        for t_ in range(8):
            for tq_ in range(t_, 8):
                bmask[s_ * 8 + t_, s_, :, tq_] = 0.0
    zall = np.zeros((128, 16, 4, 8), np.float32)
    for s_ in range(16):
        for tq_ in range(8):
            zall[s_ * 8 + tq_, s_, :, tq_] = 1.0
    zsel = np.zeros((32, 4, 256), np.float32)
    for h_ in range(4):
        for t_ in range(8):
            zsel[h_ * 8 + t_, h_, 128 + t_] = 1.0
    bmats = np.zeros((128, 4, 132), np.float32)
    for j in range(129):
        for m, wgt in ((4 * j, 0.5), (4 * j + 1, 1.0), (4 * j + 2, 1.0), (4 * j + 3, 1.0), (4 * j + 4, 0.5)):
            if m < 512:
                bmats[m % 128, m // 128, j] += wgt
    esels = np.zeros((128, 8, 128), np.float32)
    for bb in range(8):
        esels[16 * bb + p // 8, bb, p] = 1.0
    def hl(pos):
        return np.stack([(pos // 64) * 64, pos % 64, np.ones_like(pos), np.ones_like(pos)], axis=-1).astype(np.float32)
    alks_n = hl(1024 * np.arange(8)[None, :, None] + 8 * p[:, None, None] + np.arange(8)[None, None, :])
    alkw = hl(7680 + 128 * np.arange(4)[None, :] + p[:, None])
    alkcs = hl(16 * (128 * np.arange(4)[None, :] + p[:, None]) + 15)
    ustr = (p[:, None] > p[None, :]).astype(np.float32)
    s0m = np.zeros((128, 32), np.float32); s0m[0, :] = NEG
    wm0 = np.where(p[:, None] <= (np.arange(32) % 8)[None, :], NEG, 0.0).astype(np.float32)
    gsum = ((np.arange(32)[:, None] % 8) == (np.arange(32)[None, :] % 8)).astype(np.float32)
    fz2s = np.zeros((32, 132), np.float32)
    fz2s[:, [0, 127, 128]] = 1.0e4
    fz2s[:, 129:] = -1.0
    ohc = np.zeros((32, 4, 3), np.float32)
    for r_ in range(32):
        ohc[r_, r_ // 8, :] = 1.0
    mulc = np.zeros((128, 12), np.float32); addc = np.zeros((128, 12), np.float32)
    mulc[:, 0:4] = 8.0; addc[:, 0:4] = (p % 8)[:, None]
    mulc[:, 4:12] = 16.0; addc[:, 4:12] = (p % 16)[:, None]
    samp = dict(bmask=bmask.reshape(128, 512).astype(bf), zall=zall.reshape(128, 512), zsel=zsel.reshape(32, 1024).astype(bf),
                bmats=bmats.reshape(128, 528).astype(bf), esels=esels.reshape(128, 1024).astype(bf), alks_n=alks_n.reshape(128, 256).astype(bf),
                alkw=alkw.reshape(128, 16).astype(bf), alkcs=alkcs.reshape(128, 16).astype(bf), ustr=ustr, s0m=s0m.astype(bf), wm0=wm0.astype(bf),
                gsum=gsum, fz2s=fz2s, ohc=ohc.reshape(32, 12), mulc=mulc, addc=addc)
    return dict(samp, ident=ident, tri_p=tri_p, tri_s=tri_s, ones=ones,
                alq=alq.reshape(128, NT * 32).astype(bf), alk=alk.reshape(128, NT * 4).astype(bf), alkc=alkc.astype(bf),
                cm_diag=cm_diag.astype(bf), cm_win=cm_win.astype(bf), cmaskc=cmaskc.astype(bf),
                esel=esel.reshape(32, 16 * 128).astype(bf), bmat=bmat.astype(bf),
                fz2=fz2.reshape(128, 512), vz=vz.reshape(128, 512), selT=selT.reshape(17, 256))


_NC = [None]


def kernel(**inp):
    f32 = lambda a: np.ascontiguousarray(np.asarray(a), dtype=np.float32)
    if _NC[0] is None:
        _NC[0] = build_program()
    nc = _NC[0]
    shared = dict(_consts())
    shared["w_ada"] = f32(inp["w_ada"][0])
    shared["b_adaT"] = f32(np.asarray(inp["b_ada"][0]).reshape(48, 128).T)
    shared["b_ada"] = f32(np.asarray(inp["b_ada"][0]).reshape(1, 6144))
    shared["w_in"] = f32(inp["w_in"][0])
    shared["g1T"] = f32(np.asarray(inp["norm1_g"][0]).reshape(8, 128).T)
    shared["g2T"] = f32(np.asarray(inp["norm2_g"][0]).reshape(8, 128).T)
    shared["b_fox_f"] = f32(inp["b_fox_f"])
    for g in ("fox_qn_g", "fox_kn_g", "nsa_qn_g", "nsa_kn_cmp_g", "nsa_kn_slc_g", "nsa_kn_win_g"):
        shared[g] = f32(inp[g])
    for ty in ("k", "v"):
        w1 = np.asarray(inp["cmp_w1_" + ty][0]).reshape(32, 64, 128)
        w1 = w1.transpose(1, 0, 2).reshape(64, 32 * 128)
        shared["w1" + ty] = f32(np.concatenate([w1, w1], axis=0))
        shared["w2" + ty] = f32(inp["cmp_w2_" + ty][0])
        pos = np.asarray(inp["cmp_pos_" + ty][0]).T
        shared["pos" + ty + "T"] = f32(np.concatenate([pos, pos], axis=0))
    shared["w_br_fox"] = f32(inp["w_br_fox"][0])
    shared["w_br_nsa"] = f32(inp["w_br_nsa"][0])
    shared["w_out"] = f32(inp["w_out"][0])
    shared["w_up"] = f32(inp["w_up"][0])
    shared["w_down"] = f32(inp["w_down"][0])
    shared["cache_fox_kv"] = f32(inp["cache_fox_kv"]).reshape(NPHYS * 8, 4096)
    nsa4 = np.asarray(inp["cache_nsa_kv"]).reshape(NPHYS * 128, 4, 128)
    shared["cache_nsa_cmp"] = f32(nsa4[:, 0:2, :]).reshape(NPHYS * 16, 2048)
    shared["cache_nsa_slc"] = f32(nsa4[:, 2:4, :]).reshape(NPHYS * 16, 2048)
    shared["cache_fox_logf"] = f32(inp["cache_fox_logf"]).reshape(NPHYS * 8, 128)
    pt = np.asarray(inp["page_table"]).astype(np.int32)
    xp = np.asarray(inp["x_prompt"])
    xs = np.asarray(inp["x_sample"])
    cp = np.asarray(inp["c_prompt"])
    cs = np.asarray(inp["c_sample"])
    in_maps = []
    for c in range(8):
        m = dict(shared)
        m["xp"] = f32(xp[c])
        m["xs"] = f32(xs[16 * c:16 * c + 16].reshape(128, 1024))
        call = np.concatenate([cp[c:c + 1], cs[16 * c:16 * c + 16]], axis=0)
        m["cT"] = f32(call.reshape(17, 8, 128).transpose(2, 1, 0))
        m["win_state"] = f32(np.asarray(inp["state_win_kv"][0, 16 * c:16 * c + 16]).reshape(16, 512, 256))
        ptc = pt[16 * c:16 * c + 16]
        pq = np.arange(128)
        ptq = np.zeros((128, 16, 12), np.int32)
        for bq in range(4):
            ptq[:, :, bq] = ptc[:, 16 * bq + pq // 8].T
        for bq in range(8):
            ptq[:, :, 4 + bq] = ptc[:, 8 * bq + pq // 16].T
        m["ptq"] = np.ascontiguousarray(ptq.reshape(128, 192))
        in_maps.append(m)
    res = run_bass_kernel_spmd(nc, in_maps, core_ids=list(range(8)))
    R = res.results
    cat = lambda n: np.stack([np.asarray(R[c][n]) for c in range(8)], axis=0)
    yp = cat("yp").reshape(8, 2048, 1024)
    ys = cat("ys").reshape(128, 8, 1024)
    fox_p = cat("fox_p").reshape(1, 8, 2048, 2, 2, 64)
    lf_p = cat("lf_p").reshape(1, 8, 2048, 8)
    nsa_p = cat("nsa_p").reshape(1, 8, 2048, 4, 2, 64)
    win_p = cat("win_p").reshape(1, 8, 512, 2, 2, 64)
    fox_s = cat("fox_s").reshape(1, 128, 8, 2, 2, 64)
    lf_s = cat("lf_s").reshape(1, 128, 8, 8)
    nsa_s = cat("nsa_s").reshape(1, 128, 8, 4, 2, 64)
    win_s = cat("win_s").reshape(1, 128, 512, 2, 2, 64)
    return (yp, ys, fox_p, lf_p, nsa_p, win_p, fox_s, lf_s, nsa_s, win_s)
```

```python
import numpy as np
import ml_dtypes
import concourse.bass as bass
import concourse.mybir as mybir
from concourse.bass_utils import run_bass_kernel_spmd

F32 = mybir.dt.float32
BF16 = mybir.dt.bfloat16
I32 = mybir.dt.int32
AF = mybir.ActivationFunctionType
ALU = mybir.AluOpType
AX = mybir.AxisListType

NT = 17
EPS = 1e-6
NEG = -30000.0


class Buf:
    def __init__(self, name, t):
        self.name = name
        self.t = t
        self.writers = {}
        self.readers = {}
        self.dsem = None
        self.dcount = 0
        self.excl = False

    def __getitem__(self, k):
        return self.t[k]


class Op:
    __slots__ = ("eng", "fn", "deps", "signal", "idx", "count", "dtok")

    def __init__(self, eng, fn):
        self.eng = eng
        self.fn = fn
        self.deps = {}
        self.signal = False
        self.count = None
        self.dtok = None


ENGS = ("pe", "act", "dve", "pool", "sp")


class Prog:
    def __init__(self, nc):
        self.nc = nc
        self.ops = {e: [] for e in ENGS}
        self.sems = {e: nc.alloc_semaphore("s_" + e) for e in ENGS}
        self.known = {e: {} for e in ENGS}
        self.final = {}
        self.rr = 0

    def sbuf(self, name, shape, dt):
        return Buf(name, self.nc.alloc_sbuf_tensor("sb_" + name, list(shape), dt))

    def psum(self, name, shape, dt=F32):
        b = Buf(name, self.nc.alloc_psum_tensor("ps_" + name, list(shape), dt))
        b.excl = True
        return b

    def _collect(self, eng, reads, writes):
        deps = {}

        def add(d):
            for k, v in d.items():
                if deps.get(k, -1) < v:
                    deps[k] = v

        for b in reads:
            add(b.writers)
        for b in writes:
            add(b.writers)
            add(b.readers)
        out = {}
        kn = self.known[eng]
        for k, v in deps.items():
            if k == eng and eng == "pe":
                continue
            if kn.get(k, -1) >= v:
                continue
            kn[k] = v
            out[k] = v
            if isinstance(k, str):
                self.ops[k][v].signal = True
        return out

    def op(self, eng, fn, reads=(), writes=(), full=True):
        o = Op(eng, fn)
        xr = [b for b in reads if b.excl]
        o.deps = self._collect(eng, reads, list(writes) + xr)
        o.idx = len(self.ops[eng])
        self.ops[eng].append(o)
        for b in reads:
            if b.readers.get(eng, -1) < o.idx:
                b.readers[eng] = o.idx
        for b in writes:
            if full:
                b.writers = {eng: o.idx}
                b.readers = {}
            else:
                b.writers[eng] = o.idx
        return o

    def dma(self, q, fn, sb, reads=(), writes=(), final=False):
        o = Op(q, fn)
        o.deps = self._collect(q, reads, writes)
        o.idx = len(self.ops[q])
        self.ops[q].append(o)
        if sb.dsem is None:
            sb.dsem = self.nc.alloc_semaphore("d_" + sb.name)
        sb.dcount += 16
        o.dtok = (sb.dsem, sb.dcount)
        key = sb.dsem
        for b in reads:
            if b.readers.get(key, -1) < sb.dcount:
                b.readers[key] = sb.dcount
        for b in writes:
            b.writers[key] = sb.dcount
        if final:
            self.final[key] = sb.dcount
        return o

    def emit(self):
        nc = self.nc
        for e in ENGS:
            c = 0
            for o in self.ops[e]:
                if o.signal:
                    c += 1
                    o.count = c
        prog = self
        print("signal counts:", {e: sum(1 for o in self.ops[e] if o.signal) for e in ENGS},
              "waits:", {e: sum(len(o.deps) for o in self.ops[e]) for e in ENGS},
              "dma sem max:", max([0] + [o.dtok[1] for e in ENGS for o in self.ops[e] if o.dtok]))

        def run(e, eng):
            for o in prog.ops[e]:
                for k, v in o.deps.items():
                    if isinstance(k, str):
                        eng.wait_ge(prog.sems[k], prog.ops[k][v].count)
                    else:
                        eng.wait_ge(k, v)
                ins = o.fn(eng)
                if o.dtok is not None:
                    ins.then_inc(o.dtok[0], 16)
                elif o.signal:
                    ins.then_inc(prog.sems[e], 1)
            if e == "sp":
                for k, v in prog.final.items():
                    eng.wait_ge(k, v)

        with nc.Block() as block:
            @block.tensor
            def _(eng):
                run("pe", eng)

            @block.scalar
            def _(eng):
                run("act", eng)

            @block.vector
            def _(eng):
                run("dve", eng)

            @block.gpsimd
            def _(eng):
                run("pool", eng)

            @block.sync
            def _(eng):
                run("sp", eng)


import os
NPHYS = int(os.environ.get("KNPHYS", "10240"))
IN_SPECS = [
    ("xp", [2048, 1024], F32), ("xs", [128, 1024], F32), ("cT", [128, 8, 17], F32),
    ("w_ada", [1024, 6144], F32), ("b_adaT", [128, 48], F32), ("b_ada", [1, 6144], F32), ("w_in", [1024, 4128], F32),
    ("g1T", [128, 8], F32), ("g2T", [128, 8], F32), ("b_fox_f", [1, 8], F32),
    ("fox_qn_g", [1, 64], F32), ("fox_kn_g", [1, 64], F32), ("nsa_qn_g", [1, 64], F32),
    ("nsa_kn_cmp_g", [1, 64], F32), ("nsa_kn_slc_g", [1, 64], F32), ("nsa_kn_win_g", [1, 64], F32),
    ("w1k", [128, 32 * 128], F32), ("w1v", [128, 32 * 128], F32), ("w2k", [128, 64], F32), ("w2v", [128, 64], F32),
    ("poskT", [128, 32], F32), ("posvT", [128, 32], F32),
    ("w_br_fox", [512, 1024], F32), ("w_br_nsa", [512, 1024], F32), ("w_out", [1024, 1024], F32),
    ("w_up", [1024, 4096], F32), ("w_down", [4096, 1024], F32),
    ("win_state", [16, 512, 256], F32),
    ("ident", [128, 128], BF16), ("tri_p", [128, 128], F32), ("tri_s", [128, 128], F32),
    ("ones", [128, 128], F32), ("alq", [128, NT * 32], BF16), ("alk", [128, NT * 4], BF16), ("alkc", [128, 4], BF16),
    ("cm_diag", [128, 512], BF16), ("cm_win", [128, 512], BF16), ("cmaskc", [128, 2048], BF16),
    ("esel", [32, 16 * 128], BF16), ("bmat", [128, 32], BF16), ("fz2", [128, 16 * 32], F32), ("vz", [128, 16 * 32], F32),
    ("selT", [17, 256], F32),
    ("cache_fox_kv", [NPHYS * 8, 4096], F32), ("cache_nsa_cmp", [NPHYS * 16, 2048], F32), ("cache_nsa_slc", [NPHYS * 16, 2048], F32), ("cache_fox_logf", [NPHYS * 8, 128], F32),
    ("bmask", [128, 512], BF16), ("zall", [128, 512], F32), ("zsel", [32, 1024], BF16), ("bmats", [128, 528], BF16), ("esels", [128, 1024], BF16),
    ("alks_n", [128, 256], BF16), ("alkw", [128, 16], BF16), ("alkcs", [128, 16], BF16), ("ustr", [128, 128], F32), ("s0m", [128, 32], BF16),
    ("wm0", [128, 32], BF16), ("gsum", [32, 32], F32), ("fz2s", [32, 132], F32), ("ohc", [32, 12], F32), ("ptq", [128, 192], I32),
    ("mulc", [128, 12], F32), ("addc", [128, 12], F32),
]
OUT_SPECS = [
    ("yp", [2048, 1024]), ("ys", [128, 1024]), ("fox_p", [2048, 256]), ("lf_p", [2048, 8]),
    ("nsa_p", [2048, 512]), ("win_p", [512, 256]), ("fox_s", [128, 256]), ("lf_s", [128, 8]),
    ("nsa_s", [128, 512]), ("win_s", [16, 512, 256]),
]


import os
STAGE = int(os.environ.get("KSTAGE", "9"))
KSKIP = os.environ.get("KSKIP", "")
KCMP = int(os.environ.get("KCMP", "9"))


def build_program():
    nc = bass.Bass("TRN2", target_bir_lowering=False)
    I = {n: nc.dram_tensor(n, s, d, kind="ExternalInput").ap() for n, s, d in IN_SPECS}
    O = {n: nc.dram_tensor(n, s, F32, kind="ExternalOutput").ap() for n, s in OUT_SPECS}
    P = Prog(nc)

    def load(name, shape, dt, src, q="sp"):
        b = P.sbuf(name, shape, dt)
        P.dma(q, lambda e: e.dma_start(out=b[:], in_=src), b, writes=[b])
        return b

    ident = load("ident", [128, 128], BF16, I["ident"])
    tri_p = load("tri_p", [128, 128], F32, I["tri_p"])
    tri_s = load("tri_s", [128, 128], F32, I["tri_s"])
    ones = load("ones", [128, 128], F32, I["ones"])
    alq = load("alq", [128, NT, 8, 4], BF16, I["alq"].rearrange("p (t h c) -> p t h c", t=NT, h=8))
    alk = load("alk", [128, NT, 4], BF16, I["alk"].rearrange("p (t c) -> p t c", t=NT))
    alkc = load("alkc", [128, 4], BF16, I["alkc"])
    cm_diag = load("cm_diag", [128, 512], BF16, I["cm_diag"])
    cm_win = load("cm_win", [128, 512], BF16, I["cm_win"])
    cmaskc = load("cmaskc", [128, 2048], BF16, I["cmaskc"])
    esel = load("esel", [32, 16, 128], BF16, I["esel"].rearrange("p (k t) -> p k t", k=16))
    bmat = load("bmat", [128, 32], BF16, I["bmat"])
    fz2 = load("fz2", [128, 16, 32], F32, I["fz2"].rearrange("p (t j) -> p t j", t=16))
    vz = load("vz", [128, 16, 32], F32, I["vz"].rearrange("p (t j) -> p t j", t=16))
    selT = load("selT", [17, 2, 128], F32, I["selT"].rearrange("p (a t) -> p a t", a=2))
    cT = load("cT", [128, 8, 17], F32, I["cT"])
    b_adaT = load("b_adaT", [128, 48], F32, I["b_adaT"])
    g1T = load("g1T", [128, 8], F32, I["g1T"])
    g2T = load("g2T", [128, 8], F32, I["g2T"])
    bff = load("bff", [128, 8], F32, I["b_fox_f"].partition_broadcast(128))
    gains = {}
    for gname in ("fox_qn_g", "fox_kn_g", "nsa_qn_g", "nsa_kn_cmp_g", "nsa_kn_slc_g", "nsa_kn_win_g"):
        gains[gname] = load(gname, [128, 64], F32, I[gname].partition_broadcast(128))
    for gname in ("fox_qn_g", "nsa_qn_g"):
        gb = gains[gname]
        P.op("dve", lambda e, gb=gb: e.tensor_scalar(out=gb[:], in0=gb[:], scalar1=0.125, scalar2=None, op0=ALU.mult),
             reads=[gb], writes=[gb])

    ptr = P.psum("ptr", [128, 1024], BF16)
    pz = [P.psum("pz%d" % j, [128, 512], F32) for j in range(2)]
    pmisc = P.psum("pmisc", [128, 512], F32)
    pss = [P.psum("pss%d" % j, [128, 512], F32) for j in range(2)]
    po = [P.psum("po%d" % j, [128, 512], F32) for j in range(2)]
    cnt = {"pz": 0, "pss": 0, "pt": 0}

    def nxt(lst, key):
        b = lst[cnt[key] % len(lst)]
        cnt[key] += 1
        return b

    wst = [P.sbuf("wst0", [128, 2080], F32)] * 2
    wbf = P.sbuf("wbf", [128, 8, 2080], BF16)
    kvreg_t = nc.alloc_sbuf_tensor("sb_kvreg", [128, 20 * 1024], BF16)
    W17 = NT * 128
    fkT = Buf("fkT", None); skT = Buf("skT", None); wkT = Buf("wkT", None)
    FVA = Buf("FVA", None); SVA = Buf("SVA", None); WVA = Buf("WVA", None)
    wbf2 = Buf("wbf2", None)
    kv_all = [fkT, skT, wkT, FVA, SVA, WVA]
    o_kT = {"f": 0, "s": 2 * W17, "w": 4 * W17}
    o_VA = {"f": 6 * W17, "s": 6 * W17 + NT * 130, "w": 6 * W17 + 2 * NT * 130}
    assert 6 * W17 + 3 * NT * 130 <= 20 * 1024

    def kT_ap(which, g, c0, c1):
        base = o_kT[which] + g * W17
        return kvreg_t[:, base + c0:base + c1]

    def kT_tile2(which, i):
        base = o_kT[which]
        return kvreg_t[:, base:base + 2 * W17].rearrange("p (g c) -> p g c", g=2)[:, :, i * 128:(i + 1) * 128]

    def VA_ap(which, i, g):
        base = o_VA[which] + (i * 2 + g) * 65
        return kvreg_t[:, base:base + 65]

    def VA_tile(which, i):
        base = o_VA[which] + i * 130
        return kvreg_t[:, base:base + 130].rearrange("p (g c) -> p g c", g=2)

    def VA_all(which):
        base = o_VA[which]
        return kvreg_t[:, base:base + NT * 130].rearrange("p (n c) -> p n c", c=65)

    wbf2_v = kvreg_t[:, 0:16 * 1024].rearrange("p (k c) -> p k c", k=16)

    oscr = nc.dram_tensor("oscr", [NT, 128, 1024], BF16).ap()
    hscr = nc.dram_tensor("hscr", [NT, 128, 1024], BF16).ap()
    oscr_b = [Buf("oscr%d" % i, None) for i in range(NT)]
    hscr_b = [Buf("hscr%d" % i, None) for i in range(NT)]
    ot = P.sbuf("ot", [128, 1024], BF16)
    h2t = P.sbuf("h2t", [128, 8, 128], BF16)
    arena_t = nc.alloc_sbuf_tensor("sb_arena", [128, 18 * 1024], BF16)
    ar_off = [0]
    ar_bufs = []

    def carve(name, shape, dt):
        n = 1
        for d_ in shape[1:]:
            n *= d_
        nb = n * (2 if dt == F32 else 1)
        ap = arena_t[:, ar_off[0]:ar_off[0] + nb]
        ar_off[0] += nb
        assert ar_off[0] <= 18 * 1024, (name, ar_off[0])
        if dt == F32:
            ap = ap.bitcast(F32)
        if len(shape) == 3:
            ap = ap.rearrange("p (a b) -> p a b", a=shape[1])
        b = Buf(name, ap)
        ar_bufs.append(b)
        return b

    shreg_t = nc.alloc_sbuf_tensor("sb_shreg", [128, 16 * 512], BF16)
    fqT = Buf("fqT", None); nqT = Buf("nqT", None); uT = Buf("uT", None)
    fqT_v = shreg_t[:, 0:4096].rearrange("p (h c) -> p h c", h=8)
    nqT_v = shreg_t[:, 4096:8192].rearrange("p (h c) -> p h c", h=8)
    uT_v = shreg_t[:, :].rearrange("p (f c) -> p f c", f=16)

    scT = P.sbuf("scT", [128, 8, 17], BF16)
    P.op("act", lambda e: e.activation(out=scT[:], in_=cT[:], func=AF.Silu), reads=[cT], writes=[scT])
    modT = P.sbuf("modT", [128, 48, 17], F32)
    modrow = P.sbuf("modrow", [17, 2048], F32)
    P.dma("sp", lambda e: e.dma_start(out=modrow[:, 0:1024], in_=I["b_ada"][:, 2048:3072].partition_broadcast(17)), modrow, writes=[modrow])
    P.dma("sp", lambda e: e.dma_start(out=modrow[:, 1024:2048], in_=I["b_ada"][:, 5120:6144].partition_broadcast(17)), modrow, writes=[modrow])
    pmod = [pz[0], pz[1]]
    wada_v = I["w_ada"].rearrange("(k p) c -> p k c", p=128)
    for eg in range(24):
        st = wst[eg % 2]
        P.dma("sp", lambda e, st=st, eg=eg: e.dma_start(out=st[:, 0:2048].rearrange("p (k c) -> p k c", k=8),
                                                        in_=wada_v[:, :, eg * 256:(eg + 1) * 256]), st, writes=[st])
        ce = "pool" if eg % 2 == 0 else "dve"
        P.op(ce, lambda e, st=st: e.tensor_copy(out=wbf[:, :, 0:256], in_=st[:, 0:2048].rearrange("p (k c) -> p k c", k=8)),
             reads=[st], writes=[wbf])
        for ec in range(2):
            e_idx = eg * 2 + ec
            pm = pmod[e_idx // 24]
            col = (e_idx % 24) * 17
            for k in range(8):
                P.op("pe", lambda e, pm=pm, col=col, k=k, ec=ec: e.matmul(pm[:, col:col + 17], lhsT=wbf[:, k, ec * 128:(ec + 1) * 128],
                                                                         rhs=scT[:, k, :], start=(k == 0), stop=(k == 7)),
                     reads=[wbf, scT], writes=[pm], full=False)
        mr = None
        if 8 <= eg < 12:
            mr = (eg - 8) * 256
        if 20 <= eg < 24:
            mr = 1024 + (eg - 20) * 256
        if mr is not None:
            for k in range(8):
                P.op("pe", lambda e, k=k: e.matmul(pss[0][0:17, 0:256], lhsT=scT[:, k, :], rhs=wbf[:, k, 0:256], start=(k == 0), stop=(k == 7)),
                     reads=[wbf, scT], writes=[pss[0]], full=False)
            P.op("dve", lambda e, mr=mr: e.tensor_tensor(out=modrow[:, mr:mr + 256], in0=pss[0][0:17, 0:256], in1=modrow[:, mr:mr + 256], op=ALU.add),
                 reads=[pss[0], modrow], writes=[modrow], full=False)
    for hlf in range(2):
        P.op("dve", lambda e, hlf=hlf: e.tensor_tensor(out=modT[:, hlf * 24:(hlf + 1) * 24, :],
                                                       in0=pmod[hlf][:, 0:408].rearrange("p (c j) -> p c j", j=17),
                                                       in1=b_adaT[:, hlf * 24:(hlf + 1) * 24].unsqueeze(2).to_broadcast([128, 24, 17]),
                                                       op=ALU.add),
             reads=[pmod[hlf], b_adaT], writes=[modT], full=False)
    G1 = P.sbuf("G1", [128, 8, 17], F32)
    G2 = P.sbuf("G2", [128, 8, 17], F32)
    P.op("dve", lambda e: e.scalar_tensor_tensor(out=G1[:], in0=modT[:, 8:16, :], scalar=1.0,
                                                 in1=g1T[:].unsqueeze(2).to_broadcast([128, 8, 17]), op0=ALU.add, op1=ALU.mult),
         reads=[modT, g1T], writes=[G1])
    P.op("dve", lambda e: e.scalar_tensor_tensor(out=G2[:], in0=modT[:, 32:40, :], scalar=1.0,
                                                 in1=g2T[:].unsqueeze(2).to_broadcast([128, 8, 17]), op0=ALU.add, op1=ALU.mult),
         reads=[modT, g2T], writes=[G2])

    def seq_bc(src3, lo, i):
        if i < 16:
            return src3[:, lo:lo + 8, 0:1].to_broadcast([128, 8, 128])
        return src3[:, lo:lo + 8, 1:17].unsqueeze(3).to_broadcast([128, 8, 16, 8])

    cast_engs = ["pool", "dve"]

    def load_weight(dst_fn, src_fn, nk, width, reads_bufs, dst_buf):
        for k in range(nk):
            st = wst[k % 2]
            src = src_fn(k)
            dst = dst_fn(k)
            P.dma("sp", lambda e, st=st, src=src: e.dma_start(out=st[:, 0:width], in_=src), st, writes=[st])
            ce = cast_engs[k % 2]
            P.op(ce, lambda e, st=st, dst=dst: e.tensor_copy(out=dst, in_=st[:, 0:width]), reads=[st], writes=[dst_buf], full=False)

    win_v = I["w_in"].rearrange("(k p) c -> p k c", p=128)
    load_weight(lambda k: wbf[:, k, 0:2080], lambda k: win_v[:, k, 0:2080], 8, 2080, [], wbf)

    w1 = {}
    w2 = {}
    posT = {}
    for ty, nm1, nm2, nmp in (("k", "w1k", "w2k", "poskT"), ("v", "w1v", "w2v", "posvT")):
        w1[ty] = carve("w1" + ty, [128, 32, 128], BF16)
        for half in range(2):
            st = wst[half]
            P.dma("sp", lambda e, st=st, nm1=nm1, half=half: e.dma_start(out=st[:, 0:2048], in_=I[nm1][:, half * 2048:(half + 1) * 2048]), st, writes=[st])
            P.op("pool", lambda e, st=st, ty=ty, half=half: e.tensor_copy(out=w1[ty][:, half * 16:(half + 1) * 16, :],
                                                                         in_=st[:, 0:2048].rearrange("p (c h) -> p c h", c=16)),
                 reads=[st], writes=[w1[ty]], full=False)
        w2f = load("w2f" + ty, [128, 64], F32, I[nm2])
        w2[ty] = P.sbuf("w2" + ty, [128, 64], BF16)
        P.op("pool", lambda e, ty=ty, w2f=w2f: e.tensor_copy(out=w2[ty][:], in_=w2f[:]), reads=[w2f], writes=[w2[ty]])
        pf = load("posf" + ty, [128, 32], F32, I[nmp])
        posT[ty] = P.sbuf("posT" + ty, [128, 32], BF16)
        P.op("pool", lambda e, ty=ty, pf=pf: e.tensor_copy(out=posT[ty][:], in_=pf[:]), reads=[pf], writes=[posT[ty]])
    cbias = P.sbuf("cbias", [128, 2], F32)
    for ti, ty in enumerate(("k", "v") if "b" not in KSKIP else ()):
        for lc in range(32):
            P.op("pe", lambda e, ty=ty, lc=lc, ti=ti: e.matmul(pmisc[:, 300 + ti:301 + ti], lhsT=w1[ty][0:64, lc, :], rhs=posT[ty][0:64, lc:lc + 1],
                                                              start=(lc == 0), stop=(lc == 31)),
                 reads=[w1[ty], posT[ty]], writes=[pmisc], full=False)
    if "b" not in KSKIP:
        P.op("act", lambda e: e.copy(out=cbias[:], in_=pmisc[:, 300:302]), reads=[pmisc], writes=[cbias])

    xt = [P.sbuf("xt0", [128, 1024], F32)] * 2
    ssq = P.sbuf("ssq", [128, 1], F32)
    rstd = P.sbuf("rstd", [128, 1], F32)
    xn = P.sbuf("xn", [128, 1024], BF16)
    htmp = P.sbuf("htmp", [128, 8, 128], F32)
    hT = P.sbuf("hT", [128, 8, 128], BF16)
    sqb = P.sbuf("sqb", [128, 512], F32)
    ss8 = P.sbuf("ss8", [128, 8], F32)
    ntmp = P.sbuf("ntmp", [128, 512], F32)
    rows_fox = P.sbuf("rows_fox", [128, 256], F32)
    rows_nsa = P.sbuf("rows_nsa", [128, 512], F32)
    rows_win = P.sbuf("rows_win", [128, 256], F32)
    lf = [P.sbuf("lf%d" % j, [128, 8], F32) for j in range(2)]
    t8 = P.sbuf("t8", [128, 8], F32)
    c8 = P.sbuf("c8", [128, 8], F32)
    r8 = P.sbuf("r8", [128, 8], F32)
    carry = P.sbuf("carry", [128, 8], F32)
    PC = P.sbuf("PC", [128, 8, 3], BF16)
    QA = P.sbuf("QA", [128, 8, 128], BF16)
    NQA = P.sbuf("NQA", [128, 8, 128], BF16)
    KA = P.sbuf("KA", [128, 2, 128], BF16)
    SKA = P.sbuf("SKA", [128, 2, 128], BF16)
    WKA = P.sbuf("WKA", [128, 2, 128], BF16)
    XB = P.sbuf("XB", [128, 256], BF16)
    XT = carve("XT", [128, 4, 512], BF16)
    gates = P.sbuf("gates", [128, NT, 24], F32)
    PT = [P.sbuf("PT%d" % j, [128, 512], BF16) for j in range(2)]
    den8 = P.sbuf("den8", [128, 8], F32)
    wk8 = P.sbuf("wk8", [128, 8], F32)
    oacc = carve("oacc", [128, 512], F32)
    otmp = carve("otmp", [128, 512], F32)
    Lsb = carve("Lsb", [128, 8, 32], F32)
    pre = carve("pre", [128, 4, 32], F32)
    leadprev = P.sbuf("leadprev", [128, 4], F32)
    Spad = carve("Spad", [128, 4, 128], BF16)
    KCA = P.sbuf("KCA", [128, 2, 128], BF16)
    kcn = carve("kcn", [128, 128], F32)
    kcT = P.sbuf("kcT", [128, 2, 128], BF16)
    VCA = P.sbuf("VCA", [128, 2, 65], BF16)
    PTc = carve("PTc", [128, 2, 512], BF16)
    imp = carve("imp", [128, 8, 32], F32)
    score = P.sbuf("score", [128, 2, 32], F32)
    swork = P.sbuf("swork", [128, 2, 32], F32)
    mx8 = P.sbuf("mx8", [128, 2, 8], F32)
    thr = P.sbuf("thr", [128, 2], F32)
    nsel = P.sbuf("nsel", [128, 2, 32], BF16)
    nselT = carve("nselT", [128, 2, 512], BF16)

    P.op("pool", lambda e: e.memset(carry[:], 0.0), writes=[carry])
    for b_ in (QA, NQA, KA, SKA, WKA, Spad, KCA, VCA, leadprev, XT):
        P.op("pool", lambda e, b_=b_: e.memset(b_[:], 0.0), writes=[b_])
    for h in range(8):
        r = h % 4
        P.op("pool", lambda e, h=h, r=r: e.memset(QA[:, h, 64 + 3 * r:64 + 3 * r + 3], 1.0), writes=[QA], full=False)
    P.op("pool", lambda e: e.memset(KA[:, :, 76:79], 1.0), writes=[KA], full=False)
    P.op("pool", lambda e: e.memset(VCA[:, :, 64:65], 1.0), writes=[VCA], full=False)
    for which in ("f", "s", "w"):
        P.op("pool", lambda e, which=which: e.memset(VA_all(which)[:, :, 64:65], 1.0), writes=kv_all, full=False)
    for g in range(2):
        P.op("pool", lambda e, g=g: e.tensor_copy(out=KCA[:, g, 64:68], in_=alkc[:]), reads=[alkc], writes=[KCA], full=False)

    src_buf = [None]

    def headnorm(src_ap, nh, gain, out_ap, out_bufs, p0=0, p1=128):
        w = nh * 64
        sb = src_buf[0]
        P.op("act", lambda e: e.activation(out=sqb[p0:p1, 0:w], in_=src_ap, func=AF.Square), reads=[sb], writes=[sqb])
        P.op("dve", lambda e: e.reduce_sum(out=ss8[p0:p1, 0:nh], in_=sqb[p0:p1, 0:w].rearrange("p (h d) -> p h d", h=nh), axis=AX.X),
             reads=[sqb], writes=[ss8])
        P.op("act", lambda e: e.activation(out=ss8[p0:p1, 0:nh], in_=ss8[p0:p1, 0:nh], func=AF.Sqrt, scale=1.0 / 64, bias=EPS),
             reads=[ss8], writes=[ss8])
        P.op("dve", lambda e: e.reciprocal(out=ss8[p0:p1, 0:nh], in_=ss8[p0:p1, 0:nh]), reads=[ss8], writes=[ss8])
        P.op("dve", lambda e: e.tensor_tensor(out=ntmp[p0:p1, 0:w].rearrange("p (h d) -> p h d", h=nh),
                                              in0=src_ap.rearrange("p (h d) -> p h d", h=nh),
                                              in1=ss8[p0:p1, 0:nh].unsqueeze(2).to_broadcast([p1 - p0, nh, 64]), op=ALU.mult),
             reads=[sb, ss8], writes=[ntmp])
        P.op("dve", lambda e: e.tensor_tensor(out=out_ap, in0=ntmp[p0:p1, 0:w].rearrange("p (h d) -> p h d", h=nh),
                                              in1=gain[p0:p1, :].unsqueeze(1).to_broadcast([p1 - p0, nh, 64]), op=ALU.mult),
             reads=[ntmp, gain], writes=out_bufs, full=False)

    def norm_hT(x_t, G, lo_sh, i):
        P.op("pool", lambda e: e.memset(ssq[:], 0.0), writes=[ssq])
        P.op("act", lambda e: e.activation(out=htmp[:].rearrange("p k c -> p (k c)"), in_=x_t[:], func=AF.Square, accum_out=ssq[:]), reads=[x_t], writes=[htmp, ssq])
        P.op("act", lambda e: e.activation(out=rstd[:], in_=ssq[:], func=AF.Sqrt, scale=1.0 / 1024, bias=EPS), reads=[ssq], writes=[rstd])
        P.op("dve", lambda e: e.reciprocal(out=rstd[:], in_=rstd[:]), reads=[rstd], writes=[rstd])
        P.op("dve", lambda e: e.tensor_scalar(out=xn[:], in0=x_t[:], scalar1=rstd[:, 0:1], scalar2=None, op0=ALU.mult),
             reads=[x_t, rstd], writes=[xn])
        for k in range(8):
            P.op("pe", lambda e, k=k: e.transpose(out=ptr[:, k * 128:(k + 1) * 128], in_=xn[:, k * 128:(k + 1) * 128], identity=ident[:]),
                 reads=[xn, ident], writes=[ptr], full=False)
        vw = (lambda a: a) if i < 16 else (lambda a: a.rearrange("p k (s t) -> p k s t", t=8))
        P.op("dve", lambda e: e.tensor_tensor(out=vw(htmp[:]), in0=vw(ptr[:].rearrange("p (k t) -> p k t", k=8)),
                                              in1=seq_bc(G[:], 0, i), op=ALU.mult), reads=[ptr, G], writes=[htmp])
        return vw

    def mm8(out_ap, out_buf, lhs_buf, lhs_fn, rhs_buf, rhs_fn, nk=8):
        rb = list(lhs_buf) if isinstance(lhs_buf, (list, tuple)) else [lhs_buf]
        rb += list(rhs_buf) if isinstance(rhs_buf, (list, tuple)) else [rhs_buf]
        for k in range(nk):
            l_ap = lhs_fn(k)
            r_ap = rhs_fn(k)
            P.op("pe", lambda e, k=k, l_ap=l_ap, r_ap=r_ap: e.matmul(out_ap, lhsT=l_ap, rhs=r_ap, start=(k == 0), stop=(k == nk - 1)),
                 reads=rb, writes=[out_buf], full=False)

    CH = [(0, 512), (512, 264), (776, 512), (1288, 512), (1800, 280)]

    def premixer(i):
        x_t = xt[i % 2]
        rf, rn, rw, lfi = rows_fox, rows_nsa, rows_win, lf[i % 2]
        li = i % 4
        xsrc = I["xp"][i * 128:(i + 1) * 128, :] if i < 16 else I["xs"]
        P.dma("sp", lambda e: e.dma_start(out=x_t[:], in_=xsrc), x_t, writes=[x_t])
        vw = norm_hT(x_t, G1, 0, i)
        P.op("dve", lambda e: e.tensor_tensor(out=vw(hT[:]), in0=vw(htmp[:]), in1=seq_bc(modT[:], 0, i), op=ALU.add),
             reads=[htmp, modT], writes=[hT])

        def zchunk(ci):
            c0, cw = CH[ci]
            pzb = nxt(pz, "pz")
            mm8(pzb[:, 0:cw], pzb, hT, lambda k: hT[:, k, :], wbf, lambda k: wbf[:, k, c0:c0 + cw])
            src_buf[0] = pzb
            return pzb

        zA = zchunk(0)
        headnorm(zA[:, 0:512], 8, gains["fox_qn_g"], QA[:, :, 0:64], [QA])
        zB = zchunk(1)
        headnorm(zB[:, 0:128], 2, gains["fox_kn_g"], rf[:, 0:128].rearrange("p (h d) -> p h d", h=2), [rf])
        P.op("act", lambda e: e.copy(out=rf[:, 128:256], in_=zB[:, 128:256]), reads=[zB], writes=[rf], full=False)
        P.op("dve", lambda e: e.tensor_tensor(out=t8[:], in0=zB[:, 256:264], in1=bff[:], op=ALU.add), reads=[zB, bff], writes=[t8])
        P.op("act", lambda e: e.activation(out=t8[:], in_=t8[:], func=AF.Exp, scale=-1.0), reads=[t8], writes=[t8])
        P.op("act", lambda e: e.activation(out=t8[:], in_=t8[:], func=AF.Ln, bias=1.0), reads=[t8], writes=[t8])
        P.op("dve", lambda e: e.tensor_scalar(out=lfi[:], in0=t8[:], scalar1=-1.0, scalar2=None, op0=ALU.mult), reads=[t8], writes=[lfi])
        P.op("pool", lambda e: e.tensor_copy(out=KA[:, :, 0:64], in_=rf[:, 0:128].rearrange("p (h d) -> p h d", h=2)),
             reads=[rf], writes=[KA], full=False)
        P.op("pool", lambda e: e.tensor_copy(out=VA_tile("f", i)[:, :, 0:64], in_=rf[:, 128:256].rearrange("p (h d) -> p h d", h=2)),
             reads=[rf], writes=[FVA], full=False)
        if i < 16:
            P.dma("sp", lambda e: e.dma_start(out=O["fox_p"][i * 128:(i + 1) * 128, :], in_=rf[:]), rf, reads=[rf], final=True)
            P.dma("sp", lambda e: e.dma_start(out=O["lf_p"][i * 128:(i + 1) * 128, :], in_=lfi[:]), lfi, reads=[lfi], final=True)
        else:
            P.dma("sp", lambda e: e.dma_start(out=O["fox_s"], in_=rf[:]), rf, reads=[rf], final=True)
            P.dma("sp", lambda e: e.dma_start(out=O["lf_s"], in_=lfi[:]), lfi, reads=[lfi], final=True)
        tri = tri_p if i < 16 else tri_s
        P.op("pe", lambda e: e.matmul(pmisc[:, 0:8], lhsT=tri[:], rhs=lfi[:], start=True, stop=True), reads=[tri, lfi], writes=[pmisc], full=False)
        if i < 16:
            P.op("pe", lambda e: e.matmul(pmisc[:, 8:16], lhsT=ones[:], rhs=lfi[:], start=True, stop=True), reads=[ones, lfi], writes=[pmisc], full=False)
            P.op("dve", lambda e: e.tensor_tensor(out=c8[:], in0=pmisc[:, 0:8], in1=carry[:], op=ALU.add), reads=[pmisc, carry], writes=[c8])
            P.op("dve", lambda e: e.tensor_tensor(out=carry[:], in0=pmisc[:, 8:16], in1=carry[:], op=ALU.add), reads=[pmisc, carry], writes=[carry])
        else:
            P.op("dve", lambda e: e.tensor_copy(out=c8[:], in_=pmisc[:, 0:8]), reads=[pmisc], writes=[c8])
        P.op("dve", lambda e: e.tensor_copy(out=PC[:, :, 0], in_=c8[:]), reads=[c8], writes=[PC], full=False)
        P.op("dve", lambda e: e.tensor_tensor(out=r8[:], in0=c8[:], in1=PC[:, :, 0], op=ALU.subtract), reads=[c8, PC], writes=[r8])
        P.op("dve", lambda e: e.tensor_copy(out=PC[:, :, 1], in_=r8[:]), reads=[r8], writes=[PC], full=False)
        P.op("dve", lambda e: e.tensor_tensor(out=c8[:], in0=r8[:], in1=PC[:, :, 1], op=ALU.subtract), reads=[r8, PC], writes=[c8])
        P.op("dve", lambda e: e.tensor_copy(out=PC[:, :, 2], in_=c8[:]), reads=[c8], writes=[PC], full=False)
        for g in range(2):
            P.op("pool", lambda e, g=g: e.tensor_scalar(out=KA[:, g, 64:76].rearrange("p (r c) -> p r c", c=3), in0=PC[:, 4 * g:4 * g + 4, :],
                                                        scalar1=-1.0, scalar2=None, op0=ALU.mult), reads=[PC], writes=[KA], full=False)
        P.op("pool", lambda e: e.tensor_copy(out=QA[:, :, 76:79], in_=PC[:]), reads=[PC], writes=[QA], full=False)
        zC = zchunk(2)
        headnorm(zC[:, 0:512], 8, gains["nsa_qn_g"], NQA[:, :, 0:64], [NQA])
        P.op("pool", lambda e: e.tensor_copy(out=NQA[:, :, 64:68], in_=alq[:, i, :, :]), reads=[alq], writes=[NQA], full=False)
        zD = zchunk(3)
        P.op("act", lambda e: e.copy(out=rn[:, 0:256], in_=zD[:, 0:256]), reads=[zD], writes=[rn], full=False)
        headnorm(zD[:, 256:384], 2, gains["nsa_kn_slc_g"], rn[:, 256:384].rearrange("p (h d) -> p h d", h=2), [rn])
        P.op("act", lambda e: e.copy(out=rn[:, 384:512], in_=zD[:, 384:512]), reads=[zD], writes=[rn], full=False)
        P.op("pool", lambda e: e.tensor_copy(out=SKA[:, :, 0:64], in_=rn[:, 256:384].rearrange("p (h d) -> p h d", h=2)),
             reads=[rn], writes=[SKA], full=False)
        P.op("pool", lambda e: e.tensor_copy(out=SKA[:, :, 64:68], in_=alk[:, i, :].unsqueeze(1).to_broadcast([128, 2, 4])),
             reads=[alk], writes=[SKA], full=False)
        P.op("pool", lambda e: e.tensor_copy(out=VA_tile("s", i)[:, :, 0:64], in_=rn[:, 384:512].rearrange("p (h d) -> p h d", h=2)),
             reads=[rn], writes=[SVA], full=False)
        if i < 16:
            P.op("pool", lambda e: e.tensor_copy(out=XB[:], in_=rn[:, 0:256]), reads=[rn], writes=[XB])
        zE = zchunk(4)
        headnorm(zE[:, 0:128], 2, gains["nsa_kn_win_g"], rw[:, 0:128].rearrange("p (h d) -> p h d", h=2), [rw])
        P.op("act", lambda e: e.copy(out=rw[:, 128:256], in_=zE[:, 128:256]), reads=[zE], writes=[rw], full=False)
        P.op("act", lambda e: e.activation(out=gates[:, i, :], in_=zE[:, 256:280], func=AF.Sigmoid), reads=[zE], writes=[gates], full=False)
        P.op("pool", lambda e: e.tensor_copy(out=WKA[:, :, 0:64], in_=rw[:, 0:128].rearrange("p (h d) -> p h d", h=2)),
             reads=[rw], writes=[WKA], full=False)
        P.op("pool", lambda e: e.tensor_copy(out=WKA[:, :, 64:68], in_=alk[:, i, :].unsqueeze(1).to_broadcast([128, 2, 4])),
             reads=[alk], writes=[WKA], full=False)
        P.op("pool", lambda e: e.tensor_copy(out=VA_tile("w", i)[:, :, 0:64], in_=rw[:, 128:256].rearrange("p (h d) -> p h d", h=2)),
             reads=[rw], writes=[WVA], full=False)
        if i < 16:
            P.dma("sp", lambda e: e.dma_start(out=O["nsa_p"][i * 128:(i + 1) * 128, :], in_=rn[:]), rn, reads=[rn], final=True)
            if i >= 12:
                P.dma("sp", lambda e: e.dma_start(out=O["win_p"][(i - 12) * 128:(i - 11) * 128, :], in_=rw[:]), rw, reads=[rw], final=True)
        else:
            P.dma("sp", lambda e: e.dma_start(out=O["nsa_s"], in_=rn[:]), rn, reads=[rn], final=True)
            P.dma("sp", lambda e: e.dma_start(out=O["win_s"][:, 504:512, :], in_=rw[:]), rw, reads=[rw], final=True)
        if "c" in KSKIP:
            return
        for j, (src, which, kb) in enumerate(((KA, "f", fkT), (SKA, "s", skT), (WKA, "w", wkT)) if "k" not in KSKIP else ()):
            for g in range(2):
                P.op("pe", lambda e, src=src, g=g, j=j: e.transpose(out=ptr[:, (2 * j + g) * 128:(2 * j + g + 1) * 128], in_=src[:, g, :], identity=ident[:]),
                     reads=[src, ident], writes=[ptr], full=False)
        if i < 16 and "x" not in KSKIP:
            for ty in range(2):
                P.op("pe", lambda e, ty=ty: e.transpose(out=ptr[:, (6 + ty) * 128:(7 + ty) * 128], in_=XB[:, ty * 128:(ty + 1) * 128], identity=ident[:]),
                     reads=[XB, ident], writes=[ptr], full=False)
        for j, (which, kb) in enumerate((("f", fkT), ("s", skT), ("w", wkT)) if "k" not in KSKIP else ()):
            eng = "act" if j != 1 else "dve"
            P.op(eng, lambda e, j=j, which=which, eng=eng: e.tensor_copy(out=kT_tile2(which, i), in_=ptr[:, 2 * j * 128:(2 * j + 2) * 128].rearrange("p (g c) -> p g c", g=2))
                 if eng == "dve" else e.copy(out=kT_tile2(which, i), in_=ptr[:, 2 * j * 128:(2 * j + 2) * 128].rearrange("p (g c) -> p g c", g=2)),
                 reads=[ptr], writes=[kb], full=False)
        if i < 16 and "x" not in KSKIP:
            for g in range(2):
                P.op("dve", lambda e, g=g: e.tensor_copy(
                    out=XT[g * 64:(g + 1) * 64, 2 * g:2 * g + 2, :].rearrange("p t (c n) -> p t c n", c=16)[:, :, :, li * 8:(li + 1) * 8],
                    in_=ptr[g * 64:(g + 1) * 64, 768:1024].rearrange("p (t n c) -> p t c n", t=2, c=16)),
                    reads=[ptr], writes=[XT], full=False)
        if "q" not in KSKIP:
            for (src, dstb, dv) in ((QA, fqT, fqT_v), (NQA, nqT, nqT_v))[int(os.environ.get("KQ0", "0")):int(os.environ.get("KQ1", "2"))]:
                for h in range(8):
                    P.op("pe", lambda e, src=src, h=h: e.transpose(out=ptr[:, h * 128:(h + 1) * 128], in_=src[:, h, :], identity=ident[:]),
                         reads=[src, ident], writes=[ptr], full=False)
                if "V" not in KSKIP:
                    P.op("dve", lambda e, dv=dv: e.tensor_copy(out=dv[:, :, li * 128:(li + 1) * 128], in_=ptr[:].rearrange("p (h c) -> p h c", h=8)),
                         reads=[ptr], writes=[dstb], full=False)
                else:
                    P.op("act", lambda e, dv=dv: e.copy(out=dv[:, :, li * 128:(li + 1) * 128], in_=ptr[:].rearrange("p (h c) -> p h c", h=8)),
                         reads=[ptr], writes=[dstb], full=False)

    def compress(sg):
        for ty in range(2):
            tyn = "kv"[ty]
            for g in range(2):
                for lt in range(2):
                    col = ((ty * 2 + g) * 2 + lt) * 32
                    for c in range(16):
                        P.op("pe", lambda e, tyn=tyn, ty=ty, g=g, lt=lt, c=c, col=col: e.matmul(
                            pmisc[:, col:col + 32], lhsT=w1[tyn][:, lt * 16 + c, :],
                            rhs=XT[:, 2 * g + ty, c * 32:(c + 1) * 32], start=(c == 0), stop=(c == 15)),
                            reads=[w1[tyn], XT], writes=[pmisc], full=False)
        P.op("act", lambda e: e.copy(out=Lsb[:].rearrange("p a c -> p (a c)"), in_=pmisc[:, 0:256]), reads=[pmisc], writes=[Lsb])
        if KCMP < 2:
            return
        L4 = Lsb[:].rearrange("p (a l) c -> p a l c", l=2)
        P.op("dve", lambda e: e.tensor_tensor(out=pre[:, :, 1:32], in0=L4[:, :, 0, 0:31], in1=L4[:, :, 1, 1:32], op=ALU.add), reads=[Lsb], writes=[pre], full=False)
        P.op("dve", lambda e: e.tensor_tensor(out=pre[:, :, 0:1], in0=leadprev[:].unsqueeze(2), in1=L4[:, :, 1, 0:1], op=ALU.add),
             reads=[Lsb, leadprev], writes=[pre], full=False)
        P.op("dve", lambda e: e.tensor_copy(out=leadprev[:].unsqueeze(2), in_=L4[:, :, 0, 31:32]), reads=[Lsb], writes=[leadprev])
        for ty in range(2):
            P.op("act", lambda e, ty=ty: e.activation(out=Spad[:, 2 * ty:2 * ty + 2, 32 * sg:32 * sg + 32], in_=pre[:, 2 * ty:2 * ty + 2, :],
                                                      func=AF.Silu, bias=cbias[:, ty:ty + 1]), reads=[pre, cbias], writes=[Spad], full=False)
        if KCMP < 3:
            return
        pzb = nxt(pz, "pz")
        for ty in range(2):
            tyn = "kv"[ty]
            for g in range(2):
                cc = (ty * 2 + g) * 64
                P.op("pe", lambda e, ty=ty, g=g, cc=cc, tyn=tyn: e.matmul(pzb[:, cc:cc + 64], lhsT=Spad[:, 2 * ty + g, :], rhs=w2[tyn][:],
                                                                         start=True, stop=True), reads=[Spad, w2[tyn]], writes=[pzb], full=False)
        if KCMP < 4:
            return
        p0, p1 = 32 * sg, 32 * sg + 32
        src_buf[0] = pzb
        headnorm(pzb[p0:p1, 0:128], 2, gains["nsa_kn_cmp_g"], kcn[p0:p1, :].rearrange("p (h d) -> p h d", h=2), [kcn], p0=p0, p1=p1)
        if KCMP < 5:
            return
        P.op("pool", lambda e: e.tensor_copy(out=KCA[p0:p1, :, 0:64], in_=kcn[p0:p1, :].rearrange("p (h d) -> p h d", h=2)),
             reads=[kcn], writes=[KCA], full=False)
        P.op("act", lambda e: e.copy(out=VCA[p0:p1, :, 0:64], in_=pzb[p0:p1, 128:256].rearrange("p (h d) -> p h d", h=2)),
             reads=[pzb], writes=[VCA], full=False)
        if KCMP < 6:
            return
        for g in range(2):
            P.op("pe", lambda e, g=g: e.transpose(out=ptr[:, g * 128:(g + 1) * 128], in_=KCA[:, g, :], identity=ident[:]),
                 reads=[KCA, ident], writes=[ptr], full=False)
        P.op("act", lambda e: e.copy(out=kcT[:], in_=ptr[:, 0:256].rearrange("p (g c) -> p g c", g=2)), reads=[ptr], writes=[kcT])

    def attend(g, q_rhs, q_buf, k_list, pog):
        n = len(k_list)
        for idx, (kT, kTb, va, vab, masks, keep) in enumerate(k_list):
            ps = nxt(pss, "pss")
            P.op("pe", lambda e, ps=ps, kT=kT, masks=masks: e.matmul(ps[:, 0:512], lhsT=kT, rhs=q_rhs, start=True, stop=(len(masks) == 0)),
                 reads=[kTb, q_buf], writes=[ps], full=False)
            for mi, (ml, mr, c0, c1, mb) in enumerate(masks):
                P.op("pe", lambda e, ps=ps, ml=ml, mr=mr, c0=c0, c1=c1, mi=mi, masks=masks: e.matmul(
                    ps[:, c0:c1], lhsT=ml, rhs=mr, start=False, stop=(mi == len(masks) - 1), skip_group_check=True),
                    reads=list(mb), writes=[ps], full=False)
            if keep is not None:
                pt_ap, pt_buf = keep
            else:
                ptb = nxt(PT, "pt")
                pt_ap, pt_buf = ptb[:], ptb
            P.op("act", lambda e, ps=ps, pt_ap=pt_ap: e.activation(out=pt_ap, in_=ps[:, 0:512], func=AF.Exp), reads=[ps], writes=[pt_buf],
                 full=(keep is None))
            for h in range(4):
                P.op("pe", lambda e, h=h, idx=idx, pt_ap=pt_ap, va=va: e.matmul(
                    pog[:, h * 65:(h + 1) * 65], lhsT=pt_ap[:, h * 128:(h + 1) * 128], rhs=va,
                    start=(idx == 0 and h == 0), stop=(idx == n - 1), skip_group_check=True),
                    reads=[pt_buf, vab], writes=[pog], full=False)

    def den_recip(pog, g):
        dv = pog[:, 0:260].rearrange("p (h c) -> p h c", c=65)[:, :, 64]
        P.op("dve", lambda e: e.tensor_scalar(out=den8[:, 4 * g:4 * g + 4], in0=dv, scalar1=1e-30, scalar2=None, op0=ALU.max),
             reads=[pog], writes=[den8], full=False)
        P.op("dve", lambda e: e.reciprocal(out=den8[:, 4 * g:4 * g + 4], in_=den8[:, 4 * g:4 * g + 4]), reads=[den8], writes=[den8], full=False)

    def po_view(pog):
        return pog[:, 0:260].rearrange("p (h c) -> p h c", c=65)[:, :, 0:64]

    def q_rhs(qv, g, li):
        return qv[:, 4 * g:4 * g + 4, li * 128:(li + 1) * 128]

    def attention_prompt(i):
        li = i % 4
        sl = ot
        o_fox = ot[:, 0:512]
        o_nsa = ot[:, 512:1024]
        diag = (ident[:], cm_diag[:], 0, 512, [ident, cm_diag])
        for g in range(2):
            kl = []
            for kt in range(i + 1):
                masks = [diag] if kt == i else []
                kl.append((kT_ap("f", g, kt * 128, (kt + 1) * 128), fkT, VA_ap("f", kt, g), FVA, masks, None))
            attend(g, q_rhs(fqT_v, g, li), fqT, kl, po[g])
            den_recip(po[g], g)
            P.op("dve", lambda e, g=g: e.tensor_tensor(out=o_fox[:, g * 256:(g + 1) * 256].rearrange("p (h d) -> p h d", h=4), in0=po_view(po[g]),
                                                       in1=den8[:, 4 * g:4 * g + 4].unsqueeze(2).to_broadcast([128, 4, 64]), op=ALU.mult),
                 reads=[po[g], den8], writes=[sl], full=False)
        gate3 = gates[:, i, :].rearrange("p (h k) -> p h k", k=3)
        for g in range(2):
            masks = [(ident[:], cmaskc[:, i * 128:(i + 1) * 128], h * 128, (h + 1) * 128, [ident, cmaskc]) for h in range(4)]
            kl = [(kcT[:, g, :], kcT, VCA[:, g, :], VCA, masks, (PTc[:, g, :], PTc))]
            attend(g, q_rhs(nqT_v, g, li), nqT, kl, po[g])
            den_recip(po[g], g)
        if i >= 8:
            for g in range(2):
                for h in range(4):
                    hh = 4 * g + h
                    P.op("pe", lambda e, g=g, h=h, hh=hh: e.matmul(pmisc[:, hh * 32:(hh + 1) * 32], lhsT=PTc[:, g, h * 128:(h + 1) * 128], rhs=bmat[:],
                                                                  start=True, stop=True), reads=[PTc, bmat], writes=[pmisc], full=False)
            P.op("dve", lambda e: e.tensor_tensor(out=imp[:], in0=pmisc[:, 0:256].rearrange("p (h j) -> p h j", h=8),
                                                  in1=den8[:].unsqueeze(2).to_broadcast([128, 8, 32]), op=ALU.mult), reads=[pmisc, den8], writes=[imp])
            P.op("dve", lambda e: e.reduce_sum(out=score[:], in_=imp[:].rearrange("p (g h) j -> p g j h", g=2), axis=AX.X), reads=[imp], writes=[score])
            P.op("dve", lambda e: e.tensor_tensor(out=score[:], in0=score[:], in1=vz[:, i, :].unsqueeze(1).to_broadcast([128, 2, 32]), op=ALU.mult),
                 reads=[score, vz], writes=[score])
            P.op("dve", lambda e: e.tensor_tensor(out=score[:], in0=score[:], in1=fz2[:, i, :].unsqueeze(1).to_broadcast([128, 2, 32]), op=ALU.add),
                 reads=[score, fz2], writes=[score])
            for g in range(2):
                P.op("dve", lambda e, g=g: e.max(out=mx8[:, g, :], in_=score[:, g, :]), reads=[score], writes=[mx8], full=False)
                P.op("dve", lambda e, g=g: e.match_replace(out=swork[:, g, :], in_to_replace=mx8[:, g, :], in_values=score[:, g, :], imm_value=-2.0),
                     reads=[score, mx8], writes=[swork], full=False)
                P.op("dve", lambda e, g=g: e.max(out=mx8[:, g, :], in_=swork[:, g, :]), reads=[swork], writes=[mx8], full=False)
                P.op("dve", lambda e, g=g: e.tensor_reduce(out=thr[:, g:g + 1], in_=mx8[:, g, :], axis=AX.X, op=ALU.min), reads=[mx8], writes=[thr], full=False)
                P.op("dve", lambda e, g=g: e.tensor_scalar(out=nsel[:, g, :], in0=score[:, g, :], scalar1=thr[:, g:g + 1], scalar2=NEG,
                                                           op0=ALU.is_lt, op1=ALU.mult), reads=[score, thr], writes=[nsel], full=False)
            for g in range(2):
                P.op("pe", lambda e, g=g: e.transpose(out=ptr[0:32, g * 128:(g + 1) * 128], in_=nsel[:, g, :], identity=ident[:]),
                     reads=[nsel, ident], writes=[ptr], full=False)
            P.op("dve", lambda e: e.tensor_copy(out=nselT[0:32, :, :].rearrange("p g (h c) -> p g h c", h=4),
                                                in_=ptr[0:32, 0:256].rearrange("p (g c) -> p g c", g=2).unsqueeze(2).to_broadcast([32, 2, 4, 128])),
                 reads=[ptr], writes=[nselT])
        P.op("dve", lambda e: e.tensor_tensor(out=wk8[:], in0=den8[:], in1=gate3[:, :, 0], op=ALU.mult), reads=[den8, gates], writes=[wk8])
        for g in range(2):
            P.op("dve", lambda e, g=g: e.tensor_tensor(out=oacc[:, g * 256:(g + 1) * 256].rearrange("p (h d) -> p h d", h=4), in0=po_view(po[g]),
                                                       in1=wk8[:, 4 * g:4 * g + 4].unsqueeze(2).to_broadcast([128, 4, 64]), op=ALU.mult),
                 reads=[po[g], wk8], writes=[oacc], full=False)
        for br, which, kb, vb, kts in ((1, "s", skT, SVA, list(range(i + 1))), (2, "w", wkT, WVA, list(range(max(0, i - 4), i + 1)))):
            for g in range(2):
                kl = []
                for kt in kts:
                    masks = []
                    if kt == i:
                        masks.append(diag)
                    if br == 2 and kt == i - 4:
                        masks.append((ident[:], cm_win[:], 0, 512, [ident, cm_win]))
                    if br == 1 and i >= 8:
                        masks.append((esel[:, kt, :], nselT[0:32, g, :], 0, 512, [esel, nselT]))
                    kl.append((kT_ap(which, g, kt * 128, (kt + 1) * 128), kb, VA_ap(which, kt, g), vb, masks, None))
                attend(g, q_rhs(nqT_v, g, li), nqT, kl, po[g])
                den_recip(po[g], g)
            P.op("dve", lambda e, br=br: e.tensor_tensor(out=wk8[:], in0=den8[:], in1=gate3[:, :, br], op=ALU.mult), reads=[den8, gates], writes=[wk8])
            for g in range(2):
                P.op("dve", lambda e, g=g: e.tensor_tensor(out=otmp[:, g * 256:(g + 1) * 256].rearrange("p (h d) -> p h d", h=4), in0=po_view(po[g]),
                                                           in1=wk8[:, 4 * g:4 * g + 4].unsqueeze(2).to_broadcast([128, 4, 64]), op=ALU.mult),
                     reads=[po[g], wk8], writes=[otmp], full=False)
            if br == 1:
                P.op("dve", lambda e: e.tensor_tensor(out=oacc[:], in0=oacc[:], in1=otmp[:], op=ALU.add), reads=[oacc, otmp], writes=[oacc])
            else:
                P.op("dve", lambda e: e.tensor_tensor(out=o_nsa, in0=oacc[:], in1=otmp[:], op=ALU.add), reads=[oacc, otmp], writes=[sl], full=False)
        P.dma("sp", lambda e: e.dma_start(out=oscr[i], in_=ot[:]), ot, reads=[ot], writes=[oscr_b[i]])

    for sg in range(4):
        for i in range(4 * sg, 4 * sg + 4):
            premixer(i)
        if STAGE >= 1:
            compress(sg)
        for i in range(4 * sg, 4 * sg + 4):
            if STAGE >= 2:
                attention_prompt(i)
    premixer(16)
    def mk_carver(flat_ap, nelem, bufs):
        off = [0]

        def cv(name, shape, dt):
            n = 1
            for d_ in shape[1:]:
                n *= d_
            nb = n * (2 if dt in (F32, I32) else 1)
            ap = flat_ap[:, off[0]:off[0] + nb]
            off[0] += nb
            assert off[0] <= nelem, (name, off[0], nelem)
            if dt in (F32, I32):
                ap = ap.bitcast(dt)
            if len(shape) == 3:
                ap = ap.rearrange("p (a b) -> p a b", a=shape[1])
            elif len(shape) == 4:
                ap = ap.rearrange("p (a b c) -> p a b c", a=shape[1], b=shape[2])
            elif len(shape) == 5:
                ap = ap.rearrange("p (a b c d) -> p a b c d", a=shape[1], b=shape[2], c=shape[3])
            if shape[0] < 128:
                ap = ap[0:shape[0]]
            b = Buf(name, ap)
            bufs.append(b)
            return b
        return cv

    sbufs = []
    c1bufs = []
    cw = mk_carver(wbf[:].rearrange("p k c -> p (k c)"), 8 * 2080, sbufs)
    ca = mk_carver(arena_t[:, 8192:18 * 1024], 10 * 1024, sbufs)
    c1 = mk_carver(arena_t[:, 0:8192], 8192, c1bufs)
    fresh_t = nc.alloc_sbuf_tensor("sb_sfresh", [128, 2304], BF16)
    cs = mk_carver(fresh_t[:, :], 2304, sbufs)
    stg = cw("stg", [128, 4096], F32)
    KAs = cw("KAs", [128, 16, 2, 80], BF16)
    VAs = cw("VAs", [128, 16, 2, 65], BF16)
    KTs = cw("KTs", [128, 1024], BF16)
    Ls = cw("Ls", [128, 4, 128], F32)
    sbfc = KAs.t.rearrange("p a b c -> p (a b c)")[:, 0:2048].rearrange("p (t q r d) -> p t q r d", t=4, q=4, r=2)
    W1s = {"k": ca("W1sk", [128, 16, 128], BF16), "v": ca("W1sv", [128, 16, 128], BF16)}
    OTs = ca("OTs", [128, 1024], F32)
    Lf0 = ca("Lf0", [128, 4, 16, 8], F32)
    Lf1 = ca("Lf1", [128, 4, 16, 8], F32)
    PCs = ca("PCs", [128, 4, 16, 24], BF16)
    kcTs = c1("kcTs", [128, 2, 512], BF16)
    VCAs = c1("VCAs", [128, 4, 2, 65], BF16)
    KCAs = c1("KCAs", [128, 2, 80], BF16)
    Ssp = c1("Ssp", [128, 4, 128], BF16)
    PTcs = c1("PTcs", [128, 256], BF16)
    nselTs = c1("nselTs", [128, 2, 32], BF16)
    lt64 = c1("lt64", [128, 2, 64], F32)
    pre64 = c1("pre64", [128, 64], F32)
    lps = c1("lps", [128, 4], F32)
    ctot = c1("ctot", [128, 4, 8], F32)
    scb = c1("scb", [128, 2, 4, 8], F32)
    S2 = c1("S2", [128, 4, 8], F32)
    impn = c1("impn", [32, 2, 132], F32)
    scs = c1("scs", [32, 132], F32)
    sws = c1("sws", [32, 132], F32)
    mx8s = c1("mx8s", [32, 8], F32)
    thrs = c1("thrs", [32, 2], F32)
    nsels = c1("nsels", [32, 128], BF16)
    ghall = c1("ghall", [32, 24], F32)
    ghm = c1("ghm", [32, 4, 3], F32)
    gh = c1("gh", [32, 2, 3], F32)
    rd1 = c1("rd1", [32, 2], F32)
    wk1 = c1("wk1", [32, 2], F32)
    oaccs = c1("oaccs", [32, 2, 64], F32)
    otmps = c1("otmps", [32, 2, 64], F32)
    osb = c1("osb", [32, 4, 64], BF16)
    KTs2 = c1("KTs2", [128, 1024], BF16)
    KTb = [KTs, KTs2]
    idxf = cs("idxf", [128, 16, 12], F32)
    idxi = cs("idxi", [128, 16, 12], I32)
    ptq = cs("ptq", [128, 16, 12], I32)
    clist = []

    def cdef(cv, name, shape, dt, src):
        b = cv(name, shape, dt)
        clist.append((b, src))
        return b
    bmask = cdef(ca, "bmask", [128, 512], BF16, I["bmask"])
    esels = cdef(cs, "esels", [128, 8, 128], BF16, I["esels"].rearrange("p (t j) -> p t j", t=8))
    zall = cdef(cw, "zall", [128, 512], F32, I["zall"])
    bmats = cdef(cw, "bmats", [128, 4, 132], BF16, I["bmats"].rearrange("p (t j) -> p t j", t=4))
    zsel = cdef(c1, "zsel", [32, 4, 256], BF16, I["zsel"].rearrange("p (h c) -> p h c", h=4))
    alks_n = cdef(c1, "alks_n", [128, 8, 8, 4], BF16, I["alks_n"].rearrange("p (b r c) -> p b r c", b=8, r=8))
    alkw = cdef(c1, "alkw", [128, 4, 4], BF16, I["alkw"].rearrange("p (t c) -> p t c", t=4))
    alkcs = cdef(c1, "alkcs", [128, 4, 4], BF16, I["alkcs"].rearrange("p (t c) -> p t c", t=4))
    ustr = cdef(c1, "ustr", [128, 128], F32, I["ustr"])
    s0m = cdef(c1, "s0m", [128, 32], BF16, I["s0m"])
    wm0 = cdef(c1, "wm0", [128, 32], BF16, I["wm0"])
    gsum = cdef(cs, "gsum", [32, 32], F32, I["gsum"])
    fz2s = cdef(c1, "fz2s", [32, 132], F32, I["fz2s"])
    ohc = cdef(c1, "ohc", [32, 4, 3], F32, I["ohc"].rearrange("p (h k) -> p h k", h=4))
    mulc = cdef(cs, "mulc", [128, 12], F32, I["mulc"])
    addc = cdef(cs, "addc", [128, 12], F32, I["addc"])
    clist.append((ptq, I["ptq"].rearrange("p (s b) -> p s b", s=16)))
    P.op("pool", lambda e: e.memset(ssq[:], 0.0), reads=[], writes=[wbf, fqT, nqT] + [b for b in ar_bufs if b not in (w1["k"], w1["v"])] + sbufs + [ssq])
    for ty in ("k", "v"):
        for r2 in range(2):
            P.op("dve", lambda e, ty=ty, r2=r2: e.tensor_copy(
                out=W1s[ty][r2 * 64:(r2 + 1) * 64, :, :],
                in_=w1[ty][r2 * 64:(r2 + 1) * 64, :, :].rearrange("p (a r) h -> p a r h", r=2)[:, :, r2, :]),
                reads=[w1[ty]], writes=[W1s[ty]], full=False)
    P.op("pool", lambda e: e.memset(ssq[:], 0.0), reads=[], writes=[w1["k"], w1["v"], ssq] + c1bufs)
    sbufs += c1bufs
    for (b_, src_) in clist:
        P.dma("sp", lambda e, b_=b_, src_=src_: e.dma_start(out=b_[:], in_=src_), b_, writes=[b_])
    P.op("dve", lambda e: e.tensor_copy(out=idxf[:], in_=ptq[:]), reads=[ptq], writes=[idxf])
    P.op("dve", lambda e: e.tensor_tensor(out=idxf[:], in0=idxf[:], in1=mulc[:].unsqueeze(1).to_broadcast([128, 16, 12]), op=ALU.mult), reads=[idxf, mulc], writes=[idxf])
    P.op("dve", lambda e: e.tensor_tensor(out=idxf[:], in0=idxf[:], in1=addc[:].unsqueeze(1).to_broadcast([128, 16, 12]), op=ALU.add), reads=[idxf, addc], writes=[idxf])
    P.op("dve", lambda e: e.tensor_copy(out=idxi[:], in_=idxf[:]), reads=[idxf], writes=[idxi])
    for b_ in (KAs, VAs, OTs, S2, Ssp, KCAs, VCAs, kcTs):
        P.op("pool", lambda e, b_=b_: e.memset(b_[:], 0.0), writes=[b_])
    P.op("pool", lambda e: e.memset(VCAs[:, :, :, 64:65], 1.0), writes=[VCAs], full=False)
    gates16 = gates[:, 16, :]

    def ktile_ones():
        P.op("pool", lambda e: e.memset(KAs[:, :, :, 64:80], 0.0), writes=[KAs], full=False)
        P.op("pool", lambda e: e.memset(VAs[:, :, :, 64:65], 1.0), writes=[VAs], full=False)

    fox_rows = I["cache_fox_kv"]
    nsa_cmp_rows = I["cache_nsa_cmp"]
    nsa_slc_rows = I["cache_nsa_slc"]
    stgh = [Buf("stgh0", stg.t[:, 0:2048]), Buf("stgh1", stg.t[:, 2048:4096])]
    sbufs.extend(stgh)

    def stg_guard():
        P.op("pool", lambda e: e.memset(ssq[:], 0.0), reads=[], writes=[stg, stgh[0], stgh[1], ssq])
    lf_rows = I["cache_fox_logf"]

    def gather(out_ap, out_buf, rows, s, col):
        P.dma("pool", lambda e: e.indirect_dma_start(out=out_ap, out_offset=None, in_=rows,
                                                     in_offset=bass.IndirectOffsetOnAxis(ap=idxi[:, s, col:col + 1], axis=0)),
              out_buf, reads=[idxi], writes=[out_buf])

    rcnt = [0]

    def round_a(pairs, nrow, q_rhs_fn, qbuf, masks_fn):
        n = len(pairs)
        kt = KTb[rcnt[0] % 2]
        rcnt[0] += 1
        for j, (ka, kab, va, vab, g) in enumerate(pairs):
            P.op("pe", lambda e, j=j, ka=ka: e.transpose(out=ptr[0:nrow, j * 128:(j + 1) * 128], in_=ka, identity=ident[:]),
                 reads=[kab, ident], writes=[ptr], full=False)
        P.op("dve", lambda e: e.tensor_copy(out=kt[0:nrow, 0:n * 128], in_=ptr[0:nrow, 0:n * 128]), reads=[ptr], writes=[kt])
        ps = nxt(pss, "pss")
        for j, (ka, kab, va, vab, g) in enumerate(pairs):
            ms = masks_fn(j, g)
            P.op("pe", lambda e, j=j, g=g, ms=ms: e.matmul(ps[:, j * 32:(j + 1) * 32], lhsT=kt[0:nrow, j * 128:(j + 1) * 128], rhs=q_rhs_fn(g, nrow),
                                                          start=True, stop=(len(ms) == 0)), reads=[kt, qbuf], writes=[ps], full=False)
            for mi, (ml, mr, mb) in enumerate(ms):
                P.op("pe", lambda e, j=j, ml=ml, mr=mr, mi=mi, ms=ms: e.matmul(ps[:, j * 32:(j + 1) * 32], lhsT=ml, rhs=mr, start=False,
                                                                             stop=(mi == len(ms) - 1), skip_group_check=True),
                     reads=list(mb), writes=[ps], full=False)
        return (pairs, ps)

    def round_b(state, acc_started):
        pairs, ps = state
        n = len(pairs)
        ptb = nxt(PT, "pt")
        P.op("act", lambda e: e.activation(out=ptb[:, 0:n * 32], in_=ps[:, 0:n * 32], func=AF.Exp), reads=[ps], writes=[ptb])
        for j, (ka, kab, va, vab, g) in enumerate(pairs):
            first = not acc_started[g]
            acc_started[g] = True
            P.op("pe", lambda e, j=j, g=g, va=va, first=first: e.matmul(po[g][0:32, 0:65], lhsT=ptb[:, j * 32:(j + 1) * 32], rhs=va,
                                                                       start=first, stop=False, skip_group_check=True),
                 reads=[ptb, vab], writes=[po[g]], full=False)

    def run_rounds(items, acc_started):
        pending = None
        for kind, arg in items:
            if kind == "pre":
                if pending is not None:
                    round_b(pending, acc_started)
                    pending = None
                arg()
                continue
            pairs_fn, nrow, qf, qb, mf = arg
            st = round_a(pairs_fn(), nrow, qf, qb, mf)
            if pending is not None:
                round_b(pending, acc_started)
            pending = st
        if pending is not None:
            round_b(pending, acc_started)

    def s_round(pairs, nrow, q_rhs_fn, qbuf, masks_fn, acc_started):
        round_b(round_a(pairs, nrow, q_rhs_fn, qbuf, masks_fn), acc_started)

    def new_tile(which, kb, vb, q_rhs_fn, qbuf, s, acc_started):
        ps = nxt(pss, "pss")
        for g in range(2):
            P.op("pe", lambda e, g=g: e.matmul(ps[:, g * 32:(g + 1) * 32], lhsT=kT_ap(which, g, 2048, 2176), rhs=q_rhs_fn(g, 128), start=True, stop=False),
                 reads=[kb, qbuf], writes=[ps], full=False)
            P.op("pe", lambda e, g=g: e.matmul(ps[:, g * 32:(g + 1) * 32], lhsT=ident[:], rhs=bmask[:, s * 32:(s + 1) * 32], start=False, stop=True,
                                               skip_group_check=True), reads=[ident, bmask], writes=[ps], full=False)
        ptb = nxt(PT, "pt")
        P.op("act", lambda e: e.activation(out=ptb[:, 0:64], in_=ps[:, 0:64], func=AF.Exp), reads=[ps], writes=[ptb])
        for g in range(2):
            first = not acc_started[g]
            acc_started[g] = True
            P.op("pe", lambda e, g=g, first=first: e.matmul(po[g][0:32, 0:65], lhsT=ptb[:, g * 32:(g + 1) * 32], rhs=VA_ap(which, 16, g),
                                                           start=first, stop=True, skip_group_check=True), reads=[ptb, vb], writes=[po[g]], full=False)

    def s_den():
        for g in range(2):
            P.op("dve", lambda e, g=g: e.tensor_scalar(out=rd1[:, g:g + 1], in0=po[g][0:32, 64:65], scalar1=1e-30, scalar2=None, op0=ALU.max),
                 reads=[po[g]], writes=[rd1], full=False)
        P.op("dve", lambda e: e.reciprocal(out=rd1[:], in_=rd1[:]), reads=[rd1], writes=[rd1])

    def fq_rhs(s):
        return lambda g, nrow: fqT_v[0:nrow, 4 * g:4 * g + 4, 8 * s:8 * s + 8]

    def nq_rhs(s):
        return lambda g, nrow: nqT_v[0:nrow, 4 * g:4 * g + 4, 8 * s:8 * s + 8]

    nomask = lambda j, g: []

    def sample(s):
        P.op("pe", lambda e: e.matmul(pmisc[0:32, 300:324], lhsT=zall[:, s * 32:(s + 1) * 32], rhs=gates16, start=True, stop=True),
             reads=[zall, gates], writes=[pmisc], full=False)
        P.op("act", lambda e: e.copy(out=ghall[:], in_=pmisc[0:32, 300:324]), reads=[pmisc], writes=[ghall])
        for g in range(2):
            P.op("dve", lambda e, g=g: e.tensor_tensor(out=ghm[:], in0=ghall[:, g * 12:(g + 1) * 12].rearrange("p (h k) -> p h k", k=3), in1=ohc[:], op=ALU.mult),
                 reads=[ghall, ohc], writes=[ghm])
            P.op("dve", lambda e, g=g: e.reduce_sum(out=gh[:, g, :], in_=ghm[:].rearrange("p h k -> p k h"), axis=AX.X), reads=[ghm], writes=[gh], full=False)
        for b in range(4):
            gather(Lf0[:, b, :, :].rearrange("p r h -> p (r h)"), Lf0, lf_rows, s, b)
        P.op("dve", lambda e: e.reduce_sum(out=ctot[:], in_=Lf0[:].rearrange("p b r h -> p b h r"), axis=AX.X), reads=[Lf0], writes=[ctot])
        P.op("pe", lambda e: e.matmul(pmisc[:, 0:32], lhsT=ustr[:], rhs=ctot[:].rearrange("p b h -> p (b h)"), start=True, stop=True),
             reads=[ustr, ctot], writes=[pmisc], full=False)
        P.op("pe", lambda e: e.matmul(pmisc[:, 32:64], lhsT=ones[:], rhs=ctot[:].rearrange("p b h -> p (b h)"), start=True, stop=True),
             reads=[ones, ctot], writes=[pmisc], full=False)
        P.op("act", lambda e: e.copy(out=scb[:].rearrange("p a b h -> p (a b h)"), in_=pmisc[:, 0:64]), reads=[pmisc], writes=[scb])
        P.op("dve", lambda e: e.tensor_copy(out=S2[:, 2, :], in_=scb[:, 1, 3, :]), reads=[scb], writes=[S2], full=False)
        P.op("dve", lambda e: e.tensor_tensor(out=S2[:, 1, :], in0=S2[:, 2, :], in1=scb[:, 1, 2, :], op=ALU.add), reads=[scb, S2], writes=[S2], full=False)
        P.op("dve", lambda e: e.tensor_tensor(out=S2[:, 0, :], in0=S2[:, 1, :], in1=scb[:, 1, 1, :], op=ALU.add), reads=[scb, S2], writes=[S2], full=False)
        P.op("dve", lambda e: e.tensor_tensor(out=ctot[:], in0=S2[:], in1=scb[:, 0, :, :], op=ALU.add), reads=[scb, S2], writes=[ctot])
        P.op("pool", lambda e: e.memset(Lf1[:, :, 15:16, :], 0.0), writes=[Lf1], full=False)
        P.op("dve", lambda e: e.tensor_copy(out=Lf1[:, :, 0:15, :], in_=Lf0[:, :, 1:16, :]), reads=[Lf0], writes=[Lf1], full=False)
        src, dst = Lf1, Lf0
        for k in (1, 2, 4, 8):
            P.op("dve", lambda e, src=src, dst=dst, k=k: e.tensor_tensor(out=dst[:, :, 0:16 - k, :], in0=src[:, :, 0:16 - k, :], in1=src[:, :, k:16, :], op=ALU.add),
                 reads=[src], writes=[dst], full=False)
            P.op("pool", lambda e, src=src, dst=dst, k=k: e.tensor_copy(out=dst[:, :, 16 - k:16, :], in_=src[:, :, 16 - k:16, :]), reads=[src], writes=[dst], full=False)
            src, dst = dst, src
        suf, tmpb = src, dst
        P.op("dve", lambda e: e.tensor_tensor(out=suf[:], in0=suf[:], in1=ctot[:].unsqueeze(2).to_broadcast([128, 4, 16, 8]), op=ALU.add),
             reads=[suf, ctot], writes=[suf])
        PC5 = PCs[:].rearrange("p b r (h c) -> p b r h c", c=3)
        P.op("dve", lambda e: e.tensor_copy(out=PC5[:, :, :, :, 0], in_=suf[:]), reads=[suf], writes=[PCs], full=False)
        P.op("dve", lambda e: e.tensor_tensor(out=tmpb[:], in0=suf[:], in1=PC5[:, :, :, :, 0], op=ALU.subtract), reads=[suf, PCs], writes=[tmpb])
        P.op("dve", lambda e: e.tensor_copy(out=PC5[:, :, :, :, 1], in_=tmpb[:]), reads=[tmpb], writes=[PCs], full=False)
        P.op("dve", lambda e: e.tensor_tensor(out=suf[:], in0=tmpb[:], in1=PC5[:, :, :, :, 1], op=ALU.subtract), reads=[tmpb, PCs], writes=[suf])
        P.op("dve", lambda e: e.tensor_copy(out=PC5[:, :, :, :, 2], in_=suf[:]), reads=[suf], writes=[PCs], full=False)
        ktile_ones()
        P.op("pool", lambda e: e.memset(KAs[:, :, :, 76:79], 1.0), writes=[KAs], full=False)
        acc = [False, False]
        stg4 = stg[:].rearrange("p (r t d) -> p r t d", r=16, t=4)
        stg_guard()
        items = []
        for b in range(4):
            def pre(b=b):
                gather(stg[:], stg, fox_rows, s, b)
                P.op("dve", lambda e: e.tensor_copy(out=KAs[:, :, :, 0:64], in_=stg4[:, :, 0:2, :]), reads=[stg], writes=[KAs], full=False)
                P.op("pool", lambda e: e.tensor_copy(out=VAs[:, :, :, 0:64], in_=stg4[:, :, 2:4, :]), reads=[stg], writes=[VAs], full=False)
                P.op("pool", lambda e: e.tensor_copy(out=KAs[:, :, :, 64:76], in_=PCs[:, b, :, :].rearrange("p r (g c) -> p r g c", g=2)),
                     reads=[PCs], writes=[KAs], full=False)
            items.append(("pre", pre))
            for rnd in range(4):
                pf = lambda rnd=rnd: [(KAs[:, r, g, :], KAs, VAs[:, r, g, :], VAs, g) for r in range(4 * rnd, 4 * rnd + 4) for g in range(2)]
                items.append(("round", (pf, 80, fq_rhs(s), fqT, nomask)))
        run_rounds(items, acc)
        new_tile("f", fkT, FVA, fq_rhs(s), fqT, s, acc)
        s_den()
        for g in range(2):
            P.op("dve", lambda e, g=g: e.tensor_scalar(out=osb[:, g, :], in0=po[g][0:32, 0:64], scalar1=rd1[:, g:g + 1], scalar2=None, op0=ALU.mult),
                 reads=[po[g], rd1], writes=[osb], full=False)
        P.op("pool", lambda e: e.memset(lps[:], 0.0), writes=[lps])
        stg_guard()
        for bb in range(8):
            sh = stgh[bb % 2]
            gather(sh[:], sh, nsa_cmp_rows, s, 4 + bb)
            for r2 in range(2):
                P.op("dve" if r2 == 0 else "pool", lambda e, r2=r2, sh=sh: e.tensor_copy(
                    out=sbfc[:, :, :, r2, :], in_=sh[:].rearrange("p (q r t d) -> p r t q d", q=4, r=2, t=4)[:, r2, :, :, :]),
                    reads=[sh], writes=[KAs], full=False)
            for half in range(2):
                ty = "kv"[half]
                for g in range(2):
                    for rp in range(4):
                        j = g * 4 + rp
                        P.op("pe", lambda e, j=j, g=g, rp=rp, half=half: e.transpose(
                            out=ptr[:, j * 128:(j + 1) * 128], in_=sbfc[:, half * 2 + g, rp, :, :].rearrange("p r d -> p (r d)"), identity=ident[:]),
                            reads=[KAs, ident], writes=[ptr], full=False)
                P.op("dve", lambda e: e.tensor_copy(out=KTs[:], in_=ptr[:]), reads=[ptr], writes=[KTs])
                for g in range(2):
                    tg = half * 2 + g
                    pcm = nxt(pz, "pz")
                    for part in range(4):
                        for rp in range(4):
                            P.op("pe", lambda e, pcm=pcm, part=part, rp=rp, g=g, ty=ty: e.matmul(
                                pcm[:, part * 128:(part + 1) * 128], lhsT=W1s[ty][:, part * 4 + rp, :], rhs=KTs[:, (g * 4 + rp) * 128:(g * 4 + rp + 1) * 128],
                                start=(rp == 0), stop=(rp == 3)), reads=[W1s[ty], KTs], writes=[pcm], full=False)
                    P.op("act", lambda e, pcm=pcm: e.copy(out=Ls[:].rearrange("p a c -> p (a c)"), in_=pcm[:, 0:512]), reads=[pcm], writes=[Ls])
                    Lv = Ls[:].rearrange("p a (q t) -> p a q t", t=2)
                    P.op("dve", lambda e, Lv=Lv: e.tensor_tensor(out=lt64[:, 0, :], in0=Lv[:, 0, :, 0], in1=Lv[:, 1, :, 1], op=ALU.add), reads=[Ls], writes=[lt64], full=False)
                    P.op("dve", lambda e, Lv=Lv: e.tensor_tensor(out=lt64[:, 1, :], in0=Lv[:, 2, :, 0], in1=Lv[:, 3, :, 1], op=ALU.add), reads=[Ls], writes=[lt64], full=False)
                    P.op("dve", lambda e: e.tensor_tensor(out=pre64[:, 1:64], in0=lt64[:, 0, 0:63], in1=lt64[:, 1, 1:64], op=ALU.add), reads=[lt64], writes=[pre64], full=False)
                    P.op("dve", lambda e, tg=tg: e.tensor_tensor(out=pre64[:, 0:1], in0=lps[:, tg:tg + 1], in1=lt64[:, 1, 0:1], op=ALU.add),
                         reads=[lt64, lps], writes=[pre64], full=False)
                    P.op("dve", lambda e, tg=tg: e.tensor_copy(out=lps[:, tg:tg + 1], in_=lt64[:, 0, 63:64]), reads=[lt64], writes=[lps], full=False)
                    P.op("act", lambda e, tg=tg, half=half, bb=bb: e.activation(out=Ssp[:, tg, (bb % 2) * 64:(bb % 2) * 64 + 64], in_=pre64[:], func=AF.Silu,
                                                                               bias=cbias[:, half:half + 1]), reads=[pre64, cbias], writes=[Ssp], full=False)
            if bb % 2 == 1:
                st = bb // 2
                pzb = nxt(pz, "pz")
                for tg in range(4):
                    tyn = "kv"[tg // 2]
                    P.op("pe", lambda e, tg=tg, tyn=tyn, pzb=pzb: e.matmul(pzb[:, tg * 64:(tg + 1) * 64], lhsT=Ssp[:, tg, :], rhs=w2[tyn][:], start=True, stop=True),
                         reads=[Ssp, w2[tyn]], writes=[pzb], full=False)
                src_buf[0] = pzb
                headnorm(pzb[:, 0:128], 2, gains["nsa_kn_cmp_g"], KCAs[:, :, 0:64], [KCAs])
                P.op("pool", lambda e, st=st: e.tensor_copy(out=KCAs[:, :, 64:68], in_=alkcs[:, st, :].unsqueeze(1).to_broadcast([128, 2, 4])),
                     reads=[alkcs], writes=[KCAs], full=False)
                P.op("act", lambda e, st=st, pzb=pzb: e.copy(out=VCAs[:, st, :, 0:64], in_=pzb[:, 128:256].rearrange("p (h d) -> p h d", h=2)),
                     reads=[pzb], writes=[VCAs], full=False)
                for g in range(2):
                    P.op("pe", lambda e, g=g: e.transpose(out=ptr[0:80, g * 128:(g + 1) * 128], in_=KCAs[:, g, :], identity=ident[:]),
                         reads=[KCAs, ident], writes=[ptr], full=False)
                P.op("dve", lambda e, st=st: e.tensor_copy(out=kcTs[0:80, :, st * 128:(st + 1) * 128], in_=ptr[0:80, 0:256].rearrange("p (g c) -> p g c", g=2)),
                     reads=[ptr], writes=[kcTs], full=False)
        ps = nxt(pss, "pss")
        for g in range(2):
            for st in range(4):
                blk = g * 4 + st
                P.op("pe", lambda e, g=g, st=st, blk=blk: e.matmul(ps[:, blk * 32:(blk + 1) * 32], lhsT=kcTs[0:80, g, st * 128:(st + 1) * 128], rhs=nq_rhs(s)(g, 80),
                                                                 start=True, stop=(st != 0)), reads=[kcTs, nqT], writes=[ps], full=False)
                if st == 0:
                    P.op("pe", lambda e, blk=blk: e.matmul(ps[:, blk * 32:(blk + 1) * 32], lhsT=ident[:], rhs=s0m[:], start=False, stop=True, skip_group_check=True),
                         reads=[ident, s0m], writes=[ps], full=False)
        P.op("act", lambda e: e.activation(out=PTcs[:], in_=ps[:, 0:256], func=AF.Exp), reads=[ps], writes=[PTcs])
        for g in range(2):
            for st in range(4):
                blk = g * 4 + st
                P.op("pe", lambda e, g=g, st=st, blk=blk: e.matmul(po[g][0:32, 0:65], lhsT=PTcs[:, blk * 32:(blk + 1) * 32], rhs=VCAs[:, st, g, :],
                                                                 start=(st == 0), stop=(st == 3), skip_group_check=True), reads=[PTcs, VCAs], writes=[po[g]], full=False)
        for g in range(2):
            for st in range(4):
                blk = g * 4 + st
                P.op("pe", lambda e, g=g, st=st, blk=blk: e.matmul(pmisc[0:32, g * 132:g * 132 + 132], lhsT=PTcs[:, blk * 32:(blk + 1) * 32], rhs=bmats[:, st, :],
                                                                 start=(st == 0 and g == 0), stop=(st == 3), skip_group_check=True),
                     reads=[PTcs, bmats], writes=[pmisc], full=False)
        s_den()
        for g in range(2):
            P.op("dve", lambda e, g=g: e.tensor_scalar(out=impn[:, g, :], in0=pmisc[0:32, g * 132:(g + 1) * 132], scalar1=rd1[:, g:g + 1], scalar2=None, op0=ALU.mult),
                 reads=[pmisc, rd1], writes=[impn], full=False)
        P.op("dve", lambda e: e.tensor_tensor(out=wk1[:], in0=rd1[:], in1=gh[:, :, 0], op=ALU.mult), reads=[rd1, gh], writes=[wk1])
        for g in range(2):
            P.op("dve", lambda e, g=g: e.tensor_scalar(out=oaccs[:, g, :], in0=po[g][0:32, 0:64], scalar1=wk1[:, g:g + 1], scalar2=None, op0=ALU.mult),
                 reads=[po[g], wk1], writes=[oaccs], full=False)
        for g in range(2):
            P.op("pe", lambda e, g=g: e.matmul(pmisc[0:32, 300:432], lhsT=gsum[:], rhs=impn[:, g, :], start=True, stop=True), reads=[gsum, impn], writes=[pmisc], full=False)
            P.op("dve", lambda e: e.tensor_tensor(out=scs[:], in0=pmisc[0:32, 300:432], in1=fz2s[:], op=ALU.add), reads=[pmisc, fz2s], writes=[scs])
            P.op("dve", lambda e: e.max(out=mx8s[:], in_=scs[:]), reads=[scs], writes=[mx8s])
            P.op("dve", lambda e: e.match_replace(out=sws[:], in_to_replace=mx8s[:], in_values=scs[:], imm_value=-2.0), reads=[scs, mx8s], writes=[sws])
            P.op("dve", lambda e: e.max(out=mx8s[:], in_=sws[:]), reads=[sws], writes=[mx8s])
            P.op("dve", lambda e: e.tensor_reduce(out=thrs[:, 0:1], in_=mx8s[:], axis=AX.X, op=ALU.min), reads=[mx8s], writes=[thrs])
            P.op("dve", lambda e: e.tensor_scalar(out=nsels[:], in0=scs[:, 0:128], scalar1=thrs[:, 0:1], scalar2=NEG, op0=ALU.is_lt, op1=ALU.mult),
                 reads=[scs, thrs], writes=[nsels])
            P.op("pe", lambda e: e.transpose(out=ptr[:, 0:32], in_=nsels[:], identity=ident[0:32, 0:32]), reads=[nsels, ident], writes=[ptr], full=False)
            P.op("dve", lambda e, g=g: e.tensor_copy(out=nselTs[:, g, :], in_=ptr[:, 0:32]), reads=[ptr], writes=[nselTs], full=False)
        ktile_ones()
        acc = [False, False]
        items = []
        for bb in range(8):
            def pre(bb=bb):
                sh = stgh[bb % 2]
                sh4 = sh[:].rearrange("p (r t d) -> p r t d", r=8, t=4)
                gather(sh[:], sh, nsa_slc_rows, s, 4 + bb)
                P.op("dve", lambda e: e.tensor_copy(out=KAs[:, 0:8, :, 0:64], in_=sh4[:, :, 0:2, :]), reads=[sh], writes=[KAs], full=False)
                P.op("pool", lambda e: e.tensor_copy(out=VAs[:, 0:8, :, 0:64], in_=sh4[:, :, 2:4, :]), reads=[sh], writes=[VAs], full=False)
                P.op("pool", lambda e: e.tensor_copy(out=KAs[:, 0:8, :, 64:68], in_=alks_n[:, bb, :, :].unsqueeze(2).to_broadcast([128, 8, 2, 4])),
                     reads=[alks_n], writes=[KAs], full=False)
            items.append(("pre", pre))
            mfn = lambda j, g, bb=bb: [(esels[:, bb, :], nselTs[:, g, :], [esels, nselTs])]
            for rnd in range(2):
                pf = lambda rnd=rnd: [(KAs[:, r, g, :], KAs, VAs[:, r, g, :], VAs, g) for r in range(4 * rnd, 4 * rnd + 4) for g in range(2)]
                items.append(("round", (pf, 80, nq_rhs(s), nqT, mfn)))
        run_rounds(items, acc)
        new_tile("s", skT, SVA, nq_rhs(s), nqT, s, acc)
        s_den()
        P.op("dve", lambda e: e.tensor_tensor(out=wk1[:], in0=rd1[:], in1=gh[:, :, 1], op=ALU.mult), reads=[rd1, gh], writes=[wk1])
        for g in range(2):
            P.op("dve", lambda e, g=g: e.tensor_scalar(out=otmps[:, g, :], in0=po[g][0:32, 0:64], scalar1=wk1[:, g:g + 1], scalar2=None, op0=ALU.mult),
                 reads=[po[g], wk1], writes=[otmps], full=False)
        P.op("dve", lambda e: e.tensor_tensor(out=oaccs[:], in0=oaccs[:], in1=otmps[:], op=ALU.add), reads=[oaccs, otmps], writes=[oaccs])
        stg_guard()
        P.dma("sp", lambda e: e.dma_start(out=stg[:, 0:1024].rearrange("p (t c) -> p t c", t=4), in_=I["win_state"][s].rearrange("(t p) c -> p t c", p=128)),
              stg, writes=[stg])
        stw = stg[:, 0:1024].rearrange("p (t a d) -> p t a d", t=4, a=4)
        P.op("dve", lambda e: e.tensor_copy(out=KAs[:, 0:4, :, 0:64], in_=stw[:, :, 0:2, :]), reads=[stg], writes=[KAs], full=False)
        P.op("pool", lambda e: e.tensor_copy(out=VAs[:, 0:4, :, 0:64], in_=stw[:, :, 2:4, :]), reads=[stg], writes=[VAs], full=False)
        P.op("pool", lambda e: e.tensor_copy(out=KAs[:, 0:4, :, 64:68], in_=alkw[:].unsqueeze(2).to_broadcast([128, 4, 2, 4])), reads=[alkw], writes=[KAs], full=False)
        acc = [False, False]
        pairs = [(KAs[:, wt, g, :], KAs, VAs[:, wt, g, :], VAs, g) for wt in range(4) for g in range(2)]
        wfn = lambda j, g: ([(ident[:], wm0[:], [ident, wm0])] if j < 2 else [])
        s_round(pairs, 80, nq_rhs(s), nqT, wfn, acc)
        new_tile("w", wkT, WVA, nq_rhs(s), nqT, s, acc)
        s_den()
        P.op("dve", lambda e: e.tensor_tensor(out=wk1[:], in0=rd1[:], in1=gh[:, :, 2], op=ALU.mult), reads=[rd1, gh], writes=[wk1])
        for g in range(2):
            P.op("dve", lambda e, g=g: e.tensor_scalar(out=otmps[:, g, :], in0=po[g][0:32, 0:64], scalar1=wk1[:, g:g + 1], scalar2=None, op0=ALU.mult),
                 reads=[po[g], wk1], writes=[otmps], full=False)
        P.op("dve", lambda e: e.tensor_tensor(out=osb[:, 2:4, :], in0=oaccs[:], in1=otmps[:], op=ALU.add), reads=[oaccs, otmps], writes=[osb], full=False)
        for br in range(2):
            pp = nxt(pz, "pz")
            for g in range(2):
                for h in range(4):
                    cc = (g * 4 + h) * 64
                    P.op("pe", lambda e, pp=pp, br=br, g=g, h=h, cc=cc: e.matmul(pp[:, cc:cc + 64], lhsT=zsel[:, h, 128 - 8 * s:256 - 8 * s], rhs=osb[:, br * 2 + g, :],
                                                                              start=True, stop=True), reads=[zsel, osb], writes=[pp], full=False)
            P.op("dve", lambda e, pp=pp, br=br: e.tensor_tensor(out=OTs[:, br * 512:(br + 1) * 512], in0=pp[:, 0:512], in1=OTs[:, br * 512:(br + 1) * 512], op=ALU.add),
                 reads=[pp, OTs], writes=[OTs], full=False)

    NSAMP = int(os.environ.get("KNS", "16"))
    for s in range(NSAMP):
        sample(s)
    P.op("dve", lambda e: e.tensor_copy(out=ot[:], in_=OTs[:]), reads=[OTs], writes=[ot])
    P.dma("sp", lambda e: e.dma_start(out=oscr[16], in_=ot[:]), ot, reads=[ot], writes=[oscr_b[16]])
    P.op("pool", lambda e: e.memset(ssq[:], 0.0), reads=[], writes=sbufs + [wbf, fqT, nqT, uT, ssq] + ar_bufs)

    load_weight(lambda k: wbf[:, k, 0:2048], lambda k: win_v[:, k, 2080:4128], 8, 2048, [], wbf)
    brf_v = I["w_br_fox"].rearrange("(k p) c -> p k c", p=128)
    brn_v = I["w_br_nsa"].rearrange("(k p) c -> p k c", p=128)
    wout_v = I["w_out"].rearrange("(k p) c -> p k c", p=128)
    for dst in kv_all:
        pass
    P.op("pool", lambda e: e.memset(ssq[:], 0.0), reads=[], writes=kv_all + [wbf2, ssq])
    load_weight(lambda k: wbf2_v[:, k, :], lambda k: brf_v[:, k, :], 4, 1024, [], wbf2)
    load_weight(lambda k: wbf2_v[:, 4 + k, :], lambda k: brn_v[:, k, :], 4, 1024, [], wbf2)
    load_weight(lambda k: wbf2_v[:, 8 + k, :], lambda k: wout_v[:, k, :], 8, 1024, [], wbf2)
    phaseB_bufs = list(ar_bufs)
    ar_off[0] = 0
    gsig = carve("gsig", [128, 2048], F32)
    oT = carve("oT", [128, 8, 128], BF16)
    mix = carve("mix", [128, 1024], BF16)
    mtmp = carve("mtmp", [128, 512], F32)
    mtmp2 = carve("mtmp2", [128, 512], F32)
    gtrow = carve("gtrow", [128, 1024], F32)
    rl = carve("rl", [128, 512], F32)
    h2g = carve("h2g", [128, 4, 1024], BF16)
    P.op("pool", lambda e: e.memset(ssq[:], 0.0), reads=[], writes=ar_bufs + [ssq])
    ybuf = [Buf("ybuf%d" % i, None) for i in range(NT)]

    def y_ap(i):
        return O["yp"][i * 128:(i + 1) * 128, :] if i < 16 else O["ys"]

    def gt_rows(i, which):
        for ch in range(2):
            ps = nxt(pss, "pss")
            P.op("pe", lambda e, ps=ps, ch=ch: e.matmul(ps[:, 0:512], lhsT=selT[:, 0 if i < 16 else 1, :],
                                                        rhs=modrow[:, which * 1024 + ch * 512: which * 1024 + (ch + 1) * 512], start=True, stop=True),
                 reads=[selT, modrow], writes=[ps], full=False)
            P.op("act", lambda e, ps=ps, ch=ch: e.copy(out=gtrow[:, ch * 512:(ch + 1) * 512], in_=ps[:, 0:512]), reads=[ps], writes=[gtrow], full=False)

    for i in range(NT if STAGE >= 3 else 0):
        x_t = xt[i % 2]
        xsrc = I["xp"][i * 128:(i + 1) * 128, :] if i < 16 else I["xs"]
        P.dma("sp", lambda e, x_t=x_t, xsrc=xsrc: e.dma_start(out=x_t[:], in_=xsrc), x_t, writes=[x_t])
        vw = norm_hT(x_t, G1, 0, i)
        P.op("dve", lambda e, vw=vw, i=i: e.tensor_tensor(out=vw(hT[:]), in0=vw(htmp[:]), in1=seq_bc(modT[:], 0, i), op=ALU.add),
             reads=[htmp, modT], writes=[hT])
        for c in range(4):
            pzb = nxt(pz, "pz")
            mm8(pzb[:, 0:512], pzb, hT, lambda k: hT[:, k, :], wbf, lambda k, c=c: wbf[:, k, c * 512:(c + 1) * 512])
            P.op("act", lambda e, pzb=pzb, c=c: e.activation(out=gsig[:, c * 512:(c + 1) * 512], in_=pzb[:, 0:512], func=AF.Sigmoid),
                 reads=[pzb], writes=[gsig], full=False)
        P.dma("sp", lambda e, i=i: e.dma_start(out=ot[:], in_=oscr[i]), ot, reads=[oscr_b[i]], writes=[ot])
        for k in range(8):
            P.op("pe", lambda e, k=k: e.transpose(out=ptr[:, k * 128:(k + 1) * 128], in_=ot[:, k * 128:(k + 1) * 128], identity=ident[:]),
                 reads=[ot, ident], writes=[ptr], full=False)
        P.op("act", lambda e: e.copy(out=oT[:].rearrange("p k c -> p (k c)"), in_=ptr[:]), reads=[ptr], writes=[oT])
        for ch in range(2):
            pf = nxt(pz, "pz")
            mm8(pf[:, 0:512], pf, oT, lambda k: oT[:, k, :], wbf2, lambda k, ch=ch: wbf2_v[:, k, ch * 512:(ch + 1) * 512], nk=4)
            P.op("dve", lambda e, pf=pf, ch=ch: e.tensor_tensor(out=mtmp[:], in0=pf[:, 0:512], in1=gsig[:, ch * 512:(ch + 1) * 512], op=ALU.mult),
                 reads=[pf, gsig], writes=[mtmp])
            pn = nxt(pz, "pz")
            mm8(pn[:, 0:512], pn, oT, lambda k: oT[:, 4 + k, :], wbf2, lambda k, ch=ch: wbf2_v[:, 4 + k, ch * 512:(ch + 1) * 512], nk=4)
            P.op("dve", lambda e, pn=pn, ch=ch: e.tensor_tensor(out=mtmp2[:], in0=pn[:, 0:512], in1=gsig[:, 1024 + ch * 512:1024 + (ch + 1) * 512], op=ALU.mult),
                 reads=[pn, gsig], writes=[mtmp2])
            P.op("dve", lambda e, ch=ch: e.tensor_tensor(out=mix[:, ch * 512:(ch + 1) * 512], in0=mtmp[:], in1=mtmp2[:], op=ALU.add),
                 reads=[mtmp, mtmp2], writes=[mix], full=False)
        for k in range(8):
            P.op("pe", lambda e, k=k: e.transpose(out=ptr[:, k * 128:(k + 1) * 128], in_=mix[:, k * 128:(k + 1) * 128], identity=ident[:]),
                 reads=[mix, ident], writes=[ptr], full=False)
        P.op("act", lambda e: e.copy(out=oT[:].rearrange("p k c -> p (k c)"), in_=ptr[:]), reads=[ptr], writes=[oT])
        gt_rows(i, 0)
        for ch in range(2):
            pw = nxt(pz, "pz")
            mm8(pw[:, 0:512], pw, oT, lambda k: oT[:, k, :], wbf2, lambda k, ch=ch: wbf2_v[:, 8 + k, ch * 512:(ch + 1) * 512])
            P.op("dve", lambda e, pw=pw, ch=ch: e.tensor_tensor(out=mtmp[:], in0=pw[:, 0:512], in1=gtrow[:, ch * 512:(ch + 1) * 512], op=ALU.mult),
                 reads=[pw, gtrow], writes=[mtmp])
            P.op("dve", lambda e, ch=ch, x_t=x_t: e.tensor_tensor(out=x_t[:, ch * 512:(ch + 1) * 512], in0=mtmp[:], in1=x_t[:, ch * 512:(ch + 1) * 512], op=ALU.add),
                 reads=[mtmp, x_t], writes=[x_t], full=False)
        P.dma("sp", lambda e, x_t=x_t, i=i: e.dma_start(out=y_ap(i), in_=x_t[:]), x_t, reads=[x_t], writes=[ybuf[i]])
        vw = norm_hT(x_t, G2, 24, i)
        P.op("dve", lambda e, vw=vw, i=i: e.tensor_tensor(out=vw(h2t[:]), in0=vw(htmp[:]),
                                                          in1=seq_bc(modT[:], 24, i), op=ALU.add), reads=[htmp, modT], writes=[h2t])
        P.dma("sp", lambda e, i=i: e.dma_start(out=hscr[i], in_=h2t[:].rearrange("p k c -> p (k c)")), h2t, reads=[h2t], writes=[hscr_b[i]])

    wup_v = I["w_up"].rearrange("(k p) c -> p k c", p=128)
    wdn_v = I["w_down"].rearrange("(k p) c -> p k c", p=128)
    h2g4 = h2g[:].rearrange("p n (k c) -> p n k c", k=8)
    P.op("pool", lambda e: e.memset(ssq[:], 0.0), reads=[], writes=[fqT, nqT, uT, ssq])
    groups = [(0, 4), (4, 4), (8, 4), (12, 4), (16, 1)]
    for half in range(2 if STAGE >= 4 else 0):
        load_weight(lambda k: wbf[:, k, 0:2048], lambda k: wup_v[:, k, half * 2048:(half + 1) * 2048], 8, 2048, [], wbf)
        load_weight(lambda k: wbf2_v[:, k, :], lambda k: wdn_v[:, half * 16 + k, :], 16, 1024, [], wbf2)
        for (t0, nt) in groups:
            w = nt * 128
            P.dma("sp", lambda e, t0=t0, nt=nt: e.dma_start(out=h2g[:, 0:nt, :], in_=hscr[t0:t0 + nt].rearrange("n p c -> p n c")), h2g,
                  reads=[hscr_b[t] for t in range(t0, t0 + nt)], writes=[h2g])
            for fc in range(16):
                pu = nxt(pz, "pz")
                mm8(pu[:, 0:w], pu, wbf, lambda k, fc=fc: wbf[:, k, fc * 128:(fc + 1) * 128], h2g,
                    lambda k, nt=nt: h2g4[:, 0:nt, k, :])
                P.op("act", lambda e, pu=pu, w=w: e.activation(out=rl[:, 0:w], in_=pu[:, 0:w], func=AF.Relu), reads=[pu], writes=[rl])
                P.op("dve", lambda e, fc=fc, w=w: e.tensor_tensor(out=uT_v[:, fc, 0:w], in0=rl[:, 0:w], in1=rl[:, 0:w], op=ALU.mult),
                     reads=[rl], writes=[uT], full=False)
            for tl in range(nt):
                i = t0 + tl
                x_t = xt[i % 2]
                P.dma("sp", lambda e, x_t=x_t, i=i: e.dma_start(out=x_t[:], in_=y_ap(i)), x_t, reads=[ybuf[i]], writes=[x_t])
                gt_rows(i, 1)
                for ch in range(2):
                    pd = nxt(pz, "pz")
                    mm8(pd[:, 0:512], pd, uT, lambda k, tl=tl: uT_v[:, k, tl * 128:(tl + 1) * 128], wbf2,
                        lambda k, ch=ch: wbf2_v[:, k, ch * 512:(ch + 1) * 512], nk=16)
                    P.op("dve", lambda e, pd=pd, ch=ch: e.tensor_tensor(out=mtmp[:], in0=pd[:, 0:512], in1=gtrow[:, ch * 512:(ch + 1) * 512], op=ALU.mult),
                         reads=[pd, gtrow], writes=[mtmp])
                    P.op("dve", lambda e, ch=ch, x_t=x_t: e.tensor_tensor(out=x_t[:, ch * 512:(ch + 1) * 512], in0=mtmp[:], in1=x_t[:, ch * 512:(ch + 1) * 512], op=ALU.add),
                         reads=[mtmp, x_t], writes=[x_t], full=False)
                P.dma("sp", lambda e, x_t=x_t, i=i: e.dma_start(out=y_ap(i), in_=x_t[:]), x_t, reads=[x_t], writes=[ybuf[i]], final=(half == 1))

    dummy = Buf("dummy_ws", None)
    P.dma("sp", lambda e: e.dma_start(out=O["win_s"][:, 0:504, :], in_=I["win_state"][:, 8:512, :]), dummy, final=True)
    print("sbuf bytes remaining:", nc.sbuf_bytes_remaining, " ops:", {k: len(v) for k, v in P.ops.items()})
    P.emit()
    return nc


def _consts():
    bf = ml_dtypes.bfloat16
    ident = np.eye(128, dtype=np.float32).astype(bf)
    idx = np.arange(128)
    tri_p = (idx[:, None] <= idx[None, :]).astype(np.float32)
    same = (idx[:, None] // 8) == (idx[None, :] // 8)
    tri_s = (same & (idx[:, None] <= idx[None, :])).astype(np.float32)
    ones = np.ones((128, 128), np.float32)
    slopes = 2.0 ** (-np.arange(1, 9, dtype=np.float64))
    alq = np.zeros((128, NT, 8, 4), np.float32)
    alk = np.zeros((128, NT, 4), np.float32)
    for t in range(NT):
        pos = (t * 128 + idx) if t < 16 else (8192 + idx % 8)
        hi = (pos // 64) * 64
        lo = pos % 64
        for h in range(8):
            alq[:, t, h, 0] = slopes[h]
            alq[:, t, h, 1] = slopes[h]
            alq[:, t, h, 2] = -slopes[h] * hi
            alq[:, t, h, 3] = -slopes[h] * lo
        alk[:, t, 0] = hi
        alk[:, t, 1] = lo
        alk[:, t, 2] = 1
        alk[:, t, 3] = 1
    cend = 16 * idx + 15
    alkc = np.stack([(cend // 64) * 64, cend % 64, np.ones(128), np.ones(128)], axis=1).astype(np.float32)
    tq = idx[None, :]
    tk = idx[:, None]
    cm_diag = np.tile(np.where(tk > tq, NEG, 0.0), (1, 4)).astype(np.float32)
    cm_win = np.tile(np.where(tk <= tq, NEG, 0.0), (1, 4)).astype(np.float32)
    tq_all = np.arange(2048)[None, :]
    cmaskc = np.where((idx[:, None] == 0) | (cend[:, None] > tq_all), NEG, 0.0).astype(np.float32)
    esel = np.zeros((32, 16, 128), np.float32)
    for kt in range(16):
        esel[2 * kt, kt, 0:64] = 1
        esel[2 * kt + 1, kt, 64:128] = 1
    bmat = np.zeros((128, 32), np.float32)
    for j in range(32):
        for m, wgt in ((4 * j, 0.5), (4 * j + 1, 1.0), (4 * j + 2, 1.0), (4 * j + 3, 1.0), (4 * j + 4, 0.5)):
            if m < 128:
                bmat[m, j] += wgt
    fz2 = np.zeros((128, 16, 32), np.float32)
    vz = np.zeros((128, 16, 32), np.float32)
    blk = np.arange(32)[None, :]
    for t in range(16):
        cur = ((t * 128 + idx) // 64)[:, None]
        valid = (blk <= cur).astype(np.float32)
        forced = ((blk == 0) | (blk == cur) | (blk == cur - 1)).astype(np.float32)
        vz[:, t, :] = valid
        fz2[:, t, :] = 1.0e4 * forced * valid + (valid - 1.0)
    selT = np.zeros((17, 2, 128), np.float32)
    selT[0, 0, :] = 1
    for p in range(128):
        selT[1 + p // 8, 1, p] = 1
    p = idx
    bmask = np.full((128, 16, 4, 8), NEG, np.float32)
    for s_ in range(16):
        for t_ in range(8):
            for tq_ in range(t_, 8):
                bmask[s_ * 8 + t_, s_, :, tq_] = 0.0
    zall = np.zeros((128, 16, 4, 8), np.float32)
    for s_ in range(16):
        for tq_ in range(8):
            zall[s_ * 8 + tq_, s_, :, tq_] = 1.0
    zsel = np.zeros((32, 4, 256), np.float32)
    for h_ in range(4):
        for t_ in range(8):
            zsel[h_ * 8 + t_, h_, 128 + t_] = 1.0
    bmats = np.zeros((128, 4, 132), np.float32)
    for j in range(129):
        for m, wgt in ((4 * j, 0.5), (4 * j + 1, 1.0), (4 * j + 2, 1.0), (4 * j + 3, 1.0), (4 * j + 4, 0.5)):
            if m < 512:
                bmats[m % 128, m // 128, j] += wgt
    esels = np.zeros((128, 8, 128), np.float32)
    for bb in range(8):
        esels[16 * bb + p // 8, bb, p] = 1.0
    def hl(pos):
        return np.stack([(pos // 64) * 64, pos % 64, np.ones_like(pos), np.ones_like(pos)], axis=-1).astype(np.float32)
    alks_n = hl(1024 * np.arange(8)[None, :, None] + 8 * p[:, None, None] + np.arange(8)[None, None, :])
    alkw = hl(7680 + 128 * np.arange(4)[None, :] + p[:, None])
    alkcs = hl(16 * (128 * np.arange(4)[None, :] + p[:, None]) + 15)
    ustr = (p[:, None] > p[None, :]).astype(np.float32)
    s0m = np.zeros((128, 32), np.float32); s0m[0, :] = NEG
    wm0 = np.where(p[:, None] <= (np.arange(32) % 8)[None, :], NEG, 0.0).astype(np.float32)
    gsum = ((np.arange(32)[:, None] % 8) == (np.arange(32)[None, :] % 8)).astype(np.float32)
    fz2s = np.zeros((32, 132), np.float32)
    fz2s[:, [0, 127, 128]] = 1.0e4
    fz2s[:, 129:] = -1.0
    ohc = np.zeros((32, 4, 3), np.float32)
    for r_ in range(32):
        ohc[r_, r_ // 8, :] = 1.0
    mulc = np.zeros((128, 12), np.float32); addc = np.zeros((128, 12), np.float32)
    mulc[:, 0:4] = 8.0; addc[:, 0:4] = (p % 8)[:, None]
    mulc[:, 4:12] = 16.0; addc[:, 4:12] = (p % 16)[:, None]
    samp = dict(bmask=bmask.reshape(128, 512).astype(bf), zall=zall.reshape(128, 512), zsel=zsel.reshape(32, 1024).astype(bf),
                bmats=bmats.reshape(128, 528).astype(bf), esels=esels.reshape(128, 1024).astype(bf), alks_n=alks_n.reshape(128, 256).astype(bf),
                alkw=alkw.reshape(128, 16).astype(bf), alkcs=alkcs.reshape(128, 16).astype(bf), ustr=ustr, s0m=s0m.astype(bf), wm0=wm0.astype(bf),
                gsum=gsum, fz2s=fz2s, ohc=ohc.reshape(32, 12), mulc=mulc, addc=addc)
    return dict(samp, ident=ident, tri_p=tri_p, tri_s=tri_s, ones=ones,
                alq=alq.reshape(128, NT * 32).astype(bf), alk=alk.reshape(128, NT * 4).astype(bf), alkc=alkc.astype(bf),
                cm_diag=cm_diag.astype(bf), cm_win=cm_win.astype(bf), cmaskc=cmaskc.astype(bf),
                esel=esel.reshape(32, 16 * 128).astype(bf), bmat=bmat.astype(bf),
                fz2=fz2.reshape(128, 512), vz=vz.reshape(128, 512), selT=selT.reshape(17, 256))


_NC = [None]


def kernel(**inp):
    f32 = lambda a: np.ascontiguousarray(np.asarray(a), dtype=np.float32)
    if _NC[0] is None:
        _NC[0] = build_program()
    nc = _NC[0]
    shared = dict(_consts())
    shared["w_ada"] = f32(inp["w_ada"][0])
    shared["b_adaT"] = f32(np.asarray(inp["b_ada"][0]).reshape(48, 128).T)
    shared["b_ada"] = f32(np.asarray(inp["b_ada"][0]).reshape(1, 6144))
    shared["w_in"] = f32(inp["w_in"][0])
    shared["g1T"] = f32(np.asarray(inp["norm1_g"][0]).reshape(8, 128).T)
    shared["g2T"] = f32(np.asarray(inp["norm2_g"][0]).reshape(8, 128).T)
    shared["b_fox_f"] = f32(inp["b_fox_f"])
    for g in ("fox_qn_g", "fox_kn_g", "nsa_qn_g", "nsa_kn_cmp_g", "nsa_kn_slc_g", "nsa_kn_win_g"):
        shared[g] = f32(inp[g])
    for ty in ("k", "v"):
        w1 = np.asarray(inp["cmp_w1_" + ty][0]).reshape(32, 64, 128)
        w1 = w1.transpose(1, 0, 2).reshape(64, 32 * 128)
        shared["w1" + ty] = f32(np.concatenate([w1, w1], axis=0))
        shared["w2" + ty] = f32(inp["cmp_w2_" + ty][0])
        pos = np.asarray(inp["cmp_pos_" + ty][0]).T
        shared["pos" + ty + "T"] = f32(np.concatenate([pos, pos], axis=0))
    shared["w_br_fox"] = f32(inp["w_br_fox"][0])
    shared["w_br_nsa"] = f32(inp["w_br_nsa"][0])
    shared["w_out"] = f32(inp["w_out"][0])
    shared["w_up"] = f32(inp["w_up"][0])
    shared["w_down"] = f32(inp["w_down"][0])
    shared["cache_fox_kv"] = f32(inp["cache_fox_kv"]).reshape(NPHYS * 8, 4096)
    nsa4 = np.asarray(inp["cache_nsa_kv"]).reshape(NPHYS * 128, 4, 128)
    shared["cache_nsa_cmp"] = f32(nsa4[:, 0:2, :]).reshape(NPHYS * 16, 2048)
    shared["cache_nsa_slc"] = f32(nsa4[:, 2:4, :]).reshape(NPHYS * 16, 2048)
    shared["cache_fox_logf"] = f32(inp["cache_fox_logf"]).reshape(NPHYS * 8, 128)
    pt = np.asarray(inp["page_table"]).astype(np.int32)
    xp = np.asarray(inp["x_prompt"])
    xs = np.asarray(inp["x_sample"])
    cp = np.asarray(inp["c_prompt"])
    cs = np.asarray(inp["c_sample"])
    in_maps = []
    for c in range(8):
        m = dict(shared)
        m["xp"] = f32(xp[c])
        m["xs"] = f32(xs[16 * c:16 * c + 16].reshape(128, 1024))
        call = np.concatenate([cp[c:c + 1], cs[16 * c:16 * c + 16]], axis=0)
        m["cT"] = f32(call.reshape(17, 8, 128).transpose(2, 1, 0))
        m["win_state"] = f32(np.asarray(inp["state_win_kv"][0, 16 * c:16 * c + 16]).reshape(16, 512, 256))
        ptc = pt[16 * c:16 * c + 16]
        pq = np.arange(128)
        ptq = np.zeros((128, 16, 12), np.int32)
        for bq in range(4):
            ptq[:, :, bq] = ptc[:, 16 * bq + pq // 8].T
        for bq in range(8):
            ptq[:, :, 4 + bq] = ptc[:, 8 * bq + pq // 16].T
        m["ptq"] = np.ascontiguousarray(ptq.reshape(128, 192))
        in_maps.append(m)
    res = run_bass_kernel_spmd(nc, in_maps, core_ids=list(range(8)))
    R = res.results
    cat = lambda n: np.stack([np.asarray(R[c][n]) for c in range(8)], axis=0)
    yp = cat("yp").reshape(8, 2048, 1024)
    ys = cat("ys").reshape(128, 8, 1024)
    fox_p = cat("fox_p").reshape(1, 8, 2048, 2, 2, 64)
    lf_p = cat("lf_p").reshape(1, 8, 2048, 8)
    nsa_p = cat("nsa_p").reshape(1, 8, 2048, 4, 2, 64)
    win_p = cat("win_p").reshape(1, 8, 512, 2, 2, 64)
    fox_s = cat("fox_s").reshape(1, 128, 8, 2, 2, 64)
    lf_s = cat("lf_s").reshape(1, 128, 8, 8)
    nsa_s = cat("nsa_s").reshape(1, 128, 8, 4, 2, 64)
    win_s = cat("win_s").reshape(1, 128, 512, 2, 2, 64)
    return (yp, ys, fox_p, lf_p, nsa_p, win_p, fox_s, lf_s, nsa_s, win_s)
```
